# Optimizing a Trainium2 kernel written in Bass

```python
import math
import jax, jax.numpy as jnp
from jax import lax
import numpy as np

D_MODEL = 1024
BATCH = 2
SEQ = 8192
DEPTH = 4

GRID_W = 64
HEAD_DIM = 64
ROPE_THETA = 10000.0
NEG_INF = -1e30
LN_EPS = 1e-5
A_HEADS = 8
A_KV_HEADS = 2
A_WINDOW = 128
A_BLOCK = 128
B_HEADS = 8
B_KH = 8
B_KW = 16
B_QW = 16
C_HEADS = 4
C_VDIM = 2 * HEAD_DIM
C_BLOCK = 128
D_WIDTH = 512
D_BLOCKS = 8
D_CONV = 4
LRU_C = 8.0
N_BRANCH = 4
BRANCH_W = 512
PEER_HEADS = 8
PEER_KEYS = 128
PEER_N = PEER_KEYS * PEER_KEYS
PEER_QDIM = 256
PEER_TOPK = 16
PEER_CHUNK = 128
ALPHA = (2.0 * DEPTH) ** 0.25
BETA = (8.0 * DEPTH) ** -0.25
A_Q = A_HEADS * HEAD_DIM
A_KV = A_KV_HEADS * HEAD_DIM
B_QKV = B_HEADS * HEAD_DIM
C_QK = C_HEADS * 2 * HEAD_DIM
C_V = C_HEADS * C_VDIM
GATE_COLS = N_BRANCH * D_MODEL
IN_SPLITS = (A_Q, A_KV, A_KV, B_QKV, B_QKV, B_QKV, C_QK, C_QK, C_V, D_WIDTH, D_WIDTH, GATE_COLS)
IN_COLS = A_Q + 2 * A_KV + 3 * B_QKV + 2 * C_QK + C_V + 2 * D_WIDTH + GATE_COLS

kernel_name = 'hybrid_gated_encoder'


def layer_norm(x, g, b):
    xf = x.astype(jnp.float32)
    mu = jnp.mean(xf, axis=-1, keepdims=True)
    var = jnp.mean(jnp.square(xf - mu), axis=-1, keepdims=True)
    return ((xf - mu) * lax.rsqrt(var + LN_EPS) * g.astype(jnp.float32) + b.astype(jnp.float32)).astype(x.dtype)


def rope_tables(seq):
    inv = 1.0 / (ROPE_THETA ** (jnp.arange(0, HEAD_DIM, 2, dtype=jnp.float32) / HEAD_DIM))
    ang = jnp.arange(seq, dtype=jnp.float32)[:, None] * inv[None, :]
    return jnp.cos(ang), jnp.sin(ang)


def apply_rope(x, cos, sin):
    x1, x2 = jnp.split(x, 2, axis=-1)
    shape = (1, cos.shape[0]) + (1,) * (x.ndim - 3) + (cos.shape[1],)
    c = cos.reshape(shape).astype(x.dtype)
    s = sin.reshape(shape).astype(x.dtype)
    return jnp.concatenate([x1 * c - x2 * s, x2 * c + x1 * s], axis=-1)


def window_gqa(q, k, v, sink, cos, sin):
    bsz, seq = q.shape[:2]
    nb = seq // A_BLOCK
    grp = A_HEADS // A_KV_HEADS
    q = apply_rope(q, cos, sin) * (HEAD_DIM ** -0.5)
    k = apply_rope(k, cos, sin)
    pad = ((0, 0), (A_BLOCK, A_BLOCK), (0, 0), (0, 0))
    kp = jnp.pad(k, pad).reshape(bsz, nb + 2, A_BLOCK, A_KV_HEADS, HEAD_DIM)
    vp = jnp.pad(v, pad).reshape(bsz, nb + 2, A_BLOCK, A_KV_HEADS, HEAD_DIM)
    kw = jnp.concatenate([kp[:, :-2], kp[:, 1:-1], kp[:, 2:]], axis=2)
    vw = jnp.concatenate([vp[:, :-2], vp[:, 1:-1], vp[:, 2:]], axis=2)
    qb = q.reshape(bsz, nb, A_BLOCK, A_KV_HEADS, grp, HEAD_DIM)
    s = jnp.einsum('bnqkgd,bnjkd->bnkgqj', qb, kw).astype(jnp.float32)
    qi = np.arange(A_BLOCK)[:, None]
    kj = np.arange(3 * A_BLOCK)[None, :]
    rel = kj - A_BLOCK - qi
    kpos = np.arange(nb)[:, None, None] * A_BLOCK + kj[None] - A_BLOCK
    valid = (np.abs(rel)[None] <= A_WINDOW) & (kpos >= 0) & (kpos < seq)
    s = jnp.where(valid[None, :, None, None], s, NEG_INF)
    sk = sink.astype(jnp.float32).reshape(A_KV_HEADS, grp)[None, None, :, :, None, None]
    m = jnp.maximum(jnp.max(s, axis=-1, keepdims=True), sk)
    p = jnp.exp(s - m)
    p = (p / (jnp.sum(p, axis=-1, keepdims=True) + jnp.exp(sk - m))).astype(v.dtype)
    o = jnp.einsum('bnkgqj,bnjkd->bnqkgd', p, vw)
    return o.reshape(bsz, seq, A_HEADS * HEAD_DIM)


def neighbourhood_attention(q, k, v, rpb):
    bsz, seq = q.shape[:2]
    rows = seq // GRID_W
    kh = min(B_KH, rows)
    kcw = B_QW + B_KW
    qg = (q * HEAD_DIM ** -0.5).reshape(bsz, rows, GRID_W, B_HEADS, HEAD_DIM)
    kg = k.reshape(bsz, rows, GRID_W, B_HEADS, HEAD_DIM)
    vg = v.reshape(bsz, rows, GRID_W, B_HEADS, HEAD_DIM)
    r = np.arange(rows)
    row_idx = np.clip(r - B_KH // 2, 0, rows - kh)[:, None] + np.arange(kh)[None, :]
    drow = row_idx - r[:, None] + (B_KH - 1)
    outs = []
    for c0 in range(0, GRID_W, B_QW):
        cs = int(np.clip(c0 - B_KW // 2, 0, GRID_W - kcw))
        qcol = np.arange(c0, c0 + B_QW)
        kcol = np.arange(cs, cs + kcw)
        cstart = np.clip(qcol - B_KW // 2, 0, GRID_W - B_KW)
        cvalid = (kcol[None, :] >= cstart[:, None]) & (kcol[None, :] < cstart[:, None] + B_KW)
        mask = np.broadcast_to(cvalid[:, None, :], (B_QW, kh, kcw)).reshape(B_QW, kh * kcw)
        dcol = np.clip(kcol[None, :] - qcol[:, None], 1 - B_KW, B_KW - 1) + (B_KW - 1)
        bias = rpb[:, drow[:, None, :, None], dcol[None, :, None, :]]
        bias = jnp.transpose(bias, (1, 0, 2, 3, 4)).reshape(rows, B_HEADS, B_QW, kh * kcw)
        kb = kg[:, :, cs:cs + kcw][:, row_idx].reshape(bsz, rows, kh * kcw, B_HEADS, HEAD_DIM)
        vb = vg[:, :, cs:cs + kcw][:, row_idx].reshape(bsz, rows, kh * kcw, B_HEADS, HEAD_DIM)
        s = jnp.einsum('brqhd,brjhd->brhqj', qg[:, :, c0:c0 + B_QW], kb).astype(jnp.float32)
        s = jnp.where(mask, s + bias.astype(jnp.float32), NEG_INF)
        p = jax.nn.softmax(s, axis=-1).astype(v.dtype)
        outs.append(jnp.einsum('brhqj,brjhd->brqhd', p, vb))
    o = jnp.concatenate(outs, axis=2)
    return o.reshape(bsz, seq, B_HEADS * HEAD_DIM)


def diff_lambda_init(layer):
    return 0.8 - 0.6 * math.exp(-0.3 * layer)


def diff_attention(q, k, v, lam, norm_g, lam_init, cos, sin):
    bsz, seq = q.shape[:2]
    nb = seq // C_BLOCK
    q = apply_rope(q, cos, sin) * (HEAD_DIM ** -0.5)
    k = apply_rope(k, cos, sin)
    lf = lam.astype(jnp.float32)
    lam_val = jnp.exp(jnp.sum(lf[0] * lf[1])) - jnp.exp(jnp.sum(lf[2] * lf[3])) + lam_init
    qb = jnp.transpose(q.reshape(bsz, nb, C_BLOCK, C_HEADS, 2, HEAD_DIM), (1, 0, 2, 3, 4, 5))

    def block(qblk):
        s = jnp.einsum('bqhmd,bkhmd->bhmqk', qblk, k).astype(jnp.float32)
        p = jax.nn.softmax(s, axis=-1)
        a = (p[:, :, 0] - lam_val * p[:, :, 1]).astype(v.dtype)
        return jnp.einsum('bhqk,bkhe->bqhe', a, v)

    o = lax.map(block, qb)
    o = jnp.transpose(o, (1, 0, 2, 3, 4)).reshape(bsz, seq, C_HEADS, C_VDIM)
    of = o.astype(jnp.float32)
    of = of * lax.rsqrt(jnp.mean(jnp.square(of), axis=-1, keepdims=True) + LN_EPS)
    of = of.reshape(bsz, seq, C_HEADS * C_VDIM) * norm_g.astype(jnp.float32) * (1.0 - lam_init)
    return of.astype(v.dtype)


def _lin_combine(e1, e2):
    a1, b1 = e1
    a2, b2 = e2
    return a1 * a2, a2 * b1 + b2


def rg_lru_branch(xd, gate_in, conv_w, conv_b, wa, ba, wx, bx, lam):
    bsz, seq, width = xd.shape
    left = D_CONV // 2
    xc = lax.conv_general_dilated(xd, conv_w[:, None, :], window_strides=(1,),
                                  padding=((left, D_CONV - 1 - left),),
                                  dimension_numbers=('NWC', 'WIO', 'NWC'),
                                  feature_group_count=width) + conv_b
    xb = xc.reshape(bsz, seq, D_BLOCKS, width // D_BLOCKS)

    def direction(d, reverse):
        r = jax.nn.sigmoid(jnp.einsum('bsgi,gij->bsgj', xb, wa[d]).reshape(bsz, seq, width) + ba[d])
        i = jax.nn.sigmoid(jnp.einsum('bsgi,gij->bsgj', xb, wx[d]).reshape(bsz, seq, width) + bx[d])
        log_a = -LRU_C * r.astype(jnp.float32) * jax.nn.softplus(-lam[d].astype(jnp.float32))
        a = jnp.exp(log_a)
        b = jnp.sqrt(-jnp.expm1(2.0 * log_a)) * (i * xc).astype(jnp.float32)
        return lax.associative_scan(_lin_combine, (a, b), axis=1, reverse=reverse)[1]

    h = direction(0, False) + direction(1, True)
    return h.astype(xd.dtype) * jax.nn.gelu(gate_in)


def peer_ffn(h, wq, subkeys, u_tab, v_tab):
    bsz, seq, dm = h.shape
    t = bsz * seq
    ht = h.reshape(t, dm)
    q = (ht @ wq).reshape(t, PEER_HEADS, 2, PEER_QDIM // 2)
    s = jnp.einsum('thpd,hpkd->thpk', q, subkeys).astype(jnp.float32)
    sv, si = lax.top_k(s, PEER_TOPK)
    cand_s = (sv[:, :, 0, :, None] + sv[:, :, 1, None, :]).reshape(t, PEER_HEADS, PEER_TOPK * PEER_TOPK)
    cand_i = (si[:, :, 0, :, None] * PEER_KEYS + si[:, :, 1, None, :]).reshape(t, PEER_HEADS, PEER_TOPK * PEER_TOPK)
    top_s, top_j = lax.top_k(cand_s, PEER_TOPK)
    idx = jnp.take_along_axis(cand_i, top_j, axis=-1)
    g = jax.nn.softmax(top_s, axis=-1)
    nchunk = t // PEER_CHUNK

    def chunk(args):
        hc, ic, gc = args
        act = jax.nn.gelu(jnp.einsum('cd,chkd->chk', hc, u_tab[ic]))
        w = (gc * act.astype(jnp.float32)).astype(hc.dtype)
        return jnp.einsum('chk,chkd->cd', w, v_tab[ic])

    out = lax.map(chunk, (ht.reshape(nchunk, PEER_CHUNK, dm),
                          idx.reshape(nchunk, PEER_CHUNK, PEER_HEADS, PEER_TOPK),
                          g.reshape(nchunk, PEER_CHUNK, PEER_HEADS, PEER_TOPK)))
    return out.reshape(bsz, seq, dm)


def setup_inputs(seed: int = 0) -> dict:
    key = jax.random.key(seed)
    ks = jax.random.split(key, 26)
    f32 = jnp.float32

    def nrm(k, shape, scale):
        return jax.random.normal(k, shape, f32) * scale

    bs = D_WIDTH // D_BLOCKS
    u = jax.random.uniform(ks[18], (DEPTH, 2, D_WIDTH), f32, 0.9, 0.999)
    a0 = u ** (1.0 / LRU_C)
    return {
        'x': nrm(ks[0], (BATCH, SEQ, D_MODEL), 1.0),
        'c': nrm(ks[1], (BATCH, D_MODEL), 1.0),
        'w_ada': nrm(ks[2], (DEPTH, D_MODEL, 6 * D_MODEL), 0.5 * D_MODEL ** -0.5),
        'b_ada': nrm(ks[3], (DEPTH, 6 * D_MODEL), 0.02),
        'w_in': nrm(ks[4], (DEPTH, D_MODEL, IN_COLS), D_MODEL ** -0.5),
        'b_gate': nrm(ks[5], (DEPTH, N_BRANCH * D_MODEL), 0.02),
        'a_sink': nrm(ks[6], (DEPTH, A_HEADS), 0.5),
        'b_rpb': nrm(ks[7], (DEPTH, B_HEADS, 2 * B_KH - 1, 2 * B_KW - 1), 0.1),
        'c_lambda': nrm(ks[8], (DEPTH, 4, HEAD_DIM), 0.1),
        'c_norm_g': 1.0 + nrm(ks[9], (DEPTH, C_HEADS * C_VDIM), 0.02),
        'd_conv_w': nrm(ks[10], (DEPTH, D_CONV, D_WIDTH), D_CONV ** -0.5),
        'd_conv_b': nrm(ks[11], (DEPTH, D_WIDTH), 0.02),
        'd_wa': nrm(ks[12], (DEPTH, 2, D_BLOCKS, bs, bs), bs ** -0.5),
        'd_ba': nrm(ks[13], (DEPTH, 2, D_WIDTH), 0.02),
        'd_wx': nrm(ks[14], (DEPTH, 2, D_BLOCKS, bs, bs), bs ** -0.5),
        'd_bx': nrm(ks[15], (DEPTH, 2, D_WIDTH), 0.02),
        'd_lam': jnp.log(a0) - jnp.log1p(-a0),
        'w_branch': nrm(ks[16], (DEPTH, N_BRANCH, BRANCH_W, D_MODEL), BRANCH_W ** -0.5),
        'w_out': nrm(ks[17], (DEPTH, D_MODEL, D_MODEL), BETA * D_MODEL ** -0.5),
        'ln_g': 1.0 + nrm(ks[19], (DEPTH, 2, D_MODEL), 0.02),
        'ln_b': nrm(ks[20], (DEPTH, 2, D_MODEL), 0.02),
        'p_wq': nrm(ks[21], (DEPTH, D_MODEL, PEER_HEADS * PEER_QDIM), D_MODEL ** -0.5),
        'p_subkeys': nrm(ks[22], (DEPTH, PEER_HEADS, 2, PEER_KEYS, PEER_QDIM // 2), (PEER_QDIM // 2) ** -0.5),
        'p_u': nrm(ks[23], (DEPTH, PEER_N, D_MODEL), D_MODEL ** -0.5),
        'p_v': nrm(ks[24], (DEPTH, PEER_N, D_MODEL), BETA),
    }


def reference(x, c, w_ada, b_ada, w_in, b_gate, a_sink, b_rpb, c_lambda, c_norm_g,
              d_conv_w, d_conv_b, d_wa, d_ba, d_wx, d_bx, d_lam, w_branch, w_out,
              ln_g, ln_b, p_wq, p_subkeys, p_u, p_v):
    bsz, seq, _ = x.shape
    cos, sin = rope_tables(seq)
    c_act = jax.nn.silu(c)
    split_points = np.cumsum(IN_SPLITS)[:-1].tolist()
    for l in range(DEPTH):
        mod = (c_act @ w_ada[l] + b_ada[l]).reshape(bsz, 6, 1, D_MODEL)
        shift1, scale1, gate1 = mod[:, 0], mod[:, 1], mod[:, 2]
        shift2, scale2, gate2 = mod[:, 3], mod[:, 4], mod[:, 5]
        h = x * (1.0 + scale1) + shift1
        aq, ak, av, bq, bk, bv, cq, ck, cv, dx, dg, gl = jnp.split(h @ w_in[l], split_points, axis=-1)
        oa = window_gqa(aq.reshape(bsz, seq, A_HEADS, HEAD_DIM),
                        ak.reshape(bsz, seq, A_KV_HEADS, HEAD_DIM),
                        av.reshape(bsz, seq, A_KV_HEADS, HEAD_DIM), a_sink[l], cos, sin)
        ob = neighbourhood_attention(bq.reshape(bsz, seq, B_HEADS, HEAD_DIM),
                                     bk.reshape(bsz, seq, B_HEADS, HEAD_DIM),
                                     bv.reshape(bsz, seq, B_HEADS, HEAD_DIM), b_rpb[l])
        oc = diff_attention(cq.reshape(bsz, seq, C_HEADS, 2, HEAD_DIM),
                            ck.reshape(bsz, seq, C_HEADS, 2, HEAD_DIM),
                            cv.reshape(bsz, seq, C_HEADS, C_VDIM),
                            c_lambda[l], c_norm_g[l], diff_lambda_init(l), cos, sin)
        od = rg_lru_branch(dx, dg, d_conv_w[l], d_conv_b[l], d_wa[l], d_ba[l], d_wx[l], d_bx[l], d_lam[l])
        branches = jnp.stack([oa, ob, oc, od], axis=2)
        proj = jnp.einsum('bsne,ned->bsnd', branches, w_branch[l])
        gates = jax.nn.sigmoid(gl.reshape(bsz, seq, N_BRANCH, D_MODEL) + b_gate[l].reshape(N_BRANCH, D_MODEL))
        mix = jnp.sum(gates * proj, axis=2) @ w_out[l]
        x = layer_norm(ALPHA * x + gate1 * mix, ln_g[l, 0], ln_b[l, 0])
        h2 = x * (1.0 + scale2) + shift2
        y = peer_ffn(h2, p_wq[l], p_subkeys[l], p_u[l], p_v[l])
        x = layer_norm(ALPHA * x + gate2 * y, ln_g[l, 1], ln_b[l, 1])
    return x
```

```python
import math
import numpy as np
import concourse.bass as bass
import concourse.mybir as mybir
from concourse.bass_utils import run_bass_kernel_spmd

F32 = mybir.dt.float32
BF16 = mybir.dt.bfloat16
I32 = mybir.dt.int32
U32 = mybir.dt.uint32
AF = mybir.ActivationFunctionType
ALU = mybir.AluOpType
AX = mybir.AxisListType

D_MODEL = 1024
BATCH = 2
SEQ = 8192
DEPTH = 4
NCORES = 8
TPC = BATCH * SEQ // NCORES
ALPHA = (2.0 * DEPTH) ** 0.25
LN_EPS = 1e-5
NEG = -30000.0


class Buf:
    __slots__ = ("w", "r", "name")

    def __init__(self, name=""):
        self.w = None
        self.r = {}
        self.name = name


class Prog:
    ENGS = ("pe", "dve", "act", "pool", "sp")

    def __init__(self, nc):
        self.nc = nc
        self.ops = {e: [] for e in self.ENGS}
        self.cnt = {}
        self.waited = {e: {} for e in self.ENGS}
        self.dma_rr = {"sp": 0, "pool": 0, "act": 0}
        self.ndsem = 12

    def _waits(self, eng, deps):
        for (key, val) in deps:
            if key == "c_pe" and eng == "pe":
                continue
            if self.waited[eng].get(key, 0) >= val:
                continue
            self.waited[eng][key] = val
            self.ops[eng].append(("wait", key, val))

    def _deps(self, reads, writes):
        deps = {}
        def add(m):
            if m is None:
                return
            k, v = m
            if deps.get(k, 0) < v:
                deps[k] = v
        for b in reads:
            add(b.w)
        for b in writes:
            add(b.w)
            for k, v in b.r.items():
                add((k, v))
        return list(deps.items())

    def _mark(self, marker, reads, writes):
        k, v = marker
        for b in reads:
            if b.r.get(k, 0) < v:
                b.r[k] = v
        for b in writes:
            b.w = marker
            b.r = {}

    def op(self, eng, fn, reads=(), writes=()):
        self._waits(eng, self._deps(reads, writes))
        key = "c_" + eng
        self.cnt[key] = self.cnt.get(key, 0) + 1
        self.ops[eng].append(("op", fn, key, 1))
        self._mark((key, self.cnt[key]), reads, writes)

    def dma(self, q, out, in_, reads=(), writes=(), **kw):
        self._waits(q, self._deps(reads, writes))
        i = self.dma_rr[q]
        self.dma_rr[q] = (i + 1) % self.ndsem
        key = "d_%s%d" % (q, i)
        self.cnt[key] = self.cnt.get(key, 0) + 16
        self.ops[q].append(("op", lambda e: e.dma_start(out=out, in_=in_, **kw), key, 16))
        self._mark((key, self.cnt[key]), reads, writes)

    def finish(self, bufs):
        self._waits("sp", self._deps(bufs, ()))

    def emit(self, stack):
        nc = self.nc
        sems = {}
        for key in sorted(self.cnt):
            sems[key] = stack.enter_context(nc.semaphore(key))
        block = stack.enter_context(nc.Block())
        def run(eng):
            def body(e):
                for it in self.ops[eng]:
                    if it[0] == "wait":
                        e.wait_ge(sems[it[1]], it[2])
                    else:
                        it[1](e).then_inc(sems[it[2]], it[3])
            return body
        block.tensor(run("pe"))
        block.vector(run("dve"))
        block.scalar(run("act"))
        block.gpsimd(run("pool"))
        block.sync(run("sp"))


class Ring:
    def __init__(self, items):
        self.items = items
        self.i = 0

    def next(self):
        it = self.items[self.i]
        self.i = (self.i + 1) % len(self.items)
        return it


def _sb(stack, nc, name, shape, dt):
    return stack.enter_context(nc.sbuf_tensor(name, shape, dt))


def _psum_banks(stack, nc, n=8):
    return [(stack.enter_context(nc.psum_tensor("psb%d" % i, [128, 512], F32)), Buf("ps%d" % i)) for i in range(n)]


L1_NPAIR = 13
L1_NPLAIN = 16
L1_NF = L1_NPAIR + L1_NPLAIN
L1_WF_CHUNKS = 2 * L1_NPAIR + L1_NPLAIN
L1_TM = 1152


def build_l1():
    from contextlib import ExitStack
    nc = bass.Bass("TRN2", target_bir_lowering=False)
    T = TPC
    xT = nc.dram_tensor("xT", [1024, T], F32, kind="ExternalInput").ap()
    mod = nc.dram_tensor("mod", [128, 16], F32, kind="ExternalInput").ap()
    wf = nc.dram_tensor("wf", [1024, L1_WF_CHUNKS * 128], F32, kind="ExternalInput").ap()
    wt = nc.dram_tensor("wt", [1024, L1_TM], F32, kind="ExternalInput").ap()
    cs = nc.dram_tensor("cs", [128, 2, T], F32, kind="ExternalInput").ap()
    oF = nc.dram_tensor("oF", [L1_NF, 128, T], F32, kind="ExternalOutput").ap()
    oT = nc.dram_tensor("oT", [T, L1_TM], F32, kind="ExternalOutput").ap()
    with ExitStack() as st:
        P = Prog(nc)
        hT = _sb(st, nc, "hT", [128, 8, T], BF16); hT_b = [Buf() for _ in range(8)]
        xs = Ring([(_sb(st, nc, "xs%d" % i, [128, T], F32), Buf()) for i in range(2)])
        modt = _sb(st, nc, "modt", [128, 16], F32); mod_b = Buf()
        sc1p = _sb(st, nc, "sc1p", [128, 8], F32); sc1p_b = Buf()
        cst = _sb(st, nc, "cst", [128, 2, T], F32); cs_b = Buf()
        wtb = _sb(st, nc, "wtb", [128, 8, L1_TM], BF16); wt_b = Buf()
        wr = Ring([(_sb(st, nc, "wb%d" % i, [128, 8, 128], BF16), Buf()) for i in range(6)])
        stF = Ring([(_sb(st, nc, "stF%d" % i, [128, T], F32), Buf()) for i in range(3)])
        stT = Ring([(_sb(st, nc, "stT%d" % i, [128, L1_TM], F32), Buf()) for i in range(2)])
        t1r = Ring([(_sb(st, nc, "t1_%d" % i, [128, 512], F32), Buf()) for i in range(2)])
        t2r = Ring([(_sb(st, nc, "t2_%d" % i, [128, 512], F32), Buf()) for i in range(2)])
        banks = Ring(_psum_banks(st, nc))
        oF_b = Buf(); oT_b = Buf()

        P.dma("sp", modt[:], mod[:, :], writes=[mod_b])
        P.dma("sp", cst[:], cs[:, :, :], writes=[cs_b])
        P.op("dve", lambda e: e.tensor_scalar(out=sc1p[:], in0=modt[:, 0:8], scalar1=1.0, scalar2=None, op0=ALU.add),
             reads=[mod_b], writes=[sc1p_b])
        P.dma("pool", wtb[:], wt.rearrange("(k p) c -> p k c", p=128), writes=[wt_b])
        for k in range(8):
            xa, xb_ = xs.next()
            P.dma("sp", xa[:], xT[k * 128:(k + 1) * 128, :], writes=[xb_])
            P.op("act", lambda e, xa=xa, k=k: e.activation(out=hT[:, k, :], in_=xa[:], func=AF.Identity,
                                                          scale=sc1p[:, k:k + 1], bias=modt[:, 8 + k:9 + k]),
                 reads=[xb_, sc1p_b, mod_b], writes=[hT_b[k]])

        def load_w(c):
            wa, wb_ = wr.next()
            P.dma("pool", wa[:], wf[:, c * 128:(c + 1) * 128].rearrange("(k p) c -> p k c", p=128), writes=[wb_])
            return wa, wb_

        def mm_feat(wa, wb_, tb):
            ps, pb = banks.next()
            for k in range(8):
                P.op("pe", lambda e, ps=ps, wa=wa, k=k, tb=tb: e.matmul(
                    ps[:, :], wa[:, k, :], hT[:, k, tb * 512:(tb + 1) * 512], start=(k == 0), stop=(k == 7)),
                    reads=[wb_, hT_b[k]], writes=[pb])
            return ps, pb

        nevac = 0
        for u in range(L1_NF):
            sa, sb_ = stF.next()
            if u < L1_NPAIR:
                wA = load_w(2 * u); wB = load_w(2 * u + 1)
                for tb in range(4):
                    pA, pAb = mm_feat(wA[0], wA[1], tb)
                    pB, pBb = mm_feat(wB[0], wB[1], tb)
                    t1, t1b = t1r.next(); t2, t2b = t2r.next()
                    sl = slice(tb * 512, (tb + 1) * 512)
                    P.op("dve", lambda e, t1=t1, pA=pA, sl=sl: e.tensor_tensor(out=t1[:], in0=pA[:, :], in1=cst[:, 0, sl], op=ALU.mult),
                         reads=[pAb, cs_b], writes=[t1b])
                    P.op("dve", lambda e, t2=t2, pB=pB, sl=sl: e.tensor_tensor(out=t2[:], in0=pB[:, :], in1=cst[:, 1, sl], op=ALU.mult),
                         reads=[pBb, cs_b], writes=[t2b])
                    P.op("pool", lambda e, sa=sa, t1=t1, t2=t2, sl=sl: e.tensor_tensor(out=sa[:, sl], in0=t1[:], in1=t2[:], op=ALU.add),
                         reads=[t1b, t2b], writes=[sb_])
            else:
                wA = load_w(2 * L1_NPAIR + (u - L1_NPAIR))
                for tb in range(4):
                    pA, pAb = mm_feat(wA[0], wA[1], tb)
                    sl = slice(tb * 512, (tb + 1) * 512)
                    if nevac % 2 == 0:
                        P.op("act", lambda e, sa=sa, pA=pA, sl=sl: e.copy(out=sa[:, sl], in_=pA[:, :]), reads=[pAb], writes=[sb_])
                    else:
                        P.op("dve", lambda e, sa=sa, pA=pA, sl=sl: e.tensor_copy(out=sa[:, sl], in_=pA[:, :]), reads=[pAb], writes=[sb_])
                    nevac += 1
            P.dma("sp", oF[u, :, :], sa[:], reads=[sb_], writes=[oF_b])

        groups = [(0, 128), (128, 512), (640, 512)]
        for i in range(T // 128):
            sa, sb_ = stT.next()
            for (c0, n) in groups:
                ps, pb = banks.next()
                for k in range(8):
                    P.op("pe", lambda e, ps=ps, k=k, i=i, c0=c0, n=n: e.matmul(
                        ps[:, 0:n], hT[:, k, i * 128:(i + 1) * 128], wtb[:, k, c0:c0 + n], start=(k == 0), stop=(k == 7)),
                        reads=[wt_b, hT_b[k]], writes=[pb])
                if nevac % 2 == 0:
                    P.op("act", lambda e, sa=sa, ps=ps, c0=c0, n=n: e.copy(out=sa[:, c0:c0 + n], in_=ps[:, 0:n]), reads=[pb], writes=[sb_])
                else:
                    P.op("dve", lambda e, sa=sa, ps=ps, c0=c0, n=n: e.tensor_copy(out=sa[:, c0:c0 + n], in_=ps[:, 0:n]), reads=[pb], writes=[sb_])
                nevac += 1
            P.dma("sp", oT[i * 128:(i + 1) * 128, :], sa[:], reads=[sb_], writes=[oT_b])
        P.finish([oF_b, oT_b])
        P.emit(st)
    return nc


def build_c():
    from contextlib import ExitStack
    nc = bass.Bass("TRN2", target_bir_lowering=False)
    S = SEQ
    qT = nc.dram_tensor("qT", [128, S], F32, kind="ExternalInput").ap()
    kT = nc.dram_tensor("kT", [128, S], F32, kind="ExternalInput").ap()
    v = nc.dram_tensor("v", [S, 128], F32, kind="ExternalInput").ap()
    lamb = nc.dram_tensor("lamb", [128, 4, 64], F32, kind="ExternalInput").ap()
    cst = nc.dram_tensor("cst", [128, 2], F32, kind="ExternalInput").ap()
    ng = nc.dram_tensor("ng", [128, 128], F32, kind="ExternalInput").ap()
    oc = nc.dram_tensor("oc", [S, 128], F32, kind="ExternalOutput").ap()
    with ExitStack() as st:
        P = Prog(nc)
        qTb = _sb(st, nc, "qTb", [128, S], BF16); q_b = Buf()
        kTb = _sb(st, nc, "kTb", [128, S], BF16); k_b = Buf()
        va = _sb(st, nc, "va", [128, S // 128, 130], BF16); v_b = Buf()
        lt = _sb(st, nc, "lt", [128, 4, 64], F32); l_b = Buf()
        ct = _sb(st, nc, "ct", [128, 2], F32); c_b = Buf()
        ngt = _sb(st, nc, "ngt", [128, 128], F32); ng_b = Buf()
        gl = _sb(st, nc, "gl", [128, 128], F32); gl_b = Buf()
        junk = _sb(st, nc, "junk", [128, 128], F32); junk_b = Buf()
        sm = _sb(st, nc, "sm", [128, 8], F32); sm_b = Buf()
        nlam = _sb(st, nc, "nlam", [128, 1], F32); nlam_b = Buf()
        Er = Ring([(_sb(st, nc, "E%d" % i, [128, 512], BF16), Buf()) for i in range(4)])
        stg = Ring([(_sb(st, nc, "stg%d" % i, [128, 4, 128], F32), Buf()) for i in range(2)])
        tr = Ring([(_sb(st, nc, "tt%d" % i, [128, 128], F32), Buf()) for i in range(2)])
        orr = Ring([(_sb(st, nc, "oo%d" % i, [128, 128], F32), Buf()) for i in range(2)])
        sr = Ring([(_sb(st, nc, "ss%d" % i, [128, 8], F32), Buf()) for i in range(2)])
        banks = _psum_banks(st, nc)
        accb = banks[:3]
        stb = Ring(banks[3:])
        oc_b = Buf()

        P.dma("pool", qTb[:], qT[:, :], writes=[q_b])
        P.dma("pool", kTb[:], kT[:, :], writes=[k_b])
        P.dma("pool", va[:, :, 0:128], v.rearrange("(n p) d -> p n d", p=128), writes=[v_b])
        P.op("dve", lambda e: e.memset(va[:, :, 128:130], 1.0), writes=[v_b])
        P.dma("sp", lt[:], lamb[:, :, :], writes=[l_b])
        P.dma("sp", ct[:], cst[:, :], writes=[c_b])
        P.dma("sp", ngt[:], ng[:, :], writes=[ng_b])
        P.op("dve", lambda e: e.tensor_tensor(out=junk[:, 0:64], in0=lt[:, 0, :], in1=lt[:, 1, :], op=ALU.mult), reads=[l_b], writes=[junk_b])
        P.op("dve", lambda e: e.reduce_sum(out=sm[:, 0:1], in_=junk[:, 0:64], axis=AX.X), reads=[junk_b], writes=[sm_b])
        P.op("dve", lambda e: e.tensor_tensor(out=junk[:, 64:128], in0=lt[:, 2, :], in1=lt[:, 3, :], op=ALU.mult), reads=[l_b], writes=[junk_b])
        P.op("dve", lambda e: e.reduce_sum(out=sm[:, 1:2], in_=junk[:, 64:128], axis=AX.X), reads=[junk_b], writes=[sm_b])
        P.op("act", lambda e: e.activation(out=sm[:, 2:4], in_=sm[:, 0:2], func=AF.Exp), reads=[sm_b], writes=[sm_b])
        P.op("dve", lambda e: e.tensor_tensor(out=sm[:, 4:5], in0=sm[:, 3:4], in1=sm[:, 2:3], op=ALU.subtract), reads=[sm_b], writes=[sm_b])
        P.op("dve", lambda e: e.tensor_tensor(out=nlam[:], in0=sm[:, 4:5], in1=ct[:, 0:1], op=ALU.subtract), reads=[sm_b, c_b], writes=[nlam_b])
        P.op("dve", lambda e: e.tensor_scalar(out=gl[:], in0=ngt[:], scalar1=ct[:, 1:2], scalar2=None, op0=ALU.mult), reads=[ng_b, c_b], writes=[gl_b])

        def acc_ap(m, qb):
            a = m * 4 + qb
            return accb[a // 3][0], accb[a // 3][1], (a % 3) * 130

        NG = S // 512
        NK = S // 128
        for g in range(NG):
            started = set()
            for kc in range(NK):
                for m in range(2):
                    ps, pb = stb.next()
                    P.op("pe", lambda e, ps=ps, m=m, kc=kc, g=g: e.matmul(
                        ps[:, :], kTb[m * 64:(m + 1) * 64, kc * 128:(kc + 1) * 128], qTb[m * 64:(m + 1) * 64, g * 512:(g + 1) * 512],
                        start=True, stop=True), reads=[k_b, q_b], writes=[pb])
                    E, Eb = Er.next()
                    P.op("act", lambda e, E=E, ps=ps: e.activation(out=E[:], in_=ps[:, :], func=AF.Exp, scale=0.125), reads=[pb], writes=[Eb])
                    for qb in range(4):
                        acc, ab, off = acc_ap(m, qb)
                        bi = (m * 4 + qb) // 3
                        first = bi not in started
                        started.add(bi)
                        P.op("pe", lambda e, acc=acc, off=off, E=E, qb=qb, kc=kc, first=first: e.matmul(
                            acc[:, off:off + 129], E[:, qb * 128:(qb + 1) * 128], va[:, kc, 0:129],
                            start=first, stop=(kc == NK - 1), skip_group_check=True), reads=[Eb, v_b], writes=[ab])
            sa, sb_ = stg.next()
            for qb in range(4):
                a0, a0b, o0 = acc_ap(0, qb)
                a1, a1b, o1 = acc_ap(1, qb)
                ss, ssb = sr.next(); tt, ttb = tr.next(); oo, oob = orr.next()
                P.op("dve", lambda e, ss=ss, a0=a0, o0=o0: e.reciprocal(out=ss[:, 0:1], in_=a0[:, o0 + 128:o0 + 129]), reads=[a0b], writes=[ssb])
                P.op("dve", lambda e, ss=ss, a1=a1, o1=o1: e.reciprocal(out=ss[:, 1:2], in_=a1[:, o1 + 128:o1 + 129]), reads=[a1b], writes=[ssb])
                P.op("dve", lambda e, ss=ss: e.tensor_tensor(out=ss[:, 2:3], in0=ss[:, 1:2], in1=nlam[:], op=ALU.mult), reads=[ssb, nlam_b], writes=[ssb])
                P.op("dve", lambda e, tt=tt, a1=a1, o1=o1, ss=ss: e.tensor_scalar(out=tt[:], in0=a1[:, o1:o1 + 128], scalar1=ss[:, 2:3], scalar2=None, op0=ALU.mult),
                     reads=[a1b, ssb], writes=[ttb])
                P.op("dve", lambda e, oo=oo, a0=a0, o0=o0, ss=ss, tt=tt: e.scalar_tensor_tensor(
                    out=oo[:], in0=a0[:, o0:o0 + 128], scalar=ss[:, 0:1], in1=tt[:], op0=ALU.mult, op1=ALU.add), reads=[a0b, ssb, ttb], writes=[oob])
                P.op("act", lambda e, oo=oo, ss=ss: e.activation(out=junk[:], in_=oo[:], func=AF.Square, accum_out=ss[:, 3:4]),
                     reads=[oob], writes=[junk_b, ssb])
                P.op("act", lambda e, ss=ss: e.activation(out=ss[:, 4:5], in_=ss[:, 3:4], func=AF.Sqrt, scale=1.0 / 128.0, bias=LN_EPS), reads=[ssb], writes=[ssb])
                P.op("dve", lambda e, ss=ss: e.reciprocal(out=ss[:, 5:6], in_=ss[:, 4:5]), reads=[ssb], writes=[ssb])
                P.op("dve", lambda e, sa=sa, qb=qb, oo=oo, ss=ss: e.scalar_tensor_tensor(
                    out=sa[:, qb, :], in0=oo[:], scalar=ss[:, 5:6], in1=gl[:], op0=ALU.mult, op1=ALU.mult), reads=[oob, ssb, gl_b], writes=[sb_])
            P.dma("sp", oc.rearrange("(n p) d -> p n d", p=128)[:, g * 4:(g + 1) * 4, :], sa[:], reads=[sb_], writes=[oc_b])
        P.finish([oc_b])
        P.emit(st)
    return nc


def build_a():
    from contextlib import ExitStack
    nc = bass.Bass("TRN2", target_bir_lowering=False)
    S = SEQ
    NB = S // 128
    qT = nc.dram_tensor("qT", [128, S], F32, kind="ExternalInput").ap()
    kT = nc.dram_tensor("kT", [128, S], F32, kind="ExternalInput").ap()
    v = nc.dram_tensor("v", [S, 64], F32, kind="ExternalInput").ap()
    msk = nc.dram_tensor("msk", [128, 384], F32, kind="ExternalInput").ap()
    idn = nc.dram_tensor("idn", [128, 128], F32, kind="ExternalInput").ap()
    snk = nc.dram_tensor("snk", [128, 2], F32, kind="ExternalInput").ap()
    oa = nc.dram_tensor("oa", [S, 128], F32, kind="ExternalOutput").ap()
    with ExitStack() as st:
        P = Prog(nc)
        qTb = _sb(st, nc, "qTb", [128, S], BF16); q_b = Buf()
        kTb = _sb(st, nc, "kTb", [128, S], BF16); k_b = Buf()
        va = _sb(st, nc, "va", [128, NB, 66], BF16); v_b = Buf()
        mb = _sb(st, nc, "mb", [128, 384], BF16); m_b = Buf()
        ib = _sb(st, nc, "ib", [128, 128], BF16); i_b = Buf()
        sk = _sb(st, nc, "sk", [128, 2], F32); sk_b = Buf()
        esk = _sb(st, nc, "esk", [128, 2], F32); esk_b = Buf()
        Er = Ring([(_sb(st, nc, "E%d" % i, [128, 384], BF16), Buf()) for i in range(4)])
        sr = Ring([(_sb(st, nc, "ss%d" % i, [128, 2], F32), Buf()) for i in range(4)])
        stg = _sb(st, nc, "stg", [128, NB, 128], F32); stg_b = Buf()
        banks = _psum_banks(st, nc)
        accb = banks[:4]
        stb = Ring(banks[4:])
        oa_b = Buf()
        P.dma("pool", qTb[:], qT[:, :], writes=[q_b])
        P.dma("pool", kTb[:], kT[:, :], writes=[k_b])
        P.dma("pool", va[:, :, 0:64], v.rearrange("(n p) d -> p n d", p=128), writes=[v_b])
        P.op("dve", lambda e: e.memset(va[:, :, 64:66], 1.0), writes=[v_b])
        P.dma("pool", mb[:], msk[:, :], writes=[m_b])
        P.dma("pool", ib[:], idn[:, :], writes=[i_b])
        P.dma("sp", sk[:], snk[:, :], writes=[sk_b])
        P.op("act", lambda e: e.activation(out=esk[:], in_=sk[:], func=AF.Exp), reads=[sk_b], writes=[esk_b])
        for hh in range(2):
            pr = slice(hh * 64, (hh + 1) * 64)
            for m in range(NB):
                lo = max(m - 1, 0); hi = min(m + 1, NB - 1)
                ncol = (hi - lo + 1) * 128
                off = (lo - (m - 1)) * 128
                ps, pb = stb.next()
                P.op("pe", lambda e, ps=ps, pr=pr, m=m, lo=lo, hi=hi, ncol=ncol: e.matmul(
                    ps[:, 0:ncol], kTb[pr, m * 128:(m + 1) * 128], qTb[pr, lo * 128:(hi + 1) * 128], start=True, stop=False),
                    reads=[k_b, q_b], writes=[pb])
                P.op("pe", lambda e, ps=ps, off=off, ncol=ncol: e.matmul(
                    ps[:, 0:ncol], ib[:, :], mb[:, off:off + ncol], start=False, stop=True), reads=[i_b, m_b], writes=[pb])
                E, Eb = Er.next()
                P.op("act", lambda e, E=E, ps=ps, ncol=ncol: e.activation(out=E[:, 0:ncol], in_=ps[:, 0:ncol], func=AF.Exp, scale=0.125),
                     reads=[pb], writes=[Eb])
                for qb in range(lo, hi + 1):
                    acc, ab = accb[qb % 4]
                    P.op("pe", lambda e, acc=acc, E=E, qb=qb, lo=lo, m=m: e.matmul(
                        acc[:, 0:65], E[:, (qb - lo) * 128:(qb - lo + 1) * 128], va[:, m, 0:65],
                        start=(m == max(qb - 1, 0)), stop=(m == min(qb + 1, NB - 1))), reads=[Eb, v_b], writes=[ab])
                done = [qb for qb in range(lo, hi + 1) if min(qb + 1, NB - 1) == m]
                for qb in done:
                    acc, ab = accb[qb % 4]
                    ss, ssb = sr.next()
                    P.op("dve", lambda e, ss=ss, acc=acc, hh=hh: e.tensor_tensor(out=ss[:, 0:1], in0=acc[:, 64:65], in1=esk[:, hh:hh + 1], op=ALU.add),
                         reads=[ab, esk_b], writes=[ssb])
                    P.op("dve", lambda e, ss=ss: e.reciprocal(out=ss[:, 1:2], in_=ss[:, 0:1]), reads=[ssb], writes=[ssb])
                    P.op("dve", lambda e, ss=ss, acc=acc, qb=qb, hh=hh: e.tensor_scalar(
                        out=stg[:, qb, hh * 64:(hh + 1) * 64], in0=acc[:, 0:64], scalar1=ss[:, 1:2], scalar2=None, op0=ALU.mult),
                        reads=[ab, ssb], writes=[stg_b])
        P.dma("sp", oa.rearrange("(n p) d -> p n d", p=128), stg[:], reads=[stg_b], writes=[oa_b])
        P.finish([oa_b])
        P.emit(st)
    return nc


def a_mask_np():
    kj = np.arange(128)[:, None]; qi = np.arange(128)[None, :]
    m = np.zeros((128, 384), np.float32)
    m[:, 0:128] = np.where(kj <= qi, 0.0, NEG)
    m[:, 256:384] = np.where(kj >= qi, 0.0, NEG)
    return m


def build_b():
    from contextlib import ExitStack
    nc = bass.Bass("TRN2", target_bir_lowering=False)
    S = SEQ
    NB = S // 128
    ROWS = S // 64
    qT = nc.dram_tensor("qT", [128, S], F32, kind="ExternalInput").ap()
    kT = nc.dram_tensor("kT", [128, S], F32, kind="ExternalInput").ap()
    v = nc.dram_tensor("v", [S, 128], F32, kind="ExternalInput").ap()
    vsh = nc.dram_tensor("vsh", [S, 128], F32, kind="ExternalInput").ap()
    bias = nc.dram_tensor("bias", [128, 2 * 8 * 256], F32, kind="ExternalInput").ap()
    idn = nc.dram_tensor("idn", [128, 128], F32, kind="ExternalInput").ap()
    ob = nc.dram_tensor("ob", [S, 128], F32, kind="ExternalOutput").ap()
    with ExitStack() as st:
        P = Prog(nc)
        qTb = _sb(st, nc, "qTb", [128, S], BF16); q_b = Buf()
        kTb = _sb(st, nc, "kTb", [128, S], BF16); k_b = Buf()
        va = [_sb(st, nc, "va%d" % i, [128, NB, 2, 66], BF16) for i in range(2)]; v_b = Buf()
        bb = _sb(st, nc, "bb", [128, 2, 8, 256], BF16); b_b = Buf()
        ib = _sb(st, nc, "ib", [128, 128], BF16); i_b = Buf()
        Er = Ring([(_sb(st, nc, "E%d" % i, [128, 256], BF16), Buf()) for i in range(4)])
        sr = Ring([(_sb(st, nc, "ss%d" % i, [64, 2], F32), Buf()) for i in range(4)])
        stg = _sb(st, nc, "stg", [64, ROWS, 128], F32); stg_b = Buf()
        banks = _psum_banks(st, nc)
        accb = Ring(banks[:4])
        stb = Ring(banks[4:])
        ob_b = Buf()
        P.dma("pool", qTb[:], qT[:, :], writes=[q_b])
        P.dma("pool", kTb[:], kT[:, :], writes=[k_b])
        for i, src in enumerate((v, vsh)):
            for h2 in range(2):
                P.dma("pool", va[i][:, :, h2, 0:64], src[:, h2 * 64:(h2 + 1) * 64].rearrange("(n p) d -> p n d", p=128), writes=[v_b])
            P.op("dve", lambda e, i=i: e.memset(va[i][:, :, :, 64:66], 1.0), writes=[v_b])
        P.dma("pool", bb[:], bias.rearrange("p (h c f) -> p h c f", h=2, c=8), writes=[b_b])
        P.dma("pool", ib[:], idn[:, :], writes=[i_b])
        for hh in range(2):
            pr = slice(hh * 64, (hh + 1) * 64)
            for r in range(ROWS):
                r0 = min(max(r - 4, 0), ROWS - 8)
                cls = r if r < 4 else (4 if r <= ROWS - 4 else r - (ROWS - 8))
                ps, pb = stb.next()
                for c4 in range(4):
                    kt = 64 * r0 + 128 * c4
                    P.op("pe", lambda e, ps=ps, pr=pr, kt=kt, r=r, c4=c4: e.matmul(
                        ps[:, c4 * 64:(c4 + 1) * 64], kTb[pr, kt:kt + 128], qTb[pr, r * 64:(r + 1) * 64],
                        start=(c4 == 0), stop=False, skip_group_check=True), reads=[k_b, q_b], writes=[pb])
                P.op("pe", lambda e, ps=ps, hh=hh, cls=cls: e.matmul(
                    ps[:, 0:256], ib[:, :], bb[:, hh, cls, :], start=False, stop=True, skip_group_check=True), reads=[i_b, b_b], writes=[pb])
                E, Eb = Er.next()
                P.op("act", lambda e, E=E, ps=ps: e.activation(out=E[:, :], in_=ps[:, 0:256], func=AF.Exp, scale=0.125), reads=[pb], writes=[Eb])
                acc, ab = accb.next()
                vsel = va[r0 % 2]
                for c4 in range(4):
                    n = r0 // 2 + c4
                    P.op("pe", lambda e, acc=acc, E=E, c4=c4, vsel=vsel, n=n, hh=hh: e.matmul(
                        acc[0:64, 0:65], E[:, c4 * 64:(c4 + 1) * 64], vsel[:, n, hh, 0:65], start=(c4 == 0), stop=(c4 == 3)),
                        reads=[Eb, v_b], writes=[ab])
                ss, ssb = sr.next()
                P.op("dve", lambda e, ss=ss, acc=acc: e.reciprocal(out=ss[:, 0:1], in_=acc[0:64, 64:65]), reads=[ab], writes=[ssb])
                P.op("dve", lambda e, ss=ss, acc=acc, r=r, hh=hh: e.tensor_scalar(
                    out=stg[:, r, hh * 64:(hh + 1) * 64], in0=acc[0:64, 0:64], scalar1=ss[:, 0:1], scalar2=None, op0=ALU.mult),
                    reads=[ab, ssb], writes=[stg_b])
        P.dma("sp", ob.rearrange("(r p) d -> p r d", p=64), stg[:], reads=[stg_b], writes=[ob_b])
        P.finish([ob_b])
        P.emit(st)
    return nc


def b_bias_np(rpb2):
    ROWS = SEQ // 64
    out = np.full((2, 8, 4, 128, 64), NEG, np.float32)
    reps = [0, 1, 2, 3, 4, ROWS - 3, ROWS - 2, ROWS - 1]
    qc = np.arange(64)
    cstart = np.clip(qc - 8, 0, 64 - 16)
    for ci, r in enumerate(reps):
        r0 = min(max(r - 4, 0), ROWS - 8)
        for c4 in range(4):
            for p in range(128):
                krow = r0 + 2 * c4 + p // 64
                kcol = p % 64
                drow = krow - r + 7
                valid = (kcol >= cstart) & (kcol < cstart + 16)
                dcol = np.clip(kcol - qc, -15, 15) + 15
                for h in range(2):
                    out[h, ci, c4, p, :] = np.where(valid, rpb2[h, drow, dcol], NEG)
    return np.ascontiguousarray(out.transpose(3, 0, 1, 2, 4)).reshape(128, 2 * 8 * 256)


def build_d(dbg=9):
    from contextlib import ExitStack
    nc = bass.Bass("TRN2", target_bir_lowering=False)
    S = SEQ
    CH = 2048
    NCH = S // CH
    dxT = nc.dram_tensor("dxT", [128, S], F32, kind="ExternalInput").ap()
    dgT = nc.dram_tensor("dgT", [128, S], F32, kind="ExternalInput").ap()
    wbd = nc.dram_tensor("wbd", [128, 4, 128], F32, kind="ExternalInput").ap()
    par = nc.dram_tensor("par", [128, 12], F32, kind="ExternalInput").ap()
    odT = nc.dram_tensor("odT", [128, S], F32, kind="ExternalOutput").ap()
    with ExitStack() as st:
        P = Prog(nc)
        X = _sb(st, nc, "X", [128, S], F32); X_b = Buf()
        XC = _sb(st, nc, "XC", [128, S], F32); XC_b = Buf()
        XCb = _sb(st, nc, "XCb", [128, S], BF16); XCb_b = Buf()
        wb = _sb(st, nc, "wb", [128, 4, 128], BF16); w_b = Buf()
        pt = _sb(st, nc, "pt", [128, 12], F32); p_b = Buf()
        sm = _sb(st, nc, "sm", [128, 8], F32); sm_b = Buf()
        HF_b = [Buf() for _ in range(NCH)]
        Rr = Ring([(_sb(st, nc, "R%d" % i, [128, CH], F32), Buf()) for i in range(2)])
        Ar = Ring([(_sb(st, nc, "A%d" % i, [128, CH], F32), Buf()) for i in range(2)])
        Ir = Ring([(_sb(st, nc, "I%d" % i, [128, CH], F32), Buf()) for i in range(2)])
        Sr = Ring([(_sb(st, nc, "S%d" % i, [128, CH], F32), Buf()) for i in range(2)])
        Hr = Ring([(_sb(st, nc, "H%d" % i, [128, CH], F32), Buf()) for i in range(2)])
        Gr = Ring([(_sb(st, nc, "G%d" % i, [128, CH], F32), Buf()) for i in range(2)])
        carry = _sb(st, nc, "carry", [128, 2], F32); carry_b = Buf()
        banks = Ring(_psum_banks(st, nc))
        od_b = Buf()
        P.dma("sp", X[:], dxT[:, :], writes=[X_b])
        P.dma("sp", pt[:], par[:, :], writes=[p_b])
        P.dma("pool", wb[:], wbd[:, :, :], writes=[w_b])
        P.op("act", lambda e: e.activation(out=sm[:, 0:2], in_=pt[:, 9:11], func=AF.Exp, scale=-1.0), reads=[p_b], writes=[sm_b])
        P.op("dve", lambda e: e.tensor_scalar(out=sm[:, 2:4], in0=sm[:, 0:2], scalar1=1.0, scalar2=None, op0=ALU.add), reads=[sm_b], writes=[sm_b])
        P.op("act", lambda e: e.activation(out=sm[:, 4:6], in_=sm[:, 2:4], func=AF.Ln), reads=[sm_b], writes=[sm_b])
        P.op("dve", lambda e: e.tensor_scalar(out=sm[:, 6:8], in0=sm[:, 4:6], scalar1=-8.0, scalar2=None, op0=ALU.mult), reads=[sm_b], writes=[sm_b])
        P.op("dve", lambda e: e.tensor_scalar(out=XC[:], in0=X[:], scalar1=pt[:, 2:3], scalar2=pt[:, 4:5], op0=ALU.mult, op1=ALU.add),
             reads=[X_b, p_b], writes=[XC_b])
        P.op("dve", lambda e: e.scalar_tensor_tensor(out=XC[:, 2:S], in0=X[:, 0:S - 2], scalar=pt[:, 0:1], in1=XC[:, 2:S], op0=ALU.mult, op1=ALU.add),
             reads=[X_b, p_b], writes=[XC_b])
        P.op("dve", lambda e: e.scalar_tensor_tensor(out=XC[:, 1:S], in0=X[:, 0:S - 1], scalar=pt[:, 1:2], in1=XC[:, 1:S], op0=ALU.mult, op1=ALU.add),
             reads=[X_b, p_b], writes=[XC_b])
        P.op("dve", lambda e: e.scalar_tensor_tensor(out=XC[:, 0:S - 1], in0=X[:, 1:S], scalar=pt[:, 3:4], in1=XC[:, 0:S - 1], op0=ALU.mult, op1=ALU.add),
             reads=[X_b, p_b], writes=[XC_b])
        P.op("act", lambda e: e.copy(out=XCb[:], in_=XC[:]), reads=[XC_b], writes=[XCb_b])

        if dbg == 0:
            P.dma("sp", odT[:, :], XC[:], reads=[XC_b, XCb_b, sm_b], writes=[od_b])
            P.finish([od_b]); P.emit(st)
            return nc

        def gate(dst, dstb, gi, bcol, c):
            for j in range(CH // 512):
                t0 = c * CH + j * 512
                ps, pb = banks.next()
                P.op("pe", lambda e, ps=ps, gi=gi, t0=t0: e.matmul(ps[:, :], wb[:, gi, :], XCb[:, t0:t0 + 512], start=True, stop=True),
                     reads=[w_b, XCb_b], writes=[pb])
                P.op("act", lambda e, ps=ps, dst=dst, j=j, bcol=bcol: e.activation(
                    out=dst[:, j * 512:(j + 1) * 512], in_=ps[:, :], func=AF.Sigmoid, bias=pt[:, bcol:bcol + 1]),
                    reads=[pb, p_b], writes=[dstb])

        def prep(d, c):
            R, Rb = Rr.next(); A, Ab = Ar.next(); I, Ib = Ir.next(); S2, S2b = Sr.next()
            gate(R, Rb, 2 * d, 5 + 2 * d, c)
            P.op("act", lambda e, A=A, R=R, d=d: e.activation(out=A[:], in_=R[:], func=AF.Exp, scale=sm[:, 6 + d:7 + d]), reads=[Rb, sm_b], writes=[Ab])
            gate(I, Ib, 2 * d + 1, 6 + 2 * d, c)
            P.op("dve", lambda e, I=I, c=c: e.tensor_tensor(out=I[:], in0=I[:], in1=XC[:, c * CH:(c + 1) * CH], op=ALU.mult), reads=[Ib, XC_b], writes=[Ib])
            P.op("pool", lambda e, S2=S2, A=A: e.tensor_tensor(out=S2[:], in0=A[:], in1=A[:], op=ALU.mult), reads=[Ab], writes=[S2b])
            P.op("dve", lambda e, S2=S2: e.tensor_scalar(out=S2[:], in0=S2[:], scalar1=-1.0, scalar2=1.0, op0=ALU.mult, op1=ALU.add), reads=[S2b], writes=[S2b])
            P.op("dve", lambda e, S2=S2: e.tensor_scalar(out=S2[:], in0=S2[:], scalar1=0.0, scalar2=None, op0=ALU.max), reads=[S2b], writes=[S2b])
            P.op("act", lambda e, S2=S2: e.activation(out=S2[:], in_=S2[:], func=AF.Sqrt), reads=[S2b], writes=[S2b])
            P.op("pool", lambda e, S2=S2, I=I: e.tensor_tensor(out=S2[:], in0=S2[:], in1=I[:], op=ALU.mult), reads=[S2b, Ib], writes=[S2b])
            return A, Ab, S2, S2b

        for c in range(NCH):
            A, Ab, Bv, Bvb = prep(0, c)
            sl = slice(c * CH, (c + 1) * CH)
            if c == 0:
                P.op("dve", lambda e, A=A, Bv=Bv, sl=sl: e.tensor_tensor_scan(out=X[:, sl], data0=A[:], data1=Bv[:], initial=0.0, op0=ALU.mult, op1=ALU.add),
                     reads=[Ab, Bvb, XC_b], writes=[X_b, HF_b[c]])
            else:
                P.op("dve", lambda e, A=A, Bv=Bv, sl=sl, c=c: e.tensor_tensor_scan(
                    out=X[:, sl], data0=A[:], data1=Bv[:], initial=X[:, c * CH - 1:c * CH], op0=ALU.mult, op1=ALU.add),
                    reads=[Ab, Bvb, HF_b[c - 1]], writes=[HF_b[c]])
        if dbg == 1:
            P.dma("sp", odT[:, :], X[:], reads=HF_b, writes=[od_b])
            P.finish([od_b]); P.emit(st)
            return nc
        for c in range(NCH - 1, -1, -1):
            A, Ab, Bv, Bvb = prep(1, c)
            H, Hb = Hr.next(); G, Gb = Gr.next()
            sl = slice(c * CH, (c + 1) * CH)
            P.dma("sp", G[:], dgT[:, sl], writes=[Gb])
            if c == NCH - 1:
                P.op("dve", lambda e, A=A, Bv=Bv, H=H: e.tensor_tensor_scan(
                    out=H[:, ::-1], data0=A[:, ::-1], data1=Bv[:, ::-1], initial=0.0, op0=ALU.mult, op1=ALU.add), reads=[Ab, Bvb], writes=[Hb])
            else:
                P.op("dve", lambda e, A=A, Bv=Bv, H=H: e.tensor_tensor_scan(
                    out=H[:, ::-1], data0=A[:, ::-1], data1=Bv[:, ::-1], initial=carry[:, 0:1], op0=ALU.mult, op1=ALU.add),
                    reads=[Ab, Bvb, carry_b], writes=[Hb])
            P.op("dve", lambda e, H=H: e.tensor_copy(out=carry[:, 0:1], in_=H[:, 0:1]), reads=[Hb], writes=[carry_b])
            P.op("pool", lambda e, H=H, sl=sl: e.tensor_tensor(out=H[:], in0=H[:], in1=X[:, sl], op=ALU.add), reads=[Hb, HF_b[c]], writes=[Hb])
            if dbg != 2:
                T1, T1b = Rr.next()
                P.op("pool", lambda e, T1=T1, G=G: e.tensor_tensor(out=T1[:], in0=G[:], in1=G[:], op=ALU.mult), reads=[Gb], writes=[T1b])
                P.op("dve", lambda e, T1=T1: e.tensor_scalar(out=T1[:], in0=T1[:], scalar1=0.044715, scalar2=1.0, op0=ALU.mult, op1=ALU.add), reads=[T1b], writes=[T1b])
                P.op("pool", lambda e, T1=T1, G=G: e.tensor_tensor(out=T1[:], in0=T1[:], in1=G[:], op=ALU.mult), reads=[Gb, T1b], writes=[T1b])
                P.op("act", lambda e, T1=T1: e.activation(out=T1[:], in_=T1[:], func=AF.Sigmoid, scale=2.0 * math.sqrt(2.0 / math.pi)), reads=[T1b], writes=[T1b])
                P.op("dve", lambda e, H=H, G=G: e.tensor_tensor(out=G[:], in0=H[:], in1=G[:], op=ALU.mult), reads=[Hb, Gb], writes=[Gb])
                P.op("pool", lambda e, T1=T1, G=G: e.tensor_tensor(out=G[:], in0=T1[:], in1=G[:], op=ALU.mult), reads=[Gb, T1b], writes=[Gb])
            else:
                P.op("dve", lambda e, H=H, G=G: e.tensor_copy(out=G[:], in_=H[:]), reads=[Hb, Gb], writes=[Gb])
            P.dma("sp", odT[:, sl], G[:], reads=[Gb], writes=[od_b])
        P.finish([od_b])
        P.emit(st)
    return nc


def _layernorm_tile(P, y, yb, out, outb, lngt, lnbt, ln_b, stt, mv, smb, tmp, tmpb):
    P.op("dve", lambda e: e.bn_stats(out=stt[:, 0, :], in_=y[:, 0:512]), reads=[yb], writes=[smb])
    P.op("dve", lambda e: e.bn_stats(out=stt[:, 1, :], in_=y[:, 512:1024]), reads=[yb], writes=[smb])
    P.op("dve", lambda e: e.bn_aggr(out=mv[:, 0:2], in_=stt[:, :, :]), reads=[smb], writes=[smb])
    P.op("act", lambda e: e.activation(out=mv[:, 2:3], in_=mv[:, 1:2], func=AF.Sqrt, bias=LN_EPS), reads=[smb], writes=[smb])
    P.op("dve", lambda e: e.reciprocal(out=mv[:, 3:4], in_=mv[:, 2:3]), reads=[smb], writes=[smb])
    P.op("dve", lambda e: e.scalar_tensor_tensor(out=mv[:, 4:5], in0=mv[:, 0:1], scalar=-1.0, in1=mv[:, 3:4], op0=ALU.mult, op1=ALU.mult),
         reads=[smb], writes=[smb])
    P.op("act", lambda e: e.activation(out=tmp[:], in_=y[:], func=AF.Identity, scale=mv[:, 3:4], bias=mv[:, 4:5]), reads=[yb, smb], writes=[tmpb])
    P.op("dve", lambda e: e.tensor_tensor(out=tmp[:], in0=tmp[:], in1=lngt[:], op=ALU.mult), reads=[tmpb, ln_b], writes=[tmpb])
    P.op("pool", lambda e: e.tensor_tensor(out=out[:], in0=tmp[:], in1=lnbt[:], op=ALU.add), reads=[tmpb, ln_b], writes=[outb])


def build_m():
    from contextlib import ExitStack
    nc = bass.Bass("TRN2", target_bir_lowering=False)
    T = TPC
    TH = T // 2
    xT = nc.dram_tensor("xT", [1024, T], F32, kind="ExternalInput").ap()
    x = nc.dram_tensor("x", [T, 1024], F32, kind="ExternalInput").ap()
    brT = nc.dram_tensor("brT", [2048, T], F32, kind="ExternalInput").ap()
    mod = nc.dram_tensor("mod", [128, 16], F32, kind="ExternalInput").ap()
    g1 = nc.dram_tensor("g1", [128, 1024], F32, kind="ExternalInput").ap()
    wg = nc.dram_tensor("wg", [1024, 4096], F32, kind="ExternalInput").ap()
    bg = nc.dram_tensor("bg", [128, 32], F32, kind="ExternalInput").ap()
    wbr = nc.dram_tensor("wbr", [2048, 1024], F32, kind="ExternalInput").ap()
    wo = nc.dram_tensor("wo", [1024, 1024], F32, kind="ExternalInput").ap()
    lng = nc.dram_tensor("lng", [128, 1024], F32, kind="ExternalInput").ap()
    lnb = nc.dram_tensor("lnb", [128, 1024], F32, kind="ExternalInput").ap()
    x1 = nc.dram_tensor("x1", [T, 1024], F32, kind="ExternalOutput").ap()
    with ExitStack() as st:
        P = Prog(nc)
        hT = _sb(st, nc, "hT", [128, 8, TH], BF16); hT_b = [Buf() for _ in range(8)]
        brb = _sb(st, nc, "brb", [128, 16, TH], BF16); br_b = Buf()
        mp = _sb(st, nc, "mp", [128, 8, TH], BF16); mp_b = [Buf() for _ in range(8)]
        wbrb = _sb(st, nc, "wbrb", [128, 16, 1024], BF16); wbr_b = Buf()
        wob = _sb(st, nc, "wob", [128, 8, 1024], BF16); wo_b = Buf()
        modt = _sb(st, nc, "modt", [128, 16], F32); mod_b = Buf()
        sc1p = _sb(st, nc, "sc1p", [128, 8], F32); sc1p_b = Buf()
        bgt = _sb(st, nc, "bgt", [128, 32], F32); bg_b = Buf()
        g1t = _sb(st, nc, "g1t", [128, 1024], F32); g1_b = Buf()
        lngt = _sb(st, nc, "lngt", [128, 1024], F32); lnbt = _sb(st, nc, "lnbt", [128, 1024], F32); ln_b = Buf()
        xs = Ring([(_sb(st, nc, "xs%d" % i, [128, TH], F32), Buf()) for i in range(2)])
        wr = Ring([(_sb(st, nc, "wb%d" % i, [128, 8, 128], BF16), Buf()) for i in range(8)])
        gtr = Ring([(_sb(st, nc, "gt%d" % i, [128, 512], F32), Buf()) for i in range(3)])
        tmr = Ring([(_sb(st, nc, "tm%d" % i, [128, 512], F32), Buf()) for i in range(2)])
        acr = Ring([(_sb(st, nc, "ac%d" % i, [128, 512], F32), Buf()) for i in range(2)])
        xtr = Ring([(_sb(st, nc, "xt%d" % i, [128, 1024], F32), Buf()) for i in range(2)])
        yr = Ring([(_sb(st, nc, "y%d" % i, [128, 1024], F32), Buf()) for i in range(2)])
        tr = Ring([(_sb(st, nc, "t%d" % i, [128, 1024], F32), Buf()) for i in range(2)])
        outr = Ring([(_sb(st, nc, "o%d" % i, [128, 1024], F32), Buf()) for i in range(2)])
        smr = Ring([((_sb(st, nc, "stt%d" % i, [128, 2, 6], F32), _sb(st, nc, "mv%d" % i, [128, 8], F32)), Buf()) for i in range(2)])
        banks = Ring(_psum_banks(st, nc))
        x1_b = Buf()
        P.dma("sp", modt[:], mod[:, :], writes=[mod_b])
        P.dma("sp", bgt[:], bg[:, :], writes=[bg_b])
        P.dma("sp", g1t[:], g1[:, :], writes=[g1_b])
        P.dma("sp", lngt[:], lng[:, :], writes=[ln_b])
        P.dma("sp", lnbt[:], lnb[:, :], writes=[ln_b])
        P.op("dve", lambda e: e.tensor_scalar(out=sc1p[:], in0=modt[:, 0:8], scalar1=1.0, scalar2=None, op0=ALU.add), reads=[mod_b], writes=[sc1p_b])
        P.dma("pool", wbrb[:], wbr.rearrange("(c p) d -> p c d", p=128), writes=[wbr_b])
        P.dma("pool", wob[:], wo.rearrange("(c p) d -> p c d", p=128), writes=[wo_b])
        for half in range(2):
            tsl = slice(half * TH, (half + 1) * TH)
            for k in range(8):
                xa, xb_ = xs.next()
                P.dma("sp", xa[:], xT[k * 128:(k + 1) * 128, tsl], writes=[xb_])
                P.op("act", lambda e, xa=xa, k=k: e.activation(out=hT[:, k, :], in_=xa[:], func=AF.Identity,
                                                              scale=sc1p[:, k:k + 1], bias=modt[:, 8 + k:9 + k]),
                     reads=[xb_, sc1p_b, mod_b], writes=[hT_b[k]])
            P.dma("pool", brb[:], brT[:, tsl].rearrange("(c p) t -> p c t", p=128), writes=[br_b])
            for dc in range(8):
                ws = []
                for n in range(4):
                    wa, wb_ = wr.next()
                    c0 = n * 1024 + dc * 128
                    P.dma("pool", wa[:], wg[:, c0:c0 + 128].rearrange("(k p) c -> p k c", p=128), writes=[wb_])
                    ws.append((wa, wb_))
                for tb in range(TH // 512):
                    bsl = slice(tb * 512, (tb + 1) * 512)
                    ac, acb = acr.next()
                    for n in range(4):
                        wa, wb_ = ws[n]
                        ps, pb = banks.next()
                        for k in range(8):
                            P.op("pe", lambda e, ps=ps, wa=wa, k=k, bsl=bsl: e.matmul(ps[:, :], wa[:, k, :], hT[:, k, bsl], start=(k == 0), stop=(k == 7)),
                                 reads=[wb_, hT_b[k]], writes=[pb])
                        gt, gtb = gtr.next()
                        P.op("act", lambda e, gt=gt, ps=ps, n=n, dc=dc: e.activation(out=gt[:], in_=ps[:, :], func=AF.Sigmoid, bias=bgt[:, n * 8 + dc:n * 8 + dc + 1]),
                             reads=[pb, bg_b], writes=[gtb])
                        ps2, pb2 = banks.next()
                        for ec in range(4):
                            P.op("pe", lambda e, ps2=ps2, n=n, ec=ec, dc=dc, bsl=bsl: e.matmul(
                                ps2[:, :], wbrb[:, n * 4 + ec, dc * 128:(dc + 1) * 128], brb[:, n * 4 + ec, bsl], start=(ec == 0), stop=(ec == 3)),
                                reads=[wbr_b, br_b], writes=[pb2])
                        if n == 0:
                            P.op("dve", lambda e, ac=ac, ps2=ps2, gt=gt: e.tensor_tensor(out=ac[:], in0=ps2[:, :], in1=gt[:], op=ALU.mult),
                                 reads=[pb2, gtb], writes=[acb])
                        else:
                            tm, tmb = tmr.next()
                            P.op("dve", lambda e, tm=tm, ps2=ps2, gt=gt: e.tensor_tensor(out=tm[:], in0=ps2[:, :], in1=gt[:], op=ALU.mult),
                                 reads=[pb2, gtb], writes=[tmb])
                            P.op("pool", lambda e, ac=ac, tm=tm: e.tensor_tensor(out=ac[:], in0=ac[:], in1=tm[:], op=ALU.add), reads=[tmb, acb], writes=[acb])
                    P.op("act", lambda e, ac=ac, dc=dc, bsl=bsl: e.copy(out=mp[:, dc, bsl], in_=ac[:]), reads=[acb], writes=[mp_b[dc]])
            for i in range(TH // 128):
                isl = slice(i * 128, (i + 1) * 128)
                row0 = half * TH + i * 128
                xt, xtb = xtr.next(); y, yb = yr.next(); tmp, tmpb = tr.next(); o, ob_ = outr.next(); (stt, mv), smb = smr.next()
                P.dma("sp", xt[:], x[row0:row0 + 128, :], writes=[xtb])
                for hf in range(2):
                    csl = slice(hf * 512, (hf + 1) * 512)
                    ps, pb = banks.next()
                    for k in range(8):
                        P.op("pe", lambda e, ps=ps, k=k, isl=isl, csl=csl: e.matmul(ps[:, :], mp[:, k, isl], wob[:, k, csl], start=(k == 0), stop=(k == 7)),
                             reads=[mp_b[k], wo_b], writes=[pb])
                    P.op("dve", lambda e, tmp=tmp, ps=ps, csl=csl: e.tensor_tensor(out=tmp[:, csl], in0=ps[:, :], in1=g1t[:, csl], op=ALU.mult),
                         reads=[pb, g1_b], writes=[tmpb])
                P.op("dve", lambda e, y=y, xt=xt, tmp=tmp: e.scalar_tensor_tensor(out=y[:], in0=xt[:], scalar=ALPHA, in1=tmp[:], op0=ALU.mult, op1=ALU.add),
                     reads=[xtb, tmpb], writes=[yb])
                _layernorm_tile(P, y, yb, o, ob_, lngt, lnbt, ln_b, stt, mv, smb, tmp, tmpb)
                P.dma("sp", x1[row0:row0 + 128, :], o[:], reads=[ob_], writes=[x1_b])
        P.finish([x1_b])
        P.emit(st)
    return nc


GELU_C = 2.0 * math.sqrt(2.0 / math.pi)


def build_p(ntiles=TPC // 128, T=TPC):
    from contextlib import ExitStack
    nc = bass.Bass("TRN2", target_bir_lowering=False)
    x1 = nc.dram_tensor("x1", [T, 1024], F32, kind="ExternalInput").ap()
    x1T = nc.dram_tensor("x1T", [1024, T], F32, kind="ExternalInput").ap()
    mod = nc.dram_tensor("mod", [128, 16], F32, kind="ExternalInput").ap()
    modb = nc.dram_tensor("modb", [128, 3, 1024], F32, kind="ExternalInput").ap()
    wq = nc.dram_tensor("wq", [1024, 2048], F32, kind="ExternalInput").ap()
    skT = nc.dram_tensor("skT", [128, 16, 128], F32, kind="ExternalInput").ap()
    u = nc.dram_tensor("u", [16384, 1024], F32, kind="ExternalInput").ap()
    v = nc.dram_tensor("v", [16384, 1024], F32, kind="ExternalInput").ap()
    lng = nc.dram_tensor("lng", [128, 1024], F32, kind="ExternalInput").ap()
    lnb = nc.dram_tensor("lnb", [128, 1024], F32, kind="ExternalInput").ap()
    iot = nc.dram_tensor("iot", [128, 256], F32, kind="ExternalInput").ap()
    x2 = nc.dram_tensor("x2", [T, 1024], F32, kind="ExternalOutput").ap()
    with ExitStack() as st:
        P = Prog(nc)
        wqb = _sb(st, nc, "wqb", [128, 8, 2048], BF16); wq_b = Buf()
        skb = _sb(st, nc, "skb", [128, 16, 128], BF16); sk_b = Buf()
        modt = _sb(st, nc, "modt", [128, 16], F32); mod_b = Buf()
        sc2p = _sb(st, nc, "sc2p", [128, 8], F32); sc2p_b = Buf()
        mbt = _sb(st, nc, "mbt", [128, 3, 1024], F32); mb_b = Buf()
        lngt = _sb(st, nc, "lngt", [128, 1024], F32); lnbt = _sb(st, nc, "lnbt", [128, 1024], F32); ln_b = Buf()
        xTt = _sb(st, nc, "xTt", [128, 8, 128], F32); xTt_b = Buf()
        h2T = _sb(st, nc, "h2T", [128, 8, 128], BF16); h2T_b = Buf()
        xt = _sb(st, nc, "xt", [128, 1024], F32); xt_b = Buf()
        h2 = _sb(st, nc, "h2", [128, 1024], F32); h2_b = Buf()
        qTb = _sb(st, nc, "qTb", [128, 16, 128], BF16); qT_b = Buf()
        sc = _sb(st, nc, "sc", [128, 16, 128], F32); sc_b = Buf()
        sc2 = _sb(st, nc, "sc2", [128, 16, 128], F32); sc2_b = Buf()
        sv = _sb(st, nc, "sv", [128, 16, 16], F32); sv_b = Buf()
        si = _sb(st, nc, "si", [128, 16, 16], U32); si_b = Buf()
        sif = _sb(st, nc, "sif", [128, 16, 16], F32); sif_b = Buf()
        si1x = _sb(st, nc, "si1x", [128, 8, 16], F32); si1x_b = Buf()
        cand = _sb(st, nc, "cand", [128, 8, 256], F32); cand_b = Buf()
        cand2 = _sb(st, nc, "cand2", [128, 8, 256], F32); cand2_b = Buf()
        candi = _sb(st, nc, "candi", [128, 8, 256], F32); candi_b = Buf()
        tv = _sb(st, nc, "tv", [128, 8, 16], F32); tv_b = Buf()
        tj = _sb(st, nc, "tj", [128, 8, 16], U32); tj_b = Buf()
        tjf = _sb(st, nc, "tjf", [128, 8, 16], F32); tjf_b = Buf()
        iott = _sb(st, nc, "iott", [128, 256], F32); iot_b = Buf()
        ev = _sb(st, nc, "ev", [128, 8, 16], F32); ev_b = Buf()
        gg = _sb(st, nc, "gg", [128, 128], F32); gg_b = Buf()
        sm = _sb(st, nc, "sm", [128, 32], F32); sm_b = Buf()
        idxf = _sb(st, nc, "idxf", [128, 128], F32); idxf_b = Buf()
        idxu = _sb(st, nc, "idxu", [128, 128], U32); idxu_b = Buf()
        hu = _sb(st, nc, "hu", [128, 128], F32); hu_b = Buf()
        tg = _sb(st, nc, "tg", [128, 128], F32); tg_b = Buf()
        ww = _sb(st, nc, "ww", [128, 128], F32); ww_b = Buf()
        junk = _sb(st, nc, "junk", [128, 1024], F32); junk_b = Buf()
        ubr = Ring([(_sb(st, nc, "ub%d" % i, [128, 1024], F32), Buf()) for i in range(6)])
        acc = _sb(st, nc, "acc", [128, 1024], F32); acc_b = Buf()
        y = _sb(st, nc, "y", [128, 1024], F32); y_b = Buf()
        tmp = _sb(st, nc, "tmp", [128, 1024], F32); tmp_b = Buf()
        outr = Ring([(_sb(st, nc, "o%d" % i, [128, 1024], F32), Buf()) for i in range(2)])
        stt = _sb(st, nc, "stt", [128, 2, 6], F32); mv = _sb(st, nc, "mv", [128, 8], F32); smb2 = Buf()
        banks = Ring(_psum_banks(st, nc))
        x2_b = Buf()
        P.dma("sp", modt[:], mod[:, :], writes=[mod_b])
        P.dma("sp", mbt[:], modb[:, :, :], writes=[mb_b])
        P.dma("sp", lngt[:], lng[:, :], writes=[ln_b])
        P.dma("sp", lnbt[:], lnb[:, :], writes=[ln_b])
        P.dma("sp", iott[:], iot[:, :], writes=[iot_b])
        P.dma("pool", wqb[:], wq.rearrange("(k p) c -> p k c", p=128), writes=[wq_b])
        P.dma("pool", skb[:], skT[:, :, :], writes=[sk_b])
        P.op("dve", lambda e: e.tensor_scalar(out=sc2p[:], in0=modt[:, 0:8], scalar1=1.0, scalar2=None, op0=ALU.add), reads=[mod_b], writes=[sc2p_b])
        P.op("dve", lambda e: e.tensor_scalar(out=mbt[:, 0, :], in0=mbt[:, 0, :], scalar1=1.0, scalar2=None, op0=ALU.add), reads=[mb_b], writes=[mb_b])

        def gather(table, slot):
            ub, ubb = ubr.next()
            P._waits("pool", P._deps([idxu_b], [ubb]))
            i = P.dma_rr["pool"]; P.dma_rr["pool"] = (i + 1) % P.ndsem
            key = "d_pool%d" % i
            P.cnt[key] = P.cnt.get(key, 0) + 16
            P.ops["pool"].append(("op", lambda e, ub=ub, slot=slot: e.indirect_dma_start(
                out=ub[:, :], out_offset=None, in_=table[:, :],
                in_offset=bass.IndirectOffsetOnAxis(ap=idxu[:, slot:slot + 1], axis=0)), key, 16))
            P._mark((key, P.cnt[key]), [idxu_b], [ubb])
            return ub, ubb

        for i in range(ntiles):
            tsl = slice(i * 128, (i + 1) * 128)
            P.dma("sp", xTt[:], x1T.rearrange("(k p) t -> p k t", p=128)[:, :, tsl], writes=[xTt_b])
            P.dma("sp", xt[:], x1[tsl, :], writes=[xt_b])
            for k in range(8):
                P.op("act", lambda e, k=k: e.activation(out=h2T[:, k, :], in_=xTt[:, k, :], func=AF.Identity, scale=sc2p[:, k:k + 1], bias=modt[:, 8 + k:9 + k]),
                     reads=[xTt_b, sc2p_b, mod_b], writes=[h2T_b])
            P.op("dve", lambda e: e.tensor_tensor(out=h2[:], in0=xt[:], in1=mbt[:, 0, :], op=ALU.mult), reads=[xt_b, mb_b], writes=[h2_b])
            P.op("pool", lambda e: e.tensor_tensor(out=h2[:], in0=h2[:], in1=mbt[:, 1, :], op=ALU.add), reads=[h2_b, mb_b], writes=[h2_b])
            for g4 in range(4):
                ps, pb = banks.next()
                for j in range(4):
                    hp = g4 * 4 + j
                    for k in range(8):
                        P.op("pe", lambda e, ps=ps, j=j, hp=hp, k=k: e.matmul(ps[:, j * 128:(j + 1) * 128], wqb[:, k, hp * 128:(hp + 1) * 128], h2T[:, k, :],
                                                                     start=(k == 0), stop=(k == 7), skip_group_check=True), reads=[wq_b, h2T_b], writes=[pb])
                P.op("act", lambda e, ps=ps, g4=g4: e.copy(out=qTb[:, g4 * 4:(g4 + 1) * 4, :], in_=ps[:, :].rearrange("p (a b) -> p a b", a=4)), reads=[pb], writes=[qT_b])
            for g4 in range(4):
                ps, pb = banks.next()
                for j in range(4):
                    hp = g4 * 4 + j
                    P.op("pe", lambda e, ps=ps, j=j, hp=hp: e.matmul(ps[:, j * 128:(j + 1) * 128], qTb[:, hp, :], skb[:, hp, :], start=True, stop=True, skip_group_check=True),
                         reads=[qT_b, sk_b], writes=[pb])
                P.op("act", lambda e, ps=ps, g4=g4: e.copy(out=sc[:, g4 * 4:(g4 + 1) * 4, :], in_=ps[:, :].rearrange("p (a b) -> p a b", a=4)), reads=[pb], writes=[sc_b])
            for hp in range(16):
                P.op("dve", lambda e, hp=hp: e.max(out=sv[:, hp, 0:8], in_=sc[:, hp, :]), reads=[sc_b], writes=[sv_b])
                P.op("dve", lambda e, hp=hp: e.match_replace(out=sc2[:, hp, :], in_to_replace=sv[:, hp, 0:8], in_values=sc[:, hp, :], imm_value=-1e30),
                     reads=[sc_b, sv_b], writes=[sc2_b])
                P.op("dve", lambda e, hp=hp: e.max(out=sv[:, hp, 8:16], in_=sc2[:, hp, :]), reads=[sc2_b], writes=[sv_b])
                P.op("dve", lambda e, hp=hp: e.max_index(out=si[:, hp, 0:8], in_max=sv[:, hp, 0:8], in_values=sc[:, hp, :]), reads=[sc_b, sv_b], writes=[si_b])
                P.op("dve", lambda e, hp=hp: e.max_index(out=si[:, hp, 8:16], in_max=sv[:, hp, 8:16], in_values=sc2[:, hp, :]), reads=[sc2_b, sv_b], writes=[si_b])
            P.op("dve", lambda e: e.tensor_copy(out=sif[:], in_=si[:]), reads=[si_b], writes=[sif_b])
            P.op("dve", lambda e: e.tensor_scalar(out=si1x[:], in0=sif[:].rearrange("p (h t) k -> p h t k", t=2)[:, :, 0, :], scalar1=128.0, scalar2=None, op0=ALU.mult),
                 reads=[sif_b], writes=[si1x_b])
            for h in range(8):
                for a in range(16):
                    P.op("dve", lambda e, h=h, a=a: e.tensor_scalar(out=cand[:, h, a * 16:(a + 1) * 16], in0=sv[:, 2 * h + 1, :], scalar1=sv[:, 2 * h, a:a + 1], scalar2=None, op0=ALU.add),
                         reads=[sv_b], writes=[cand_b])
                    P.op("pool", lambda e, h=h, a=a: e.tensor_scalar(out=candi[:, h, a * 16:(a + 1) * 16], in0=sif[:, 2 * h + 1, :], scalar1=si1x[:, h, a:a + 1], scalar2=None, op0=ALU.add),
                         reads=[sif_b, si1x_b], writes=[candi_b])
            for h in range(8):
                P.op("dve", lambda e, h=h: e.max(out=tv[:, h, 0:8], in_=cand[:, h, :]), reads=[cand_b], writes=[tv_b])
                P.op("dve", lambda e, h=h: e.match_replace(out=cand2[:, h, :], in_to_replace=tv[:, h, 0:8], in_values=cand[:, h, :], imm_value=-1e30),
                     reads=[cand_b, tv_b], writes=[cand2_b])
                P.op("dve", lambda e, h=h: e.max(out=tv[:, h, 8:16], in_=cand2[:, h, :]), reads=[cand2_b], writes=[tv_b])
                P.op("dve", lambda e, h=h: e.max_index(out=tj[:, h, 0:8], in_max=tv[:, h, 0:8], in_values=cand[:, h, :]), reads=[cand_b, tv_b], writes=[tj_b])
                P.op("dve", lambda e, h=h: e.max_index(out=tj[:, h, 8:16], in_max=tv[:, h, 8:16], in_values=cand2[:, h, :]), reads=[cand2_b, tv_b], writes=[tj_b])
            P.op("dve", lambda e: e.tensor_copy(out=tjf[:], in_=tj[:]), reads=[tj_b], writes=[tjf_b])
            for h in range(8):
                for k in range(16):
                    P.op("dve", lambda e, h=h, k=k: e.scalar_tensor_tensor(out=junk[:, 0:256], in0=iott[:, :], scalar=tjf[:, h, k:k + 1], in1=candi[:, h, :],
                                                                       op0=ALU.is_equal, op1=ALU.mult, accum_out=idxf[:, h * 16 + k:h * 16 + k + 1]),
                         reads=[iot_b, tjf_b, candi_b], writes=[junk_b, idxf_b])
            P.op("dve", lambda e: e.tensor_copy(out=idxu[:], in_=idxf[:]), reads=[idxf_b], writes=[idxu_b])
            P.op("dve", lambda e: e.tensor_scalar(out=sm[:, 0:8], in0=tv[:, :, 0], scalar1=-1.0, scalar2=None, op0=ALU.mult), reads=[tv_b], writes=[sm_b])
            for h in range(8):
                P.op("act", lambda e, h=h: e.activation(out=ev[:, h, :], in_=tv[:, h, :], func=AF.Exp, bias=sm[:, h:h + 1]), reads=[tv_b, sm_b], writes=[ev_b])
            P.op("dve", lambda e: e.tensor_reduce(out=sm[:, 8:16], in_=ev[:], axis=AX.X, op=ALU.add), reads=[ev_b], writes=[sm_b])
            P.op("dve", lambda e: e.reciprocal(out=sm[:, 16:24], in_=sm[:, 8:16]), reads=[sm_b], writes=[sm_b])
            for h in range(8):
                P.op("dve", lambda e, h=h: e.tensor_scalar(out=gg[:, h * 16:(h + 1) * 16], in0=ev[:, h, :], scalar1=sm[:, 16 + h:17 + h], scalar2=None, op0=ALU.mult),
                     reads=[ev_b, sm_b], writes=[gg_b])
            for slot in range(128):
                ub, ubb = gather(u, slot)
                P.op("dve", lambda e, ub=ub, slot=slot: e.scalar_tensor_tensor(out=junk[:], in0=ub[:], scalar=1.0, in1=h2[:], op0=ALU.mult, op1=ALU.mult,
                                                                            accum_out=hu[:, slot:slot + 1]), reads=[ubb, h2_b], writes=[junk_b, hu_b])
            P.op("dve", lambda e: e.tensor_tensor(out=tg[:], in0=hu[:], in1=hu[:], op=ALU.mult), reads=[hu_b], writes=[tg_b])
            P.op("dve", lambda e: e.tensor_scalar(out=tg[:], in0=tg[:], scalar1=0.044715, scalar2=1.0, op0=ALU.mult, op1=ALU.add), reads=[tg_b], writes=[tg_b])
            P.op("dve", lambda e: e.tensor_tensor(out=tg[:], in0=tg[:], in1=hu[:], op=ALU.mult), reads=[tg_b, hu_b], writes=[tg_b])
            P.op("act", lambda e: e.activation(out=tg[:], in_=tg[:], func=AF.Sigmoid, scale=GELU_C), reads=[tg_b], writes=[tg_b])
            P.op("dve", lambda e: e.tensor_tensor(out=ww[:], in0=hu[:], in1=gg[:], op=ALU.mult), reads=[hu_b, gg_b], writes=[ww_b])
            P.op("dve", lambda e: e.tensor_tensor(out=ww[:], in0=ww[:], in1=tg[:], op=ALU.mult), reads=[ww_b, tg_b], writes=[ww_b])
            for slot in range(128):
                vb, vbb = gather(v, slot)
                if slot == 0:
                    P.op("dve", lambda e, vb=vb: e.tensor_scalar(out=acc[:], in0=vb[:], scalar1=ww[:, 0:1], scalar2=None, op0=ALU.mult), reads=[vbb, ww_b], writes=[acc_b])
                else:
                    P.op("dve", lambda e, vb=vb, slot=slot: e.scalar_tensor_tensor(out=acc[:], in0=vb[:], scalar=ww[:, slot:slot + 1], in1=acc[:], op0=ALU.mult, op1=ALU.add),
                         reads=[vbb, ww_b, acc_b], writes=[acc_b])
            P.op("dve", lambda e: e.tensor_tensor(out=tmp[:], in0=acc[:], in1=mbt[:, 2, :], op=ALU.mult), reads=[acc_b, mb_b], writes=[tmp_b])
            P.op("dve", lambda e: e.scalar_tensor_tensor(out=y[:], in0=xt[:], scalar=ALPHA, in1=tmp[:], op0=ALU.mult, op1=ALU.add), reads=[xt_b, tmp_b], writes=[y_b])
            o, ob_ = outr.next()
            _layernorm_tile(P, y, y_b, o, ob_, lngt, lnbt, ln_b, stt, mv, smb2, tmp, tmp_b)
            P.dma("sp", x2[tsl, :], o[:], reads=[ob_], writes=[x2_b])
        P.finish([x2_b])
        P.emit(st)
    return nc


def build_ada():
    from contextlib import ExitStack
    nc = bass.Bass("TRN2", target_bir_lowering=False)
    wA = nc.dram_tensor("wA", [1024, 3072], F32, kind="ExternalInput").ap()
    cT = nc.dram_tensor("cT", [128, 8, 2], F32, kind="ExternalInput").ap()
    bA = nc.dram_tensor("bA", [128, 24], F32, kind="ExternalInput").ap()
    mo = nc.dram_tensor("mo", [128, 24, 2], F32, kind="ExternalOutput").ap()
    with ExitStack() as st:
        P = Prog(nc)
        w = _sb(st, nc, "w", [128, 8, 3072], F32); w_b = Buf()
        ct = _sb(st, nc, "ct", [128, 8, 2], F32); c_b = Buf()
        ca = _sb(st, nc, "ca", [128, 8, 2], F32); ca_b = Buf()
        bt = _sb(st, nc, "bt", [128, 24], F32); b_b = Buf()
        res = _sb(st, nc, "res", [128, 24, 2], F32); r_b = Buf()
        banks = _psum_banks(st, nc, 1)
        ps, pb = banks[0]
        mo_b = Buf()
        for k in range(8):
            P.dma("sp", w[:, k, :], wA[k * 128:(k + 1) * 128, :], writes=[w_b])
        P.dma("sp", ct[:], cT[:, :, :], writes=[c_b])
        P.dma("sp", bt[:], bA[:, :], writes=[b_b])
        P.op("act", lambda e: e.activation(out=ca[:], in_=ct[:], func=AF.Silu), reads=[c_b], writes=[ca_b])
        for j in range(24):
            for k in range(8):
                P.op("pe", lambda e, j=j, k=k: e.matmul(ps[:, 2 * j:2 * j + 2], w[:, k, j * 128:(j + 1) * 128], ca[:, k, :],
                                                       start=(k == 0), stop=(k == 7), skip_group_check=True), reads=[w_b, ca_b], writes=[pb])
        for b in range(2):
            P.op("dve", lambda e, b=b: e.tensor_tensor(out=res[:, :, b], in0=ps[:, 0:48].rearrange("p (j b) -> p j b", b=2)[:, :, b], in1=bt[:, :], op=ALU.add),
                 reads=[pb, b_b], writes=[r_b])
        P.dma("sp", mo[:, :, :], res[:], reads=[r_b], writes=[mo_b])
        P.finish([mo_b])
        P.emit(st)
    return nc


_PROGS = {}
_DBG = None


def _prog(name, fn):
    if name not in _PROGS:
        _PROGS[name] = fn()
    return _PROGS[name]


def _run(name, fn, in_maps):
    nc = _prog(name, fn)
    n = len(in_maps)
    in_maps = [{k: np.ascontiguousarray(v, dtype=np.float32) for k, v in m.items()} for m in in_maps]
    res = run_bass_kernel_spmd(nc, in_maps, core_ids=list(range(n)))
    return res.results


def _rep(vec):
    return np.ascontiguousarray(np.tile(np.asarray(vec, np.float32)[None], (128, 1)))


def _chunk128(vec):
    return np.ascontiguousarray(np.asarray(vec, np.float32).reshape(-1, 128).T)


def _swap_heads(w):
    k, n = w.shape
    w4 = w.reshape(k, n // 64, 2, 32)
    return np.ascontiguousarray(w4[:, :, ::-1, :]).reshape(k, n)


def _rope_tables(s0, n):
    inv = (1.0 / (10000.0 ** (np.arange(0, 64, 2, dtype=np.float32) / 64.0))).astype(np.float32)
    ang = (np.arange(s0, s0 + n, dtype=np.float32)[:, None] * inv[None, :]).astype(np.float32)
    cos = np.cos(ang).astype(np.float32).T
    sin = np.sin(ang).astype(np.float32).T
    out = np.zeros((128, 2, n), np.float32)
    for p in range(128):
        f = p % 32
        out[p, 0] = cos[f]
        out[p, 1] = -sin[f] if (p % 64) < 32 else sin[f]
    return out


def kernel(x, c, w_ada, b_ada, w_in, b_gate, a_sink, b_rpb, c_lambda, c_norm_g,
           d_conv_w, d_conv_b, d_wa, d_ba, d_wx, d_bx, d_lam, w_branch, w_out,
           ln_g, ln_b, p_wq, p_subkeys, p_u, p_v):
    f32 = np.float32
    x = np.asarray(x, f32); c = np.asarray(c, f32)
    B, S, D = BATCH, SEQ, D_MODEL
    cT = np.ascontiguousarray(c.reshape(2, 8, 128).transpose(2, 1, 0))
    maps = []
    for core in range(8):
        l, half = core // 2, core % 2
        maps.append(dict(wA=np.asarray(w_ada[l])[:, half * 3072:(half + 1) * 3072], cT=cT,
                         bA=_chunk128(np.asarray(b_ada[l])[half * 3072:(half + 1) * 3072])))
    r = _run("ada", build_ada, maps)
    mod = np.zeros((DEPTH, 2, 6144), f32)
    for core in range(8):
        l, half = core // 2, core % 2
        mo = r[core]["mo"]
        mod[l, :, half * 3072:(half + 1) * 3072] = mo.transpose(2, 1, 0).reshape(2, 3072)
    if _DBG is not None:
        _DBG["mod"] = mod
    eye = np.eye(128, dtype=f32)
    amask = a_mask_np()
    cs_tabs = [_rope_tables(j * TPC, TPC) for j in range(4)]
    xc = x.copy()
    l0 = 0
    if _DBG is not None and "start" in _DBG:
        l0, xc = _DBG["start"]
    for l in range(l0, DEPTH):
        wl = np.asarray(w_in[l], f32)
        shift1, scale1, gate1, shift2, scale2, gate2 = [mod[l][:, i * 1024:(i + 1) * 1024] for i in range(6)]
        aq, ak, av = wl[:, 0:512], wl[:, 512:640], wl[:, 640:768]
        bq, bk, bv = wl[:, 768:1280], wl[:, 1280:1792], wl[:, 1792:2304]
        cq, ck, cv = wl[:, 2304:2816], wl[:, 2816:3328], wl[:, 3328:3840]
        dx, dg, gl = wl[:, 3840:4352], wl[:, 4352:4864], wl[:, 4864:8960]
        pairs = []
        for (wm, n) in ((aq, 4), (ak, 1), (cq, 4), (ck, 4)):
            ws = _swap_heads(wm)
            for i in range(n):
                pairs.append(wm[:, i * 128:(i + 1) * 128]); pairs.append(ws[:, i * 128:(i + 1) * 128])
        plains = []
        for wm in (bq, bk, dx, dg):
            for i in range(4):
                plains.append(wm[:, i * 128:(i + 1) * 128])
        wf = np.ascontiguousarray(np.concatenate(pairs + plains, axis=1))
        wt = np.ascontiguousarray(np.concatenate([av, bv, cv], axis=1))
        mod16 = [np.concatenate([_chunk128(scale1[b]), _chunk128(shift1[b])], axis=1) for b in range(2)]
        maps = []
        for core in range(8):
            b, j = core // 4, core % 4
            maps.append(dict(xT=xc[b, j * TPC:(j + 1) * TPC, :].T, mod=mod16[b], wf=wf, wt=wt, cs=cs_tabs[j]))
        r = _run("l1", build_l1, maps)
        F = [np.concatenate([r[b * 4 + j]["oF"] for j in range(4)], axis=2) for b in range(2)]
        TM = [np.concatenate([r[b * 4 + j]["oT"] for j in range(4)], axis=0) for b in range(2)]
        BR = [np.zeros((S, 2048), f32) for _ in range(2)]
        if _DBG is not None:
            _DBG["F%d" % l] = F; _DBG["TM%d" % l] = TM; _DBG["BR%d" % l] = BR
        maps = []
        for core in range(8):
            b, j = core // 4, core % 4
            kv = j // 2
            k1 = F[b][4][kv * 64:(kv + 1) * 64]
            maps.append(dict(qT=F[b][j], kT=np.concatenate([k1, k1], 0), v=TM[b][:, kv * 64:(kv + 1) * 64], msk=amask, idn=eye,
                             snk=_rep(np.asarray(a_sink[l], f32)[2 * j:2 * j + 2])))
        r = _run("a", build_a, maps)
        for core in range(8):
            b, j = core // 4, core % 4
            BR[b][:, j * 128:(j + 1) * 128] = r[core]["oa"]
        maps = []
        for core in range(8):
            b, j = core // 4, core % 4
            vv = TM[b][:, 128 + j * 128:128 + (j + 1) * 128]
            vsh = np.zeros_like(vv); vsh[:-64] = vv[64:]
            maps.append(dict(qT=F[b][13 + j], kT=F[b][17 + j], v=vv, vsh=vsh, bias=b_bias_np(np.asarray(b_rpb[l], f32)[2 * j:2 * j + 2]), idn=8.0 * eye))
        r = _run("b", build_b, maps)
        for core in range(8):
            b, j = core // 4, core % 4
            BR[b][:, 512 + j * 128:512 + (j + 1) * 128] = r[core]["ob"]
        lam_init = 0.8 - 0.6 * math.exp(-0.3 * l)
        maps = []
        for core in range(8):
            b, j = core // 4, core % 4
            maps.append(dict(qT=F[b][5 + j], kT=F[b][9 + j], v=TM[b][:, 640 + j * 128:640 + (j + 1) * 128],
                             lamb=np.tile(np.asarray(c_lambda[l], f32)[None], (128, 1, 1)),
                             cst=_rep(np.array([lam_init, 1.0 - lam_init], f32)), ng=_rep(np.asarray(c_norm_g[l], f32)[j * 128:(j + 1) * 128])))
        r = _run("c", build_c, maps)
        for core in range(8):
            b, j = core // 4, core % 4
            BR[b][:, 1024 + j * 128:1024 + (j + 1) * 128] = r[core]["oc"]
        maps = []
        for core in range(8):
            b, j = core // 4, core % 4
            wbd = np.zeros((128, 4, 128), f32)
            for d in range(2):
                for g in range(2):
                    wbd[g * 64:(g + 1) * 64, 2 * d, g * 64:(g + 1) * 64] = np.asarray(d_wa[l], f32)[d, 2 * j + g]
                    wbd[g * 64:(g + 1) * 64, 2 * d + 1, g * 64:(g + 1) * 64] = np.asarray(d_wx[l], f32)[d, 2 * j + g]
            ch = slice(j * 128, (j + 1) * 128)
            par = np.zeros((128, 12), f32)
            par[:, 0:4] = np.asarray(d_conv_w[l], f32)[:, ch].T
            par[:, 4] = np.asarray(d_conv_b[l], f32)[ch]
            par[:, 5] = np.asarray(d_ba[l], f32)[0, ch]; par[:, 6] = np.asarray(d_bx[l], f32)[0, ch]
            par[:, 7] = np.asarray(d_ba[l], f32)[1, ch]; par[:, 8] = np.asarray(d_bx[l], f32)[1, ch]
            par[:, 9] = np.asarray(d_lam[l], f32)[0, ch]; par[:, 10] = np.asarray(d_lam[l], f32)[1, ch]
            maps.append(dict(dxT=F[b][21 + j], dgT=F[b][25 + j], wbd=wbd, par=par))
        r = _run("d", build_d, maps)
        for core in range(8):
            b, j = core // 4, core % 4
            BR[b][:, 1536 + j * 128:1536 + (j + 1) * 128] = r[core]["odT"].T
        bgc = np.ascontiguousarray(np.asarray(b_gate[l], f32).reshape(4, 8, 128).transpose(2, 0, 1).reshape(128, 32))
        maps = []
        for core in range(8):
            b, j = core // 4, core % 4
            ts = slice(j * TPC, (j + 1) * TPC)
            maps.append(dict(xT=xc[b, ts, :].T, x=xc[b, ts, :], brT=BR[b][ts, :].T, mod=mod16[b], g1=_rep(gate1[b]), wg=gl, bg=bgc,
                             wbr=np.asarray(w_branch[l], f32).reshape(2048, 1024), wo=np.asarray(w_out[l], f32),
                             lng=_rep(np.asarray(ln_g[l], f32)[0]), lnb=_rep(np.asarray(ln_b[l], f32)[0])))
        r = _run("m", build_m, maps)
        x1 = np.stack([np.concatenate([r[b * 4 + j]["x1"] for j in range(4)], axis=0) for b in range(2)])
        if _DBG is not None:
            _DBG["x1_%d" % l] = x1
            if _DBG.get("stop_after_merge") == l:
                return x1
        skT = np.ascontiguousarray(np.asarray(p_subkeys[l], f32).reshape(16, 128, 128).transpose(2, 0, 1))
        maps = []
        for b in range(2):
            maps.append(dict(x1=x1[b], x1T=x1[b].T, mod=np.concatenate([_chunk128(scale2[b]), _chunk128(shift2[b])], axis=1),
                             modb=np.stack([_rep(scale2[b]), _rep(shift2[b]), _rep(gate2[b])], 1), wq=np.asarray(p_wq[l], f32), skT=skT,
                             u=np.asarray(p_u[l], f32), v=np.asarray(p_v[l], f32),
                             lng=_rep(np.asarray(ln_g[l], f32)[1]), lnb=_rep(np.asarray(ln_b[l], f32)[1]),
                             iot=_rep(np.arange(256, dtype=f32))))
        r = _run("p", lambda: build_p(SEQ // 128, SEQ), maps)
        xc = np.stack([r[b]["x2"] for b in range(2)])
        if _DBG is not None:
            _DBG["x2_%d" % l] = xc
            if _DBG.get("stop_after_layer") == l:
                return xc
    return xc.astype(np.float32)
```

```python
import math
import numpy as np
import concourse.bass as bass
import concourse.mybir as mybir
from concourse.bass_utils import run_bass_kernel_spmd

F32 = mybir.dt.float32
BF16 = mybir.dt.bfloat16
I32 = mybir.dt.int32
U32 = mybir.dt.uint32
AF = mybir.ActivationFunctionType
ALU = mybir.AluOpType
AX = mybir.AxisListType

D_MODEL = 1024
BATCH = 2
SEQ = 8192
DEPTH = 4
NCORES = 8
TPC = BATCH * SEQ // NCORES
ALPHA = (2.0 * DEPTH) ** 0.25
LN_EPS = 1e-5
NEG = -30000.0


class Buf:
    __slots__ = ("w", "r", "name")

    def __init__(self, name=""):
        self.w = None
        self.r = {}
        self.name = name


class Prog:
    ENGS = ("pe", "dve", "act", "pool", "sp")

    def __init__(self, nc):
        self.nc = nc
        self.ops = {e: [] for e in self.ENGS}
        self.cnt = {}
        self.waited = {e: {} for e in self.ENGS}
        self.dma_rr = {"sp": 0, "pool": 0, "act": 0}
        self.ndsem = 12

    def _waits(self, eng, deps):
        for (key, val) in deps:
            if key == "c_pe" and eng == "pe":
                continue
            if self.waited[eng].get(key, 0) >= val:
                continue
            self.waited[eng][key] = val
            self.ops[eng].append(("wait", key, val))

    def _deps(self, reads, writes):
        deps = {}
        def add(m):
            if m is None:
                return
            k, v = m
            if deps.get(k, 0) < v:
                deps[k] = v
        for b in reads:
            add(b.w)
        for b in writes:
            add(b.w)
            for k, v in b.r.items():
                add((k, v))
        return list(deps.items())

    def _mark(self, marker, reads, writes):
        k, v = marker
        for b in reads:
            if b.r.get(k, 0) < v:
                b.r[k] = v
        for b in writes:
            b.w = marker
            b.r = {}

    def op(self, eng, fn, reads=(), writes=()):
        self._waits(eng, self._deps(reads, writes))
        key = "c_" + eng
        self.cnt[key] = self.cnt.get(key, 0) + 1
        self.ops[eng].append(("op", fn, key, 1))
        self._mark((key, self.cnt[key]), reads, writes)

    def dma(self, q, out, in_, reads=(), writes=(), **kw):
        self._waits(q, self._deps(reads, writes))
        i = self.dma_rr[q]
        self.dma_rr[q] = (i + 1) % self.ndsem
        key = "d_%s%d" % (q, i)
        self.cnt[key] = self.cnt.get(key, 0) + 16
        self.ops[q].append(("op", lambda e: e.dma_start(out=out, in_=in_, **kw), key, 16))
        self._mark((key, self.cnt[key]), reads, writes)

    def finish(self, bufs):
        self._waits("sp", self._deps(bufs, ()))

    def emit(self, stack):
        nc = self.nc
        sems = {}
        for key in sorted(self.cnt):
            sems[key] = stack.enter_context(nc.semaphore(key))
        block = stack.enter_context(nc.Block())
        def run(eng):
            def body(e):
                for it in self.ops[eng]:
                    if it[0] == "wait":
                        e.wait_ge(sems[it[1]], it[2])
                    else:
                        it[1](e).then_inc(sems[it[2]], it[3])
            return body
        block.tensor(run("pe"))
        block.vector(run("dve"))
        block.scalar(run("act"))
        block.gpsimd(run("pool"))
        block.sync(run("sp"))


class Ring:
    def __init__(self, items):
        self.items = items
        self.i = 0

    def next(self):
        it = self.items[self.i]
        self.i = (self.i + 1) % len(self.items)
        return it


def _sb(stack, nc, name, shape, dt):
    return stack.enter_context(nc.sbuf_tensor(name, shape, dt))


def _psum_banks(stack, nc, n=8):
    return [(stack.enter_context(nc.psum_tensor("psb%d" % i, [128, 512], F32)), Buf("ps%d" % i)) for i in range(n)]


L1_NPAIR = 13
L1_NPLAIN = 16
L1_NF = L1_NPAIR + L1_NPLAIN
L1_WF_CHUNKS = 2 * L1_NPAIR + L1_NPLAIN
L1_TM = 1152


def build_l1():
    from contextlib import ExitStack
    nc = bass.Bass("TRN2", target_bir_lowering=False)
    T = TPC
    xT = nc.dram_tensor("xT", [1024, T], F32, kind="ExternalInput").ap()
    mod = nc.dram_tensor("mod", [128, 16], F32, kind="ExternalInput").ap()
    wf = nc.dram_tensor("wf", [1024, L1_WF_CHUNKS * 128], F32, kind="ExternalInput").ap()
    wt = nc.dram_tensor("wt", [1024, L1_TM], F32, kind="ExternalInput").ap()
    cs = nc.dram_tensor("cs", [128, 2, T], F32, kind="ExternalInput").ap()
    oF = nc.dram_tensor("oF", [L1_NF, 128, T], F32, kind="ExternalOutput").ap()
    oT = nc.dram_tensor("oT", [T, L1_TM], F32, kind="ExternalOutput").ap()
    with ExitStack() as st:
        P = Prog(nc)
        hT = _sb(st, nc, "hT", [128, 8, T], BF16); hT_b = [Buf() for _ in range(8)]
        xs = Ring([(_sb(st, nc, "xs%d" % i, [128, T], F32), Buf()) for i in range(2)])
        modt = _sb(st, nc, "modt", [128, 16], F32); mod_b = Buf()
        sc1p = _sb(st, nc, "sc1p", [128, 8], F32); sc1p_b = Buf()
        cst = _sb(st, nc, "cst", [128, 2, T], F32); cs_b = Buf()
        wtb = _sb(st, nc, "wtb", [128, 8, L1_TM], BF16); wt_b = Buf()
        wr = Ring([(_sb(st, nc, "wb%d" % i, [128, 8, 128], BF16), Buf()) for i in range(6)])
        stF = Ring([(_sb(st, nc, "stF%d" % i, [128, T], F32), Buf()) for i in range(3)])
        stT = Ring([(_sb(st, nc, "stT%d" % i, [128, L1_TM], F32), Buf()) for i in range(2)])
        t1r = Ring([(_sb(st, nc, "t1_%d" % i, [128, 512], F32), Buf()) for i in range(2)])
        t2r = Ring([(_sb(st, nc, "t2_%d" % i, [128, 512], F32), Buf()) for i in range(2)])
        banks = Ring(_psum_banks(st, nc))
        oF_b = Buf(); oT_b = Buf()

        P.dma("sp", modt[:], mod[:, :], writes=[mod_b])
        P.dma("sp", cst[:], cs[:, :, :], writes=[cs_b])
        P.op("dve", lambda e: e.tensor_scalar(out=sc1p[:], in0=modt[:, 0:8], scalar1=1.0, scalar2=None, op0=ALU.add),
             reads=[mod_b], writes=[sc1p_b])
        P.dma("pool", wtb[:], wt.rearrange("(k p) c -> p k c", p=128), writes=[wt_b])
        for k in range(8):
            xa, xb_ = xs.next()
            P.dma("sp", xa[:], xT[k * 128:(k + 1) * 128, :], writes=[xb_])
            P.op("act", lambda e, xa=xa, k=k: e.activation(out=hT[:, k, :], in_=xa[:], func=AF.Identity,
                                                          scale=sc1p[:, k:k + 1], bias=modt[:, 8 + k:9 + k]),
                 reads=[xb_, sc1p_b, mod_b], writes=[hT_b[k]])

        def load_w(c):
            wa, wb_ = wr.next()
            P.dma("pool", wa[:], wf[:, c * 128:(c + 1) * 128].rearrange("(k p) c -> p k c", p=128), writes=[wb_])
            return wa, wb_

        def mm_feat(wa, wb_, tb):
            ps, pb = banks.next()
            for k in range(8):
                P.op("pe", lambda e, ps=ps, wa=wa, k=k, tb=tb: e.matmul(
                    ps[:, :], wa[:, k, :], hT[:, k, tb * 512:(tb + 1) * 512], start=(k == 0), stop=(k == 7)),
                    reads=[wb_, hT_b[k]], writes=[pb])
            return ps, pb

        nevac = 0
        for u in range(L1_NF):
            sa, sb_ = stF.next()
            if u < L1_NPAIR:
                wA = load_w(2 * u); wB = load_w(2 * u + 1)
                for tb in range(4):
                    pA, pAb = mm_feat(wA[0], wA[1], tb)
                    pB, pBb = mm_feat(wB[0], wB[1], tb)
                    t1, t1b = t1r.next(); t2, t2b = t2r.next()
                    sl = slice(tb * 512, (tb + 1) * 512)
                    P.op("dve", lambda e, t1=t1, pA=pA, sl=sl: e.tensor_tensor(out=t1[:], in0=pA[:, :], in1=cst[:, 0, sl], op=ALU.mult),
                         reads=[pAb, cs_b], writes=[t1b])
                    P.op("dve", lambda e, t2=t2, pB=pB, sl=sl: e.tensor_tensor(out=t2[:], in0=pB[:, :], in1=cst[:, 1, sl], op=ALU.mult),
                         reads=[pBb, cs_b], writes=[t2b])
                    P.op("pool", lambda e, sa=sa, t1=t1, t2=t2, sl=sl: e.tensor_tensor(out=sa[:, sl], in0=t1[:], in1=t2[:], op=ALU.add),
                         reads=[t1b, t2b], writes=[sb_])
            else:
                wA = load_w(2 * L1_NPAIR + (u - L1_NPAIR))
                for tb in range(4):
                    pA, pAb = mm_feat(wA[0], wA[1], tb)
                    sl = slice(tb * 512, (tb + 1) * 512)
                    if nevac % 2 == 0:
                        P.op("act", lambda e, sa=sa, pA=pA, sl=sl: e.copy(out=sa[:, sl], in_=pA[:, :]), reads=[pAb], writes=[sb_])
                    else:
                        P.op("dve", lambda e, sa=sa, pA=pA, sl=sl: e.tensor_copy(out=sa[:, sl], in_=pA[:, :]), reads=[pAb], writes=[sb_])
                    nevac += 1
            P.dma("sp", oF[u, :, :], sa[:], reads=[sb_], writes=[oF_b])

        groups = [(0, 128), (128, 512), (640, 512)]
        for i in range(T // 128):
            sa, sb_ = stT.next()
            for (c0, n) in groups:
                ps, pb = banks.next()
                for k in range(8):
                    P.op("pe", lambda e, ps=ps, k=k, i=i, c0=c0, n=n: e.matmul(
                        ps[:, 0:n], hT[:, k, i * 128:(i + 1) * 128], wtb[:, k, c0:c0 + n], start=(k == 0), stop=(k == 7)),
                        reads=[wt_b, hT_b[k]], writes=[pb])
                if nevac % 2 == 0:
                    P.op("act", lambda e, sa=sa, ps=ps, c0=c0, n=n: e.copy(out=sa[:, c0:c0 + n], in_=ps[:, 0:n]), reads=[pb], writes=[sb_])
                else:
                    P.op("dve", lambda e, sa=sa, ps=ps, c0=c0, n=n: e.tensor_copy(out=sa[:, c0:c0 + n], in_=ps[:, 0:n]), reads=[pb], writes=[sb_])
                nevac += 1
            P.dma("sp", oT[i * 128:(i + 1) * 128, :], sa[:], reads=[sb_], writes=[oT_b])
        P.finish([oF_b, oT_b])
        P.emit(st)
    return nc


def build_c():
    from contextlib import ExitStack
    nc = bass.Bass("TRN2", target_bir_lowering=False)
    S = SEQ
    qT = nc.dram_tensor("qT", [128, S], F32, kind="ExternalInput").ap()
    kT = nc.dram_tensor("kT", [128, S], F32, kind="ExternalInput").ap()
    v = nc.dram_tensor("v", [S, 128], F32, kind="ExternalInput").ap()
    lamb = nc.dram_tensor("lamb", [128, 4, 64], F32, kind="ExternalInput").ap()
    cst = nc.dram_tensor("cst", [128, 2], F32, kind="ExternalInput").ap()
    ng = nc.dram_tensor("ng", [128, 128], F32, kind="ExternalInput").ap()
    oc = nc.dram_tensor("oc", [S, 128], F32, kind="ExternalOutput").ap()
    with ExitStack() as st:
        P = Prog(nc)
        qTb = _sb(st, nc, "qTb", [128, S], BF16); q_b = Buf()
        kTb = _sb(st, nc, "kTb", [128, S], BF16); k_b = Buf()
        va = _sb(st, nc, "va", [128, S // 128, 130], BF16); v_b = Buf()
        lt = _sb(st, nc, "lt", [128, 4, 64], F32); l_b = Buf()
        ct = _sb(st, nc, "ct", [128, 2], F32); c_b = Buf()
        ngt = _sb(st, nc, "ngt", [128, 128], F32); ng_b = Buf()
        gl = _sb(st, nc, "gl", [128, 128], F32); gl_b = Buf()
        junk = _sb(st, nc, "junk", [128, 128], F32); junk_b = Buf()
        sm = _sb(st, nc, "sm", [128, 8], F32); sm_b = Buf()
        nlam = _sb(st, nc, "nlam", [128, 1], F32); nlam_b = Buf()
        Er = Ring([(_sb(st, nc, "E%d" % i, [128, 512], BF16), Buf()) for i in range(4)])
        stg = Ring([(_sb(st, nc, "stg%d" % i, [128, 4, 128], F32), Buf()) for i in range(2)])
        tr = Ring([(_sb(st, nc, "tt%d" % i, [128, 128], F32), Buf()) for i in range(2)])
        orr = Ring([(_sb(st, nc, "oo%d" % i, [128, 128], F32), Buf()) for i in range(2)])
        sr = Ring([(_sb(st, nc, "ss%d" % i, [128, 8], F32), Buf()) for i in range(2)])
        banks = _psum_banks(st, nc)
        accb = banks[:3]
        stb = Ring(banks[3:])
        oc_b = Buf()

        P.dma("pool", qTb[:], qT[:, :], writes=[q_b])
        P.dma("pool", kTb[:], kT[:, :], writes=[k_b])
        P.dma("pool", va[:, :, 0:128], v.rearrange("(n p) d -> p n d", p=128), writes=[v_b])
        P.op("dve", lambda e: e.memset(va[:, :, 128:130], 1.0), writes=[v_b])
        P.dma("sp", lt[:], lamb[:, :, :], writes=[l_b])
        P.dma("sp", ct[:], cst[:, :], writes=[c_b])
        P.dma("sp", ngt[:], ng[:, :], writes=[ng_b])
        P.op("dve", lambda e: e.tensor_tensor(out=junk[:, 0:64], in0=lt[:, 0, :], in1=lt[:, 1, :], op=ALU.mult), reads=[l_b], writes=[junk_b])
        P.op("dve", lambda e: e.reduce_sum(out=sm[:, 0:1], in_=junk[:, 0:64], axis=AX.X), reads=[junk_b], writes=[sm_b])
        P.op("dve", lambda e: e.tensor_tensor(out=junk[:, 64:128], in0=lt[:, 2, :], in1=lt[:, 3, :], op=ALU.mult), reads=[l_b], writes=[junk_b])
        P.op("dve", lambda e: e.reduce_sum(out=sm[:, 1:2], in_=junk[:, 64:128], axis=AX.X), reads=[junk_b], writes=[sm_b])
        P.op("act", lambda e: e.activation(out=sm[:, 2:4], in_=sm[:, 0:2], func=AF.Exp), reads=[sm_b], writes=[sm_b])
        P.op("dve", lambda e: e.tensor_tensor(out=sm[:, 4:5], in0=sm[:, 3:4], in1=sm[:, 2:3], op=ALU.subtract), reads=[sm_b], writes=[sm_b])
        P.op("dve", lambda e: e.tensor_tensor(out=nlam[:], in0=sm[:, 4:5], in1=ct[:, 0:1], op=ALU.subtract), reads=[sm_b, c_b], writes=[nlam_b])
        P.op("dve", lambda e: e.tensor_scalar(out=gl[:], in0=ngt[:], scalar1=ct[:, 1:2], scalar2=None, op0=ALU.mult), reads=[ng_b, c_b], writes=[gl_b])

        def acc_ap(m, qb):
            a = m * 4 + qb
            return accb[a // 3][0], accb[a // 3][1], (a % 3) * 130

        NG = S // 512
        NK = S // 128
        for g in range(NG):
            started = set()
            for kc in range(NK):
                for m in range(2):
                    ps, pb = stb.next()
                    P.op("pe", lambda e, ps=ps, m=m, kc=kc, g=g: e.matmul(
                        ps[:, :], kTb[m * 64:(m + 1) * 64, kc * 128:(kc + 1) * 128], qTb[m * 64:(m + 1) * 64, g * 512:(g + 1) * 512],
                        start=True, stop=True), reads=[k_b, q_b], writes=[pb])
                    E, Eb = Er.next()
                    P.op("act", lambda e, E=E, ps=ps: e.activation(out=E[:], in_=ps[:, :], func=AF.Exp, scale=0.125), reads=[pb], writes=[Eb])
                    for qb in range(4):
                        acc, ab, off = acc_ap(m, qb)
                        bi = (m * 4 + qb) // 3
                        first = bi not in started
                        started.add(bi)
                        P.op("pe", lambda e, acc=acc, off=off, E=E, qb=qb, kc=kc, first=first: e.matmul(
                            acc[:, off:off + 129], E[:, qb * 128:(qb + 1) * 128], va[:, kc, 0:129],
                            start=first, stop=(kc == NK - 1), skip_group_check=True), reads=[Eb, v_b], writes=[ab])
            sa, sb_ = stg.next()
            for qb in range(4):
                a0, a0b, o0 = acc_ap(0, qb)
                a1, a1b, o1 = acc_ap(1, qb)
                ss, ssb = sr.next(); tt, ttb = tr.next(); oo, oob = orr.next()
                P.op("dve", lambda e, ss=ss, a0=a0, o0=o0: e.reciprocal(out=ss[:, 0:1], in_=a0[:, o0 + 128:o0 + 129]), reads=[a0b], writes=[ssb])
                P.op("dve", lambda e, ss=ss, a1=a1, o1=o1: e.reciprocal(out=ss[:, 1:2], in_=a1[:, o1 + 128:o1 + 129]), reads=[a1b], writes=[ssb])
                P.op("dve", lambda e, ss=ss: e.tensor_tensor(out=ss[:, 2:3], in0=ss[:, 1:2], in1=nlam[:], op=ALU.mult), reads=[ssb, nlam_b], writes=[ssb])
                P.op("dve", lambda e, tt=tt, a1=a1, o1=o1, ss=ss: e.tensor_scalar(out=tt[:], in0=a1[:, o1:o1 + 128], scalar1=ss[:, 2:3], scalar2=None, op0=ALU.mult),
                     reads=[a1b, ssb], writes=[ttb])
                P.op("dve", lambda e, oo=oo, a0=a0, o0=o0, ss=ss, tt=tt: e.scalar_tensor_tensor(
                    out=oo[:], in0=a0[:, o0:o0 + 128], scalar=ss[:, 0:1], in1=tt[:], op0=ALU.mult, op1=ALU.add), reads=[a0b, ssb, ttb], writes=[oob])
                P.op("act", lambda e, oo=oo, ss=ss: e.activation(out=junk[:], in_=oo[:], func=AF.Square, accum_out=ss[:, 3:4]),
                     reads=[oob], writes=[junk_b, ssb])
                P.op("act", lambda e, ss=ss: e.activation(out=ss[:, 4:5], in_=ss[:, 3:4], func=AF.Sqrt, scale=1.0 / 128.0, bias=LN_EPS), reads=[ssb], writes=[ssb])
                P.op("dve", lambda e, ss=ss: e.reciprocal(out=ss[:, 5:6], in_=ss[:, 4:5]), reads=[ssb], writes=[ssb])
                P.op("dve", lambda e, sa=sa, qb=qb, oo=oo, ss=ss: e.scalar_tensor_tensor(
                    out=sa[:, qb, :], in0=oo[:], scalar=ss[:, 5:6], in1=gl[:], op0=ALU.mult, op1=ALU.mult), reads=[oob, ssb, gl_b], writes=[sb_])
            P.dma("sp", oc.rearrange("(n p) d -> p n d", p=128)[:, g * 4:(g + 1) * 4, :], sa[:], reads=[sb_], writes=[oc_b])
        P.finish([oc_b])
        P.emit(st)
    return nc


def build_a():
    from contextlib import ExitStack
    nc = bass.Bass("TRN2", target_bir_lowering=False)
    S = SEQ
    NB = S // 128
    qT = nc.dram_tensor("qT", [128, S], F32, kind="ExternalInput").ap()
    kT = nc.dram_tensor("kT", [128, S], F32, kind="ExternalInput").ap()
    v = nc.dram_tensor("v", [S, 64], F32, kind="ExternalInput").ap()
    msk = nc.dram_tensor("msk", [128, 384], F32, kind="ExternalInput").ap()
    idn = nc.dram_tensor("idn", [128, 128], F32, kind="ExternalInput").ap()
    snk = nc.dram_tensor("snk", [128, 2], F32, kind="ExternalInput").ap()
    oa = nc.dram_tensor("oa", [S, 128], F32, kind="ExternalOutput").ap()
    with ExitStack() as st:
        P = Prog(nc)
        qTb = _sb(st, nc, "qTb", [128, S], BF16); q_b = Buf()
        kTb = _sb(st, nc, "kTb", [128, S], BF16); k_b = Buf()
        va = _sb(st, nc, "va", [128, NB, 66], BF16); v_b = Buf()
        mb = _sb(st, nc, "mb", [128, 384], BF16); m_b = Buf()
        ib = _sb(st, nc, "ib", [128, 128], BF16); i_b = Buf()
        sk = _sb(st, nc, "sk", [128, 2], F32); sk_b = Buf()
        esk = _sb(st, nc, "esk", [128, 2], F32); esk_b = Buf()
        Er = Ring([(_sb(st, nc, "E%d" % i, [128, 384], BF16), Buf()) for i in range(4)])
        sr = Ring([(_sb(st, nc, "ss%d" % i, [128, 2], F32), Buf()) for i in range(4)])
        stg = _sb(st, nc, "stg", [128, NB, 128], F32); stg_b = Buf()
        banks = _psum_banks(st, nc)
        accb = banks[:4]
        stb = Ring(banks[4:])
        oa_b = Buf()
        P.dma("pool", qTb[:], qT[:, :], writes=[q_b])
        P.dma("pool", kTb[:], kT[:, :], writes=[k_b])
        P.dma("pool", va[:, :, 0:64], v.rearrange("(n p) d -> p n d", p=128), writes=[v_b])
        P.op("dve", lambda e: e.memset(va[:, :, 64:66], 1.0), writes=[v_b])
        P.dma("pool", mb[:], msk[:, :], writes=[m_b])
        P.dma("pool", ib[:], idn[:, :], writes=[i_b])
        P.dma("sp", sk[:], snk[:, :], writes=[sk_b])
        P.op("act", lambda e: e.activation(out=esk[:], in_=sk[:], func=AF.Exp), reads=[sk_b], writes=[esk_b])
        for hh in range(2):
            pr = slice(hh * 64, (hh + 1) * 64)
            for m in range(NB):
                lo = max(m - 1, 0); hi = min(m + 1, NB - 1)
                ncol = (hi - lo + 1) * 128
                off = (lo - (m - 1)) * 128
                ps, pb = stb.next()
                P.op("pe", lambda e, ps=ps, pr=pr, m=m, lo=lo, hi=hi, ncol=ncol: e.matmul(
                    ps[:, 0:ncol], kTb[pr, m * 128:(m + 1) * 128], qTb[pr, lo * 128:(hi + 1) * 128], start=True, stop=False),
                    reads=[k_b, q_b], writes=[pb])
                P.op("pe", lambda e, ps=ps, off=off, ncol=ncol: e.matmul(
                    ps[:, 0:ncol], ib[:, :], mb[:, off:off + ncol], start=False, stop=True), reads=[i_b, m_b], writes=[pb])
                E, Eb = Er.next()
                P.op("act", lambda e, E=E, ps=ps, ncol=ncol: e.activation(out=E[:, 0:ncol], in_=ps[:, 0:ncol], func=AF.Exp, scale=0.125),
                     reads=[pb], writes=[Eb])
                for qb in range(lo, hi + 1):
                    acc, ab = accb[qb % 4]
                    P.op("pe", lambda e, acc=acc, E=E, qb=qb, lo=lo, m=m: e.matmul(
                        acc[:, 0:65], E[:, (qb - lo) * 128:(qb - lo + 1) * 128], va[:, m, 0:65],
                        start=(m == max(qb - 1, 0)), stop=(m == min(qb + 1, NB - 1))), reads=[Eb, v_b], writes=[ab])
                done = [qb for qb in range(lo, hi + 1) if min(qb + 1, NB - 1) == m]
                for qb in done:
                    acc, ab = accb[qb % 4]
                    ss, ssb = sr.next()
                    P.op("dve", lambda e, ss=ss, acc=acc, hh=hh: e.tensor_tensor(out=ss[:, 0:1], in0=acc[:, 64:65], in1=esk[:, hh:hh + 1], op=ALU.add),
                         reads=[ab, esk_b], writes=[ssb])
                    P.op("dve", lambda e, ss=ss: e.reciprocal(out=ss[:, 1:2], in_=ss[:, 0:1]), reads=[ssb], writes=[ssb])
                    P.op("dve", lambda e, ss=ss, acc=acc, qb=qb, hh=hh: e.tensor_scalar(
                        out=stg[:, qb, hh * 64:(hh + 1) * 64], in0=acc[:, 0:64], scalar1=ss[:, 1:2], scalar2=None, op0=ALU.mult),
                        reads=[ab, ssb], writes=[stg_b])
        P.dma("sp", oa.rearrange("(n p) d -> p n d", p=128), stg[:], reads=[stg_b], writes=[oa_b])
        P.finish([oa_b])
        P.emit(st)
    return nc


def a_mask_np():
    kj = np.arange(128)[:, None]; qi = np.arange(128)[None, :]
    m = np.zeros((128, 384), np.float32)
    m[:, 0:128] = np.where(kj <= qi, 0.0, NEG)
    m[:, 256:384] = np.where(kj >= qi, 0.0, NEG)
    return m


def build_b():
    from contextlib import ExitStack
    nc = bass.Bass("TRN2", target_bir_lowering=False)
    S = SEQ
    NB = S // 128
    ROWS = S // 64
    qT = nc.dram_tensor("qT", [128, S], F32, kind="ExternalInput").ap()
    kT = nc.dram_tensor("kT", [128, S], F32, kind="ExternalInput").ap()
    v = nc.dram_tensor("v", [S, 128], F32, kind="ExternalInput").ap()
    vsh = nc.dram_tensor("vsh", [S, 128], F32, kind="ExternalInput").ap()
    bias = nc.dram_tensor("bias", [128, 2 * 8 * 256], F32, kind="ExternalInput").ap()
    idn = nc.dram_tensor("idn", [128, 128], F32, kind="ExternalInput").ap()
    ob = nc.dram_tensor("ob", [S, 128], F32, kind="ExternalOutput").ap()
    with ExitStack() as st:
        P = Prog(nc)
        qTb = _sb(st, nc, "qTb", [128, S], BF16); q_b = Buf()
        kTb = _sb(st, nc, "kTb", [128, S], BF16); k_b = Buf()
        va = [_sb(st, nc, "va%d" % i, [128, NB, 2, 66], BF16) for i in range(2)]; v_b = Buf()
        bb = _sb(st, nc, "bb", [128, 2, 8, 256], BF16); b_b = Buf()
        ib = _sb(st, nc, "ib", [128, 128], BF16); i_b = Buf()
        Er = Ring([(_sb(st, nc, "E%d" % i, [128, 256], BF16), Buf()) for i in range(4)])
        sr = Ring([(_sb(st, nc, "ss%d" % i, [64, 2], F32), Buf()) for i in range(4)])
        stg = _sb(st, nc, "stg", [64, ROWS, 128], F32); stg_b = Buf()
        banks = _psum_banks(st, nc)
        accb = Ring(banks[:4])
        stb = Ring(banks[4:])
        ob_b = Buf()
        P.dma("pool", qTb[:], qT[:, :], writes=[q_b])
        P.dma("pool", kTb[:], kT[:, :], writes=[k_b])
        for i, src in enumerate((v, vsh)):
            for h2 in range(2):
                P.dma("pool", va[i][:, :, h2, 0:64], src[:, h2 * 64:(h2 + 1) * 64].rearrange("(n p) d -> p n d", p=128), writes=[v_b])
            P.op("dve", lambda e, i=i: e.memset(va[i][:, :, :, 64:66], 1.0), writes=[v_b])
        P.dma("pool", bb[:], bias.rearrange("p (h c f) -> p h c f", h=2, c=8), writes=[b_b])
        P.dma("pool", ib[:], idn[:, :], writes=[i_b])
        for hh in range(2):
            pr = slice(hh * 64, (hh + 1) * 64)
            for r in range(ROWS):
                r0 = min(max(r - 4, 0), ROWS - 8)
                cls = r if r < 4 else (4 if r <= ROWS - 4 else r - (ROWS - 8))
                ps, pb = stb.next()
                for c4 in range(4):
                    kt = 64 * r0 + 128 * c4
                    P.op("pe", lambda e, ps=ps, pr=pr, kt=kt, r=r, c4=c4: e.matmul(
                        ps[:, c4 * 64:(c4 + 1) * 64], kTb[pr, kt:kt + 128], qTb[pr, r * 64:(r + 1) * 64],
                        start=(c4 == 0), stop=False, skip_group_check=True), reads=[k_b, q_b], writes=[pb])
                P.op("pe", lambda e, ps=ps, hh=hh, cls=cls: e.matmul(
                    ps[:, 0:256], ib[:, :], bb[:, hh, cls, :], start=False, stop=True, skip_group_check=True), reads=[i_b, b_b], writes=[pb])
                E, Eb = Er.next()
                P.op("act", lambda e, E=E, ps=ps: e.activation(out=E[:, :], in_=ps[:, 0:256], func=AF.Exp, scale=0.125), reads=[pb], writes=[Eb])
                acc, ab = accb.next()
                vsel = va[r0 % 2]
                for c4 in range(4):
                    n = r0 // 2 + c4
                    P.op("pe", lambda e, acc=acc, E=E, c4=c4, vsel=vsel, n=n, hh=hh: e.matmul(
                        acc[0:64, 0:65], E[:, c4 * 64:(c4 + 1) * 64], vsel[:, n, hh, 0:65], start=(c4 == 0), stop=(c4 == 3)),
                        reads=[Eb, v_b], writes=[ab])
                ss, ssb = sr.next()
                P.op("dve", lambda e, ss=ss, acc=acc: e.reciprocal(out=ss[:, 0:1], in_=acc[0:64, 64:65]), reads=[ab], writes=[ssb])
                P.op("dve", lambda e, ss=ss, acc=acc, r=r, hh=hh: e.tensor_scalar(
                    out=stg[:, r, hh * 64:(hh + 1) * 64], in0=acc[0:64, 0:64], scalar1=ss[:, 0:1], scalar2=None, op0=ALU.mult),
                    reads=[ab, ssb], writes=[stg_b])
        P.dma("sp", ob.rearrange("(r p) d -> p r d", p=64), stg[:], reads=[stg_b], writes=[ob_b])
        P.finish([ob_b])
        P.emit(st)
    return nc


def b_bias_np(rpb2):
    ROWS = SEQ // 64
    out = np.full((2, 8, 4, 128, 64), NEG, np.float32)
    reps = [0, 1, 2, 3, 4, ROWS - 3, ROWS - 2, ROWS - 1]
    qc = np.arange(64)
    cstart = np.clip(qc - 8, 0, 64 - 16)
    for ci, r in enumerate(reps):
        r0 = min(max(r - 4, 0), ROWS - 8)
        for c4 in range(4):
            for p in range(128):
                krow = r0 + 2 * c4 + p // 64
                kcol = p % 64
                drow = krow - r + 7
                valid = (kcol >= cstart) & (kcol < cstart + 16)
                dcol = np.clip(kcol - qc, -15, 15) + 15
                for h in range(2):
                    out[h, ci, c4, p, :] = np.where(valid, rpb2[h, drow, dcol], NEG)
    return np.ascontiguousarray(out.transpose(3, 0, 1, 2, 4)).reshape(128, 2 * 8 * 256)


def build_d(dbg=9):
    from contextlib import ExitStack
    nc = bass.Bass("TRN2", target_bir_lowering=False)
    S = SEQ
    CH = 2048
    NCH = S // CH
    dxT = nc.dram_tensor("dxT", [128, S], F32, kind="ExternalInput").ap()
    dgT = nc.dram_tensor("dgT", [128, S], F32, kind="ExternalInput").ap()
    wbd = nc.dram_tensor("wbd", [128, 4, 128], F32, kind="ExternalInput").ap()
    par = nc.dram_tensor("par", [128, 12], F32, kind="ExternalInput").ap()
    odT = nc.dram_tensor("odT", [128, S], F32, kind="ExternalOutput").ap()
    with ExitStack() as st:
        P = Prog(nc)
        X = _sb(st, nc, "X", [128, S], F32); X_b = Buf()
        XC = _sb(st, nc, "XC", [128, S], F32); XC_b = Buf()
        XCb = _sb(st, nc, "XCb", [128, S], BF16); XCb_b = Buf()
        wb = _sb(st, nc, "wb", [128, 4, 128], BF16); w_b = Buf()
        pt = _sb(st, nc, "pt", [128, 12], F32); p_b = Buf()
        sm = _sb(st, nc, "sm", [128, 8], F32); sm_b = Buf()
        HF_b = [Buf() for _ in range(NCH)]
        Rr = Ring([(_sb(st, nc, "R%d" % i, [128, CH], F32), Buf()) for i in range(2)])
        Ar = Ring([(_sb(st, nc, "A%d" % i, [128, CH], F32), Buf()) for i in range(2)])
        Ir = Ring([(_sb(st, nc, "I%d" % i, [128, CH], F32), Buf()) for i in range(2)])
        Sr = Ring([(_sb(st, nc, "S%d" % i, [128, CH], F32), Buf()) for i in range(2)])
        Hr = Ring([(_sb(st, nc, "H%d" % i, [128, CH], F32), Buf()) for i in range(2)])
        Gr = Ring([(_sb(st, nc, "G%d" % i, [128, CH], F32), Buf()) for i in range(2)])
        carry = _sb(st, nc, "carry", [128, 2], F32); carry_b = Buf()
        banks = Ring(_psum_banks(st, nc))
        od_b = Buf()
        P.dma("sp", X[:], dxT[:, :], writes=[X_b])
        P.dma("sp", pt[:], par[:, :], writes=[p_b])
        P.dma("pool", wb[:], wbd[:, :, :], writes=[w_b])
        P.op("act", lambda e: e.activation(out=sm[:, 0:2], in_=pt[:, 9:11], func=AF.Exp, scale=-1.0), reads=[p_b], writes=[sm_b])
        P.op("dve", lambda e: e.tensor_scalar(out=sm[:, 2:4], in0=sm[:, 0:2], scalar1=1.0, scalar2=None, op0=ALU.add), reads=[sm_b], writes=[sm_b])
        P.op("act", lambda e: e.activation(out=sm[:, 4:6], in_=sm[:, 2:4], func=AF.Ln), reads=[sm_b], writes=[sm_b])
        P.op("dve", lambda e: e.tensor_scalar(out=sm[:, 6:8], in0=sm[:, 4:6], scalar1=-8.0, scalar2=None, op0=ALU.mult), reads=[sm_b], writes=[sm_b])
        P.op("dve", lambda e: e.tensor_scalar(out=XC[:], in0=X[:], scalar1=pt[:, 2:3], scalar2=pt[:, 4:5], op0=ALU.mult, op1=ALU.add),
             reads=[X_b, p_b], writes=[XC_b])
        P.op("dve", lambda e: e.scalar_tensor_tensor(out=XC[:, 2:S], in0=X[:, 0:S - 2], scalar=pt[:, 0:1], in1=XC[:, 2:S], op0=ALU.mult, op1=ALU.add),
             reads=[X_b, p_b], writes=[XC_b])
        P.op("dve", lambda e: e.scalar_tensor_tensor(out=XC[:, 1:S], in0=X[:, 0:S - 1], scalar=pt[:, 1:2], in1=XC[:, 1:S], op0=ALU.mult, op1=ALU.add),
             reads=[X_b, p_b], writes=[XC_b])
        P.op("dve", lambda e: e.scalar_tensor_tensor(out=XC[:, 0:S - 1], in0=X[:, 1:S], scalar=pt[:, 3:4], in1=XC[:, 0:S - 1], op0=ALU.mult, op1=ALU.add),
             reads=[X_b, p_b], writes=[XC_b])
        P.op("act", lambda e: e.copy(out=XCb[:], in_=XC[:]), reads=[XC_b], writes=[XCb_b])

        if dbg == 0:
            P.dma("sp", odT[:, :], XC[:], reads=[XC_b, XCb_b, sm_b], writes=[od_b])
            P.finish([od_b]); P.emit(st)
            return nc

        def gate(dst, dstb, gi, bcol, c):
            for j in range(CH // 512):
                t0 = c * CH + j * 512
                ps, pb = banks.next()
                P.op("pe", lambda e, ps=ps, gi=gi, t0=t0: e.matmul(ps[:, :], wb[:, gi, :], XCb[:, t0:t0 + 512], start=True, stop=True),
                     reads=[w_b, XCb_b], writes=[pb])
                P.op("act", lambda e, ps=ps, dst=dst, j=j, bcol=bcol: e.activation(
                    out=dst[:, j * 512:(j + 1) * 512], in_=ps[:, :], func=AF.Sigmoid, bias=pt[:, bcol:bcol + 1]),
                    reads=[pb, p_b], writes=[dstb])

        def prep(d, c):
            R, Rb = Rr.next(); A, Ab = Ar.next(); I, Ib = Ir.next(); S2, S2b = Sr.next()
            gate(R, Rb, 2 * d, 5 + 2 * d, c)
            P.op("act", lambda e, A=A, R=R, d=d: e.activation(out=A[:], in_=R[:], func=AF.Exp, scale=sm[:, 6 + d:7 + d]), reads=[Rb, sm_b], writes=[Ab])
            gate(I, Ib, 2 * d + 1, 6 + 2 * d, c)
            P.op("dve", lambda e, I=I, c=c: e.tensor_tensor(out=I[:], in0=I[:], in1=XC[:, c * CH:(c + 1) * CH], op=ALU.mult), reads=[Ib, XC_b], writes=[Ib])
            P.op("pool", lambda e, S2=S2, A=A: e.tensor_tensor(out=S2[:], in0=A[:], in1=A[:], op=ALU.mult), reads=[Ab], writes=[S2b])
            P.op("dve", lambda e, S2=S2: e.tensor_scalar(out=S2[:], in0=S2[:], scalar1=-1.0, scalar2=1.0, op0=ALU.mult, op1=ALU.add), reads=[S2b], writes=[S2b])
            P.op("dve", lambda e, S2=S2: e.tensor_scalar(out=S2[:], in0=S2[:], scalar1=0.0, scalar2=None, op0=ALU.max), reads=[S2b], writes=[S2b])
            P.op("act", lambda e, S2=S2: e.activation(out=S2[:], in_=S2[:], func=AF.Sqrt), reads=[S2b], writes=[S2b])
            P.op("pool", lambda e, S2=S2, I=I: e.tensor_tensor(out=S2[:], in0=S2[:], in1=I[:], op=ALU.mult), reads=[S2b, Ib], writes=[S2b])
            return A, Ab, S2, S2b

        for c in range(NCH):
            A, Ab, Bv, Bvb = prep(0, c)
            sl = slice(c * CH, (c + 1) * CH)
            if c == 0:
                P.op("dve", lambda e, A=A, Bv=Bv, sl=sl: e.tensor_tensor_scan(out=X[:, sl], data0=A[:], data1=Bv[:], initial=0.0, op0=ALU.mult, op1=ALU.add),
                     reads=[Ab, Bvb, XC_b], writes=[X_b, HF_b[c]])
            else:
                P.op("dve", lambda e, A=A, Bv=Bv, sl=sl, c=c: e.tensor_tensor_scan(
                    out=X[:, sl], data0=A[:], data1=Bv[:], initial=X[:, c * CH - 1:c * CH], op0=ALU.mult, op1=ALU.add),
                    reads=[Ab, Bvb, HF_b[c - 1]], writes=[HF_b[c]])
        if dbg == 1:
            P.dma("sp", odT[:, :], X[:], reads=HF_b, writes=[od_b])
            P.finish([od_b]); P.emit(st)
            return nc
        for c in range(NCH - 1, -1, -1):
            A, Ab, Bv, Bvb = prep(1, c)
            H, Hb = Hr.next(); G, Gb = Gr.next()
            sl = slice(c * CH, (c + 1) * CH)
            P.dma("sp", G[:], dgT[:, sl], writes=[Gb])
            if c == NCH - 1:
                P.op("dve", lambda e, A=A, Bv=Bv, H=H: e.tensor_tensor_scan(
                    out=H[:, ::-1], data0=A[:, ::-1], data1=Bv[:, ::-1], initial=0.0, op0=ALU.mult, op1=ALU.add), reads=[Ab, Bvb], writes=[Hb])
            else:
                P.op("dve", lambda e, A=A, Bv=Bv, H=H: e.tensor_tensor_scan(
                    out=H[:, ::-1], data0=A[:, ::-1], data1=Bv[:, ::-1], initial=carry[:, 0:1], op0=ALU.mult, op1=ALU.add),
                    reads=[Ab, Bvb, carry_b], writes=[Hb])
            P.op("dve", lambda e, H=H: e.tensor_copy(out=carry[:, 0:1], in_=H[:, 0:1]), reads=[Hb], writes=[carry_b])
            P.op("pool", lambda e, H=H, sl=sl: e.tensor_tensor(out=H[:], in0=H[:], in1=X[:, sl], op=ALU.add), reads=[Hb, HF_b[c]], writes=[Hb])
            if dbg != 2:
                T1, T1b = Rr.next()
                P.op("pool", lambda e, T1=T1, G=G: e.tensor_tensor(out=T1[:], in0=G[:], in1=G[:], op=ALU.mult), reads=[Gb], writes=[T1b])
                P.op("dve", lambda e, T1=T1: e.tensor_scalar(out=T1[:], in0=T1[:], scalar1=0.044715, scalar2=1.0, op0=ALU.mult, op1=ALU.add), reads=[T1b], writes=[T1b])
                P.op("pool", lambda e, T1=T1, G=G: e.tensor_tensor(out=T1[:], in0=T1[:], in1=G[:], op=ALU.mult), reads=[Gb, T1b], writes=[T1b])
                P.op("act", lambda e, T1=T1: e.activation(out=T1[:], in_=T1[:], func=AF.Sigmoid, scale=2.0 * math.sqrt(2.0 / math.pi)), reads=[T1b], writes=[T1b])
                P.op("dve", lambda e, H=H, G=G: e.tensor_tensor(out=G[:], in0=H[:], in1=G[:], op=ALU.mult), reads=[Hb, Gb], writes=[Gb])
                P.op("pool", lambda e, T1=T1, G=G: e.tensor_tensor(out=G[:], in0=T1[:], in1=G[:], op=ALU.mult), reads=[Gb, T1b], writes=[Gb])
            else:
                P.op("dve", lambda e, H=H, G=G: e.tensor_copy(out=G[:], in_=H[:]), reads=[Hb, Gb], writes=[Gb])
            P.dma("sp", odT[:, sl], G[:], reads=[Gb], writes=[od_b])
        P.finish([od_b])
        P.emit(st)
    return nc


def _layernorm_tile(P, y, yb, out, outb, lngt, lnbt, ln_b, stt, mv, smb, tmp, tmpb):
    P.op("dve", lambda e: e.bn_stats(out=stt[:, 0, :], in_=y[:, 0:512]), reads=[yb], writes=[smb])
    P.op("dve", lambda e: e.bn_stats(out=stt[:, 1, :], in_=y[:, 512:1024]), reads=[yb], writes=[smb])
    P.op("dve", lambda e: e.bn_aggr(out=mv[:, 0:2], in_=stt[:, :, :]), reads=[smb], writes=[smb])
    P.op("act", lambda e: e.activation(out=mv[:, 2:3], in_=mv[:, 1:2], func=AF.Sqrt, bias=LN_EPS), reads=[smb], writes=[smb])
    P.op("dve", lambda e: e.reciprocal(out=mv[:, 3:4], in_=mv[:, 2:3]), reads=[smb], writes=[smb])
    P.op("dve", lambda e: e.scalar_tensor_tensor(out=mv[:, 4:5], in0=mv[:, 0:1], scalar=-1.0, in1=mv[:, 3:4], op0=ALU.mult, op1=ALU.mult),
         reads=[smb], writes=[smb])
    P.op("act", lambda e: e.activation(out=tmp[:], in_=y[:], func=AF.Identity, scale=mv[:, 3:4], bias=mv[:, 4:5]), reads=[yb, smb], writes=[tmpb])
    P.op("dve", lambda e: e.tensor_tensor(out=tmp[:], in0=tmp[:], in1=lngt[:], op=ALU.mult), reads=[tmpb, ln_b], writes=[tmpb])
    P.op("pool", lambda e: e.tensor_tensor(out=out[:], in0=tmp[:], in1=lnbt[:], op=ALU.add), reads=[tmpb, ln_b], writes=[outb])


def build_m():
    from contextlib import ExitStack
    nc = bass.Bass("TRN2", target_bir_lowering=False)
    T = TPC
    TH = T // 2
    xT = nc.dram_tensor("xT", [1024, T], F32, kind="ExternalInput").ap()
    x = nc.dram_tensor("x", [T, 1024], F32, kind="ExternalInput").ap()
    brT = nc.dram_tensor("brT", [2048, T], F32, kind="ExternalInput").ap()
    mod = nc.dram_tensor("mod", [128, 16], F32, kind="ExternalInput").ap()
    g1 = nc.dram_tensor("g1", [128, 1024], F32, kind="ExternalInput").ap()
    wg = nc.dram_tensor("wg", [1024, 4096], F32, kind="ExternalInput").ap()
    bg = nc.dram_tensor("bg", [128, 32], F32, kind="ExternalInput").ap()
    wbr = nc.dram_tensor("wbr", [2048, 1024], F32, kind="ExternalInput").ap()
    wo = nc.dram_tensor("wo", [1024, 1024], F32, kind="ExternalInput").ap()
    lng = nc.dram_tensor("lng", [128, 1024], F32, kind="ExternalInput").ap()
    lnb = nc.dram_tensor("lnb", [128, 1024], F32, kind="ExternalInput").ap()
    x1 = nc.dram_tensor("x1", [T, 1024], F32, kind="ExternalOutput").ap()
    with ExitStack() as st:
        P = Prog(nc)
        hT = _sb(st, nc, "hT", [128, 8, TH], BF16); hT_b = [Buf() for _ in range(8)]
        brb = _sb(st, nc, "brb", [128, 16, TH], BF16); br_b = Buf()
        mp = _sb(st, nc, "mp", [128, 8, TH], BF16); mp_b = [Buf() for _ in range(8)]
        wbrb = _sb(st, nc, "wbrb", [128, 16, 1024], BF16); wbr_b = Buf()
        wob = _sb(st, nc, "wob", [128, 8, 1024], BF16); wo_b = Buf()
        modt = _sb(st, nc, "modt", [128, 16], F32); mod_b = Buf()
        sc1p = _sb(st, nc, "sc1p", [128, 8], F32); sc1p_b = Buf()
        bgt = _sb(st, nc, "bgt", [128, 32], F32); bg_b = Buf()
        g1t = _sb(st, nc, "g1t", [128, 1024], F32); g1_b = Buf()
        lngt = _sb(st, nc, "lngt", [128, 1024], F32); lnbt = _sb(st, nc, "lnbt", [128, 1024], F32); ln_b = Buf()
        xs = Ring([(_sb(st, nc, "xs%d" % i, [128, TH], F32), Buf()) for i in range(2)])
        wr = Ring([(_sb(st, nc, "wb%d" % i, [128, 8, 128], BF16), Buf()) for i in range(8)])
        gtr = Ring([(_sb(st, nc, "gt%d" % i, [128, 512], F32), Buf()) for i in range(3)])
        tmr = Ring([(_sb(st, nc, "tm%d" % i, [128, 512], F32), Buf()) for i in range(2)])
        acr = Ring([(_sb(st, nc, "ac%d" % i, [128, 512], F32), Buf()) for i in range(2)])
        xtr = Ring([(_sb(st, nc, "xt%d" % i, [128, 1024], F32), Buf()) for i in range(2)])
        yr = Ring([(_sb(st, nc, "y%d" % i, [128, 1024], F32), Buf()) for i in range(2)])
        tr = Ring([(_sb(st, nc, "t%d" % i, [128, 1024], F32), Buf()) for i in range(2)])
        outr = Ring([(_sb(st, nc, "o%d" % i, [128, 1024], F32), Buf()) for i in range(2)])
        smr = Ring([((_sb(st, nc, "stt%d" % i, [128, 2, 6], F32), _sb(st, nc, "mv%d" % i, [128, 8], F32)), Buf()) for i in range(2)])
        banks = Ring(_psum_banks(st, nc))
        x1_b = Buf()
        P.dma("sp", modt[:], mod[:, :], writes=[mod_b])
        P.dma("sp", bgt[:], bg[:, :], writes=[bg_b])
        P.dma("sp", g1t[:], g1[:, :], writes=[g1_b])
        P.dma("sp", lngt[:], lng[:, :], writes=[ln_b])
        P.dma("sp", lnbt[:], lnb[:, :], writes=[ln_b])
        P.op("dve", lambda e: e.tensor_scalar(out=sc1p[:], in0=modt[:, 0:8], scalar1=1.0, scalar2=None, op0=ALU.add), reads=[mod_b], writes=[sc1p_b])
        P.dma("pool", wbrb[:], wbr.rearrange("(c p) d -> p c d", p=128), writes=[wbr_b])
        P.dma("pool", wob[:], wo.rearrange("(c p) d -> p c d", p=128), writes=[wo_b])
        for half in range(2):
            tsl = slice(half * TH, (half + 1) * TH)
            for k in range(8):
                xa, xb_ = xs.next()
                P.dma("sp", xa[:], xT[k * 128:(k + 1) * 128, tsl], writes=[xb_])
                P.op("act", lambda e, xa=xa, k=k: e.activation(out=hT[:, k, :], in_=xa[:], func=AF.Identity,
                                                              scale=sc1p[:, k:k + 1], bias=modt[:, 8 + k:9 + k]),
                     reads=[xb_, sc1p_b, mod_b], writes=[hT_b[k]])
            P.dma("pool", brb[:], brT[:, tsl].rearrange("(c p) t -> p c t", p=128), writes=[br_b])
            for dc in range(8):
                ws = []
                for n in range(4):
                    wa, wb_ = wr.next()
                    c0 = n * 1024 + dc * 128
                    P.dma("pool", wa[:], wg[:, c0:c0 + 128].rearrange("(k p) c -> p k c", p=128), writes=[wb_])
                    ws.append((wa, wb_))
                for tb in range(TH // 512):
                    bsl = slice(tb * 512, (tb + 1) * 512)
                    ac, acb = acr.next()
                    for n in range(4):
                        wa, wb_ = ws[n]
                        ps, pb = banks.next()
                        for k in range(8):
                            P.op("pe", lambda e, ps=ps, wa=wa, k=k, bsl=bsl: e.matmul(ps[:, :], wa[:, k, :], hT[:, k, bsl], start=(k == 0), stop=(k == 7)),
                                 reads=[wb_, hT_b[k]], writes=[pb])
                        gt, gtb = gtr.next()
                        P.op("act", lambda e, gt=gt, ps=ps, n=n, dc=dc: e.activation(out=gt[:], in_=ps[:, :], func=AF.Sigmoid, bias=bgt[:, n * 8 + dc:n * 8 + dc + 1]),
                             reads=[pb, bg_b], writes=[gtb])
                        ps2, pb2 = banks.next()
                        for ec in range(4):
                            P.op("pe", lambda e, ps2=ps2, n=n, ec=ec, dc=dc, bsl=bsl: e.matmul(
                                ps2[:, :], wbrb[:, n * 4 + ec, dc * 128:(dc + 1) * 128], brb[:, n * 4 + ec, bsl], start=(ec == 0), stop=(ec == 3)),
                                reads=[wbr_b, br_b], writes=[pb2])
                        if n == 0:
                            P.op("dve", lambda e, ac=ac, ps2=ps2, gt=gt: e.tensor_tensor(out=ac[:], in0=ps2[:, :], in1=gt[:], op=ALU.mult),
                                 reads=[pb2, gtb], writes=[acb])
                        else:
                            tm, tmb = tmr.next()
                            P.op("dve", lambda e, tm=tm, ps2=ps2, gt=gt: e.tensor_tensor(out=tm[:], in0=ps2[:, :], in1=gt[:], op=ALU.mult),
                                 reads=[pb2, gtb], writes=[tmb])
                            P.op("pool", lambda e, ac=ac, tm=tm: e.tensor_tensor(out=ac[:], in0=ac[:], in1=tm[:], op=ALU.add), reads=[tmb, acb], writes=[acb])
                    P.op("act", lambda e, ac=ac, dc=dc, bsl=bsl: e.copy(out=mp[:, dc, bsl], in_=ac[:]), reads=[acb], writes=[mp_b[dc]])
            for i in range(TH // 128):
                isl = slice(i * 128, (i + 1) * 128)
                row0 = half * TH + i * 128
                xt, xtb = xtr.next(); y, yb = yr.next(); tmp, tmpb = tr.next(); o, ob_ = outr.next(); (stt, mv), smb = smr.next()
                P.dma("sp", xt[:], x[row0:row0 + 128, :], writes=[xtb])
                for hf in range(2):
                    csl = slice(hf * 512, (hf + 1) * 512)
                    ps, pb = banks.next()
                    for k in range(8):
                        P.op("pe", lambda e, ps=ps, k=k, isl=isl, csl=csl: e.matmul(ps[:, :], mp[:, k, isl], wob[:, k, csl], start=(k == 0), stop=(k == 7)),
                             reads=[mp_b[k], wo_b], writes=[pb])
                    P.op("dve", lambda e, tmp=tmp, ps=ps, csl=csl: e.tensor_tensor(out=tmp[:, csl], in0=ps[:, :], in1=g1t[:, csl], op=ALU.mult),
                         reads=[pb, g1_b], writes=[tmpb])
                P.op("dve", lambda e, y=y, xt=xt, tmp=tmp: e.scalar_tensor_tensor(out=y[:], in0=xt[:], scalar=ALPHA, in1=tmp[:], op0=ALU.mult, op1=ALU.add),
                     reads=[xtb, tmpb], writes=[yb])
                _layernorm_tile(P, y, yb, o, ob_, lngt, lnbt, ln_b, stt, mv, smb, tmp, tmpb)
                P.dma("sp", x1[row0:row0 + 128, :], o[:], reads=[ob_], writes=[x1_b])
        P.finish([x1_b])
        P.emit(st)
    return nc


GELU_C = 2.0 * math.sqrt(2.0 / math.pi)


def build_p(ntiles=TPC // 128, T=TPC):
    from contextlib import ExitStack
    nc = bass.Bass("TRN2", target_bir_lowering=False)
    x1 = nc.dram_tensor("x1", [T, 1024], F32, kind="ExternalInput").ap()
    x1T = nc.dram_tensor("x1T", [1024, T], F32, kind="ExternalInput").ap()
    mod = nc.dram_tensor("mod", [128, 16], F32, kind="ExternalInput").ap()
    modb = nc.dram_tensor("modb", [128, 3, 1024], F32, kind="ExternalInput").ap()
    wq = nc.dram_tensor("wq", [1024, 2048], F32, kind="ExternalInput").ap()
    skT = nc.dram_tensor("skT", [128, 16, 128], F32, kind="ExternalInput").ap()
    u = nc.dram_tensor("u", [16384, 1024], F32, kind="ExternalInput").ap()
    v = nc.dram_tensor("v", [16384, 1024], F32, kind="ExternalInput").ap()
    lng = nc.dram_tensor("lng", [128, 1024], F32, kind="ExternalInput").ap()
    lnb = nc.dram_tensor("lnb", [128, 1024], F32, kind="ExternalInput").ap()
    iot = nc.dram_tensor("iot", [128, 256], F32, kind="ExternalInput").ap()
    x2 = nc.dram_tensor("x2", [T, 1024], F32, kind="ExternalOutput").ap()
    with ExitStack() as st:
        P = Prog(nc)
        wqb = _sb(st, nc, "wqb", [128, 8, 2048], BF16); wq_b = Buf()
        skb = _sb(st, nc, "skb", [128, 16, 128], BF16); sk_b = Buf()
        modt = _sb(st, nc, "modt", [128, 16], F32); mod_b = Buf()
        sc2p = _sb(st, nc, "sc2p", [128, 8], F32); sc2p_b = Buf()
        mbt = _sb(st, nc, "mbt", [128, 3, 1024], F32); mb_b = Buf()
        lngt = _sb(st, nc, "lngt", [128, 1024], F32); lnbt = _sb(st, nc, "lnbt", [128, 1024], F32); ln_b = Buf()
        xTt = _sb(st, nc, "xTt", [128, 8, 128], F32); xTt_b = Buf()
        h2T = _sb(st, nc, "h2T", [128, 8, 128], BF16); h2T_b = Buf()
        xt = _sb(st, nc, "xt", [128, 1024], F32); xt_b = Buf()
        h2 = _sb(st, nc, "h2", [128, 1024], F32); h2_b = Buf()
        qTb = _sb(st, nc, "qTb", [128, 16, 128], BF16); qT_b = Buf()
        sc = _sb(st, nc, "sc", [128, 16, 128], F32); sc_b = Buf()
        sc2 = _sb(st, nc, "sc2", [128, 16, 128], F32); sc2_b = Buf()
        sv = _sb(st, nc, "sv", [128, 16, 16], F32); sv_b = Buf()
        si = _sb(st, nc, "si", [128, 16, 16], U32); si_b = Buf()
        sif = _sb(st, nc, "sif", [128, 16, 16], F32); sif_b = Buf()
        si1x = _sb(st, nc, "si1x", [128, 8, 16], F32); si1x_b = Buf()
        cand = _sb(st, nc, "cand", [128, 8, 256], F32); cand_b = Buf()
        cand2 = _sb(st, nc, "cand2", [128, 8, 256], F32); cand2_b = Buf()
        candi = _sb(st, nc, "candi", [128, 8, 256], F32); candi_b = Buf()
        tv = _sb(st, nc, "tv", [128, 8, 16], F32); tv_b = Buf()
        tj = _sb(st, nc, "tj", [128, 8, 16], U32); tj_b = Buf()
        tjf = _sb(st, nc, "tjf", [128, 8, 16], F32); tjf_b = Buf()
        iott = _sb(st, nc, "iott", [128, 256], F32); iot_b = Buf()
        ev = _sb(st, nc, "ev", [128, 8, 16], F32); ev_b = Buf()
        gg = _sb(st, nc, "gg", [128, 128], F32); gg_b = Buf()
        sm = _sb(st, nc, "sm", [128, 32], F32); sm_b = Buf()
        idxf = _sb(st, nc, "idxf", [128, 128], F32); idxf_b = Buf()
        idxu = _sb(st, nc, "idxu", [128, 128], U32); idxu_b = Buf()
        hu = _sb(st, nc, "hu", [128, 128], F32); hu_b = Buf()
        tg = _sb(st, nc, "tg", [128, 128], F32); tg_b = Buf()
        ww = _sb(st, nc, "ww", [128, 128], F32); ww_b = Buf()
        junk = _sb(st, nc, "junk", [128, 1024], F32); junk_b = Buf()
        ubr = Ring([(_sb(st, nc, "ub%d" % i, [128, 1024], F32), Buf()) for i in range(6)])
        acc = _sb(st, nc, "acc", [128, 1024], F32); acc_b = Buf()
        y = _sb(st, nc, "y", [128, 1024], F32); y_b = Buf()
        tmp = _sb(st, nc, "tmp", [128, 1024], F32); tmp_b = Buf()
        outr = Ring([(_sb(st, nc, "o%d" % i, [128, 1024], F32), Buf()) for i in range(2)])
        stt = _sb(st, nc, "stt", [128, 2, 6], F32); mv = _sb(st, nc, "mv", [128, 8], F32); smb2 = Buf()
        banks = Ring(_psum_banks(st, nc))
        x2_b = Buf()
        P.dma("sp", modt[:], mod[:, :], writes=[mod_b])
        P.dma("sp", mbt[:], modb[:, :, :], writes=[mb_b])
        P.dma("sp", lngt[:], lng[:, :], writes=[ln_b])
        P.dma("sp", lnbt[:], lnb[:, :], writes=[ln_b])
        P.dma("sp", iott[:], iot[:, :], writes=[iot_b])
        P.dma("pool", wqb[:], wq.rearrange("(k p) c -> p k c", p=128), writes=[wq_b])
        P.dma("pool", skb[:], skT[:, :, :], writes=[sk_b])
        P.op("dve", lambda e: e.tensor_scalar(out=sc2p[:], in0=modt[:, 0:8], scalar1=1.0, scalar2=None, op0=ALU.add), reads=[mod_b], writes=[sc2p_b])
        P.op("dve", lambda e: e.tensor_scalar(out=mbt[:, 0, :], in0=mbt[:, 0, :], scalar1=1.0, scalar2=None, op0=ALU.add), reads=[mb_b], writes=[mb_b])

        def gather(table, slot):
            ub, ubb = ubr.next()
            P._waits("pool", P._deps([idxu_b], [ubb]))
            i = P.dma_rr["pool"]; P.dma_rr["pool"] = (i + 1) % P.ndsem
            key = "d_pool%d" % i
            P.cnt[key] = P.cnt.get(key, 0) + 16
            P.ops["pool"].append(("op", lambda e, ub=ub, slot=slot: e.indirect_dma_start(
                out=ub[:, :], out_offset=None, in_=table[:, :],
                in_offset=bass.IndirectOffsetOnAxis(ap=idxu[:, slot:slot + 1], axis=0)), key, 16))
            P._mark((key, P.cnt[key]), [idxu_b], [ubb])
            return ub, ubb

        for i in range(ntiles):
            tsl = slice(i * 128, (i + 1) * 128)
            P.dma("sp", xTt[:], x1T.rearrange("(k p) t -> p k t", p=128)[:, :, tsl], writes=[xTt_b])
            P.dma("sp", xt[:], x1[tsl, :], writes=[xt_b])
            for k in range(8):
                P.op("act", lambda e, k=k: e.activation(out=h2T[:, k, :], in_=xTt[:, k, :], func=AF.Identity, scale=sc2p[:, k:k + 1], bias=modt[:, 8 + k:9 + k]),
                     reads=[xTt_b, sc2p_b, mod_b], writes=[h2T_b])
            P.op("dve", lambda e: e.tensor_tensor(out=h2[:], in0=xt[:], in1=mbt[:, 0, :], op=ALU.mult), reads=[xt_b, mb_b], writes=[h2_b])
            P.op("pool", lambda e: e.tensor_tensor(out=h2[:], in0=h2[:], in1=mbt[:, 1, :], op=ALU.add), reads=[h2_b, mb_b], writes=[h2_b])
            for g4 in range(4):
                ps, pb = banks.next()
                for j in range(4):
                    hp = g4 * 4 + j
                    for k in range(8):
                        P.op("pe", lambda e, ps=ps, j=j, hp=hp, k=k: e.matmul(ps[:, j * 128:(j + 1) * 128], wqb[:, k, hp * 128:(hp + 1) * 128], h2T[:, k, :],
                                                                     start=(k == 0), stop=(k == 7), skip_group_check=True), reads=[wq_b, h2T_b], writes=[pb])
                P.op("act", lambda e, ps=ps, g4=g4: e.copy(out=qTb[:, g4 * 4:(g4 + 1) * 4, :], in_=ps[:, :].rearrange("p (a b) -> p a b", a=4)), reads=[pb], writes=[qT_b])
            for g4 in range(4):
                ps, pb = banks.next()
                for j in range(4):
                    hp = g4 * 4 + j
                    P.op("pe", lambda e, ps=ps, j=j, hp=hp: e.matmul(ps[:, j * 128:(j + 1) * 128], qTb[:, hp, :], skb[:, hp, :], start=True, stop=True, skip_group_check=True),
                         reads=[qT_b, sk_b], writes=[pb])
                P.op("act", lambda e, ps=ps, g4=g4: e.copy(out=sc[:, g4 * 4:(g4 + 1) * 4, :], in_=ps[:, :].rearrange("p (a b) -> p a b", a=4)), reads=[pb], writes=[sc_b])
            for hp in range(16):
                P.op("dve", lambda e, hp=hp: e.max(out=sv[:, hp, 0:8], in_=sc[:, hp, :]), reads=[sc_b], writes=[sv_b])
                P.op("dve", lambda e, hp=hp: e.match_replace(out=sc2[:, hp, :], in_to_replace=sv[:, hp, 0:8], in_values=sc[:, hp, :], imm_value=-1e30),
                     reads=[sc_b, sv_b], writes=[sc2_b])
                P.op("dve", lambda e, hp=hp: e.max(out=sv[:, hp, 8:16], in_=sc2[:, hp, :]), reads=[sc2_b], writes=[sv_b])
                P.op("dve", lambda e, hp=hp: e.max_index(out=si[:, hp, 0:8], in_max=sv[:, hp, 0:8], in_values=sc[:, hp, :]), reads=[sc_b, sv_b], writes=[si_b])
                P.op("dve", lambda e, hp=hp: e.max_index(out=si[:, hp, 8:16], in_max=sv[:, hp, 8:16], in_values=sc2[:, hp, :]), reads=[sc2_b, sv_b], writes=[si_b])
            P.op("dve", lambda e: e.tensor_copy(out=sif[:], in_=si[:]), reads=[si_b], writes=[sif_b])
            P.op("dve", lambda e: e.tensor_scalar(out=si1x[:], in0=sif[:].rearrange("p (h t) k -> p h t k", t=2)[:, :, 0, :], scalar1=128.0, scalar2=None, op0=ALU.mult),
                 reads=[sif_b], writes=[si1x_b])
            for h in range(8):
                for a in range(16):
                    P.op("dve", lambda e, h=h, a=a: e.tensor_scalar(out=cand[:, h, a * 16:(a + 1) * 16], in0=sv[:, 2 * h + 1, :], scalar1=sv[:, 2 * h, a:a + 1], scalar2=None, op0=ALU.add),
                         reads=[sv_b], writes=[cand_b])
                    P.op("pool", lambda e, h=h, a=a: e.tensor_scalar(out=candi[:, h, a * 16:(a + 1) * 16], in0=sif[:, 2 * h + 1, :], scalar1=si1x[:, h, a:a + 1], scalar2=None, op0=ALU.add),
                         reads=[sif_b, si1x_b], writes=[candi_b])
            for h in range(8):
                P.op("dve", lambda e, h=h: e.max(out=tv[:, h, 0:8], in_=cand[:, h, :]), reads=[cand_b], writes=[tv_b])
                P.op("dve", lambda e, h=h: e.match_replace(out=cand2[:, h, :], in_to_replace=tv[:, h, 0:8], in_values=cand[:, h, :], imm_value=-1e30),
                     reads=[cand_b, tv_b], writes=[cand2_b])
                P.op("dve", lambda e, h=h: e.max(out=tv[:, h, 8:16], in_=cand2[:, h, :]), reads=[cand2_b], writes=[tv_b])
                P.op("dve", lambda e, h=h: e.max_index(out=tj[:, h, 0:8], in_max=tv[:, h, 0:8], in_values=cand[:, h, :]), reads=[cand_b, tv_b], writes=[tj_b])
                P.op("dve", lambda e, h=h: e.max_index(out=tj[:, h, 8:16], in_max=tv[:, h, 8:16], in_values=cand2[:, h, :]), reads=[cand2_b, tv_b], writes=[tj_b])
            P.op("dve", lambda e: e.tensor_copy(out=tjf[:], in_=tj[:]), reads=[tj_b], writes=[tjf_b])
            for h in range(8):
                for k in range(16):
                    P.op("dve", lambda e, h=h, k=k: e.scalar_tensor_tensor(out=junk[:, 0:256], in0=iott[:, :], scalar=tjf[:, h, k:k + 1], in1=candi[:, h, :],
                                                                       op0=ALU.is_equal, op1=ALU.mult, accum_out=idxf[:, h * 16 + k:h * 16 + k + 1]),
                         reads=[iot_b, tjf_b, candi_b], writes=[junk_b, idxf_b])
            P.op("dve", lambda e: e.tensor_copy(out=idxu[:], in_=idxf[:]), reads=[idxf_b], writes=[idxu_b])
            P.op("dve", lambda e: e.tensor_scalar(out=sm[:, 0:8], in0=tv[:, :, 0], scalar1=-1.0, scalar2=None, op0=ALU.mult), reads=[tv_b], writes=[sm_b])
            for h in range(8):
                P.op("act", lambda e, h=h: e.activation(out=ev[:, h, :], in_=tv[:, h, :], func=AF.Exp, bias=sm[:, h:h + 1]), reads=[tv_b, sm_b], writes=[ev_b])
            P.op("dve", lambda e: e.tensor_reduce(out=sm[:, 8:16], in_=ev[:], axis=AX.X, op=ALU.add), reads=[ev_b], writes=[sm_b])
            P.op("dve", lambda e: e.reciprocal(out=sm[:, 16:24], in_=sm[:, 8:16]), reads=[sm_b], writes=[sm_b])
            for h in range(8):
                P.op("dve", lambda e, h=h: e.tensor_scalar(out=gg[:, h * 16:(h + 1) * 16], in0=ev[:, h, :], scalar1=sm[:, 16 + h:17 + h], scalar2=None, op0=ALU.mult),
                     reads=[ev_b, sm_b], writes=[gg_b])
            for slot in range(128):
                ub, ubb = gather(u, slot)
                P.op("dve", lambda e, ub=ub, slot=slot: e.scalar_tensor_tensor(out=junk[:], in0=ub[:], scalar=1.0, in1=h2[:], op0=ALU.mult, op1=ALU.mult,
                                                                            accum_out=hu[:, slot:slot + 1]), reads=[ubb, h2_b], writes=[junk_b, hu_b])
            P.op("dve", lambda e: e.tensor_tensor(out=tg[:], in0=hu[:], in1=hu[:], op=ALU.mult), reads=[hu_b], writes=[tg_b])
            P.op("dve", lambda e: e.tensor_scalar(out=tg[:], in0=tg[:], scalar1=0.044715, scalar2=1.0, op0=ALU.mult, op1=ALU.add), reads=[tg_b], writes=[tg_b])
            P.op("dve", lambda e: e.tensor_tensor(out=tg[:], in0=tg[:], in1=hu[:], op=ALU.mult), reads=[tg_b, hu_b], writes=[tg_b])
            P.op("act", lambda e: e.activation(out=tg[:], in_=tg[:], func=AF.Sigmoid, scale=GELU_C), reads=[tg_b], writes=[tg_b])
            P.op("dve", lambda e: e.tensor_tensor(out=ww[:], in0=hu[:], in1=gg[:], op=ALU.mult), reads=[hu_b, gg_b], writes=[ww_b])
            P.op("dve", lambda e: e.tensor_tensor(out=ww[:], in0=ww[:], in1=tg[:], op=ALU.mult), reads=[ww_b, tg_b], writes=[ww_b])
            for slot in range(128):
                vb, vbb = gather(v, slot)
                if slot == 0:
                    P.op("dve", lambda e, vb=vb: e.tensor_scalar(out=acc[:], in0=vb[:], scalar1=ww[:, 0:1], scalar2=None, op0=ALU.mult), reads=[vbb, ww_b], writes=[acc_b])
                else:
                    P.op("dve", lambda e, vb=vb, slot=slot: e.scalar_tensor_tensor(out=acc[:], in0=vb[:], scalar=ww[:, slot:slot + 1], in1=acc[:], op0=ALU.mult, op1=ALU.add),
                         reads=[vbb, ww_b, acc_b], writes=[acc_b])
            P.op("dve", lambda e: e.tensor_tensor(out=tmp[:], in0=acc[:], in1=mbt[:, 2, :], op=ALU.mult), reads=[acc_b, mb_b], writes=[tmp_b])
            P.op("dve", lambda e: e.scalar_tensor_tensor(out=y[:], in0=xt[:], scalar=ALPHA, in1=tmp[:], op0=ALU.mult, op1=ALU.add), reads=[xt_b, tmp_b], writes=[y_b])
            o, ob_ = outr.next()
            _layernorm_tile(P, y, y_b, o, ob_, lngt, lnbt, ln_b, stt, mv, smb2, tmp, tmp_b)
            P.dma("sp", x2[tsl, :], o[:], reads=[ob_], writes=[x2_b])
        P.finish([x2_b])
        P.emit(st)
    return nc


def build_ada():
    from contextlib import ExitStack
    nc = bass.Bass("TRN2", target_bir_lowering=False)
    wA = nc.dram_tensor("wA", [1024, 3072], F32, kind="ExternalInput").ap()
    cT = nc.dram_tensor("cT", [128, 8, 2], F32, kind="ExternalInput").ap()
    bA = nc.dram_tensor("bA", [128, 24], F32, kind="ExternalInput").ap()
    mo = nc.dram_tensor("mo", [128, 24, 2], F32, kind="ExternalOutput").ap()
    with ExitStack() as st:
        P = Prog(nc)
        w = _sb(st, nc, "w", [128, 8, 3072], F32); w_b = Buf()
        ct = _sb(st, nc, "ct", [128, 8, 2], F32); c_b = Buf()
        ca = _sb(st, nc, "ca", [128, 8, 2], F32); ca_b = Buf()
        bt = _sb(st, nc, "bt", [128, 24], F32); b_b = Buf()
        res = _sb(st, nc, "res", [128, 24, 2], F32); r_b = Buf()
        banks = _psum_banks(st, nc, 1)
        ps, pb = banks[0]
        mo_b = Buf()
        for k in range(8):
            P.dma("sp", w[:, k, :], wA[k * 128:(k + 1) * 128, :], writes=[w_b])
        P.dma("sp", ct[:], cT[:, :, :], writes=[c_b])
        P.dma("sp", bt[:], bA[:, :], writes=[b_b])
        P.op("act", lambda e: e.activation(out=ca[:], in_=ct[:], func=AF.Silu), reads=[c_b], writes=[ca_b])
        for j in range(24):
            for k in range(8):
                P.op("pe", lambda e, j=j, k=k: e.matmul(ps[:, 2 * j:2 * j + 2], w[:, k, j * 128:(j + 1) * 128], ca[:, k, :],
                                                       start=(k == 0), stop=(k == 7), skip_group_check=True), reads=[w_b, ca_b], writes=[pb])
        for b in range(2):
            P.op("dve", lambda e, b=b: e.tensor_tensor(out=res[:, :, b], in0=ps[:, 0:48].rearrange("p (j b) -> p j b", b=2)[:, :, b], in1=bt[:, :], op=ALU.add),
                 reads=[pb, b_b], writes=[r_b])
        P.dma("sp", mo[:, :, :], res[:], reads=[r_b], writes=[mo_b])
        P.finish([mo_b])
        P.emit(st)
    return nc


_PROGS = {}
_DBG = None


def _prog(name, fn):
    if name not in _PROGS:
        _PROGS[name] = fn()
    return _PROGS[name]


def _run(name, fn, in_maps):
    nc = _prog(name, fn)
    n = len(in_maps)
    in_maps = [{k: np.ascontiguousarray(v, dtype=np.float32) for k, v in m.items()} for m in in_maps]
    res = run_bass_kernel_spmd(nc, in_maps, core_ids=list(range(n)))
    return res.results


def _rep(vec):
    return np.ascontiguousarray(np.tile(np.asarray(vec, np.float32)[None], (128, 1)))


def _chunk128(vec):
    return np.ascontiguousarray(np.asarray(vec, np.float32).reshape(-1, 128).T)


def _swap_heads(w):
    k, n = w.shape
    w4 = w.reshape(k, n // 64, 2, 32)
    return np.ascontiguousarray(w4[:, :, ::-1, :]).reshape(k, n)


def _rope_tables(s0, n):
    inv = (1.0 / (10000.0 ** (np.arange(0, 64, 2, dtype=np.float32) / 64.0))).astype(np.float32)
    ang = (np.arange(s0, s0 + n, dtype=np.float32)[:, None] * inv[None, :]).astype(np.float32)
    cos = np.cos(ang).astype(np.float32).T
    sin = np.sin(ang).astype(np.float32).T
    out = np.zeros((128, 2, n), np.float32)
    for p in range(128):
        f = p % 32
        out[p, 0] = cos[f]
        out[p, 1] = -sin[f] if (p % 64) < 32 else sin[f]
    return out


def kernel(x, c, w_ada, b_ada, w_in, b_gate, a_sink, b_rpb, c_lambda, c_norm_g,
           d_conv_w, d_conv_b, d_wa, d_ba, d_wx, d_bx, d_lam, w_branch, w_out,
           ln_g, ln_b, p_wq, p_subkeys, p_u, p_v):
    f32 = np.float32
    x = np.asarray(x, f32); c = np.asarray(c, f32)
    B, S, D = BATCH, SEQ, D_MODEL
    cT = np.ascontiguousarray(c.reshape(2, 8, 128).transpose(2, 1, 0))
    maps = []
    for core in range(8):
        l, half = core // 2, core % 2
        maps.append(dict(wA=np.asarray(w_ada[l])[:, half * 3072:(half + 1) * 3072], cT=cT,
                         bA=_chunk128(np.asarray(b_ada[l])[half * 3072:(half + 1) * 3072])))
    r = _run("ada", build_ada, maps)
    mod = np.zeros((DEPTH, 2, 6144), f32)
    for core in range(8):
        l, half = core // 2, core % 2
        mo = r[core]["mo"]
        mod[l, :, half * 3072:(half + 1) * 3072] = mo.transpose(2, 1, 0).reshape(2, 3072)
    if _DBG is not None:
        _DBG["mod"] = mod
    eye = np.eye(128, dtype=f32)
    amask = a_mask_np()
    cs_tabs = [_rope_tables(j * TPC, TPC) for j in range(4)]
    xc = x.copy()
    l0 = 0
    if _DBG is not None and "start" in _DBG:
        l0, xc = _DBG["start"]
    for l in range(l0, DEPTH):
        wl = np.asarray(w_in[l], f32)
        shift1, scale1, gate1, shift2, scale2, gate2 = [mod[l][:, i * 1024:(i + 1) * 1024] for i in range(6)]
        aq, ak, av = wl[:, 0:512], wl[:, 512:640], wl[:, 640:768]
        bq, bk, bv = wl[:, 768:1280], wl[:, 1280:1792], wl[:, 1792:2304]
        cq, ck, cv = wl[:, 2304:2816], wl[:, 2816:3328], wl[:, 3328:3840]
        dx, dg, gl = wl[:, 3840:4352], wl[:, 4352:4864], wl[:, 4864:8960]
        pairs = []
        for (wm, n) in ((aq, 4), (ak, 1), (cq, 4), (ck, 4)):
            ws = _swap_heads(wm)
            for i in range(n):
                pairs.append(wm[:, i * 128:(i + 1) * 128]); pairs.append(ws[:, i * 128:(i + 1) * 128])
        plains = []
        for wm in (bq, bk, dx, dg):
            for i in range(4):
                plains.append(wm[:, i * 128:(i + 1) * 128])
        wf = np.ascontiguousarray(np.concatenate(pairs + plains, axis=1))
        wt = np.ascontiguousarray(np.concatenate([av, bv, cv], axis=1))
        mod16 = [np.concatenate([_chunk128(scale1[b]), _chunk128(shift1[b])], axis=1) for b in range(2)]
        maps = []
        for core in range(8):
            b, j = core // 4, core % 4
            maps.append(dict(xT=xc[b, j * TPC:(j + 1) * TPC, :].T, mod=mod16[b], wf=wf, wt=wt, cs=cs_tabs[j]))
        r = _run("l1", build_l1, maps)
        F = [np.concatenate([r[b * 4 + j]["oF"] for j in range(4)], axis=2) for b in range(2)]
        TM = [np.concatenate([r[b * 4 + j]["oT"] for j in range(4)], axis=0) for b in range(2)]
        BR = [np.zeros((S, 2048), f32) for _ in range(2)]
        if _DBG is not None:
            _DBG["F%d" % l] = F; _DBG["TM%d" % l] = TM; _DBG["BR%d" % l] = BR
        maps = []
        for core in range(8):
            b, j = core // 4, core % 4
            kv = j // 2
            k1 = F[b][4][kv * 64:(kv + 1) * 64]
            maps.append(dict(qT=F[b][j], kT=np.concatenate([k1, k1], 0), v=TM[b][:, kv * 64:(kv + 1) * 64], msk=amask, idn=eye,
                             snk=_rep(np.asarray(a_sink[l], f32)[2 * j:2 * j + 2])))
        r = _run("a", build_a, maps)
        for core in range(8):
            b, j = core // 4, core % 4
            BR[b][:, j * 128:(j + 1) * 128] = r[core]["oa"]
        maps = []
        for core in range(8):
            b, j = core // 4, core % 4
            vv = TM[b][:, 128 + j * 128:128 + (j + 1) * 128]
            vsh = np.zeros_like(vv); vsh[:-64] = vv[64:]
            maps.append(dict(qT=F[b][13 + j], kT=F[b][17 + j], v=vv, vsh=vsh, bias=b_bias_np(np.asarray(b_rpb[l], f32)[2 * j:2 * j + 2]), idn=8.0 * eye))
        r = _run("b", build_b, maps)
        for core in range(8):
            b, j = core // 4, core % 4
            BR[b][:, 512 + j * 128:512 + (j + 1) * 128] = r[core]["ob"]
        lam_init = 0.8 - 0.6 * math.exp(-0.3 * l)
        maps = []
        for core in range(8):
            b, j = core // 4, core % 4
            maps.append(dict(qT=F[b][5 + j], kT=F[b][9 + j], v=TM[b][:, 640 + j * 128:640 + (j + 1) * 128],
                             lamb=np.tile(np.asarray(c_lambda[l], f32)[None], (128, 1, 1)),
                             cst=_rep(np.array([lam_init, 1.0 - lam_init], f32)), ng=_rep(np.asarray(c_norm_g[l], f32)[j * 128:(j + 1) * 128])))
        r = _run("c", build_c, maps)
        for core in range(8):
            b, j = core // 4, core % 4
            BR[b][:, 1024 + j * 128:1024 + (j + 1) * 128] = r[core]["oc"]
        maps = []
        for core in range(8):
            b, j = core // 4, core % 4
            wbd = np.zeros((128, 4, 128), f32)
            for d in range(2):
                for g in range(2):
                    wbd[g * 64:(g + 1) * 64, 2 * d, g * 64:(g + 1) * 64] = np.asarray(d_wa[l], f32)[d, 2 * j + g]
                    wbd[g * 64:(g + 1) * 64, 2 * d + 1, g * 64:(g + 1) * 64] = np.asarray(d_wx[l], f32)[d, 2 * j + g]
            ch = slice(j * 128, (j + 1) * 128)
            par = np.zeros((128, 12), f32)
            par[:, 0:4] = np.asarray(d_conv_w[l], f32)[:, ch].T
            par[:, 4] = np.asarray(d_conv_b[l], f32)[ch]
            par[:, 5] = np.asarray(d_ba[l], f32)[0, ch]; par[:, 6] = np.asarray(d_bx[l], f32)[0, ch]
            par[:, 7] = np.asarray(d_ba[l], f32)[1, ch]; par[:, 8] = np.asarray(d_bx[l], f32)[1, ch]
            par[:, 9] = np.asarray(d_lam[l], f32)[0, ch]; par[:, 10] = np.asarray(d_lam[l], f32)[1, ch]
            maps.append(dict(dxT=F[b][21 + j], dgT=F[b][25 + j], wbd=wbd, par=par))
        r = _run("d", build_d, maps)
        for core in range(8):
            b, j = core // 4, core % 4
            BR[b][:, 1536 + j * 128:1536 + (j + 1) * 128] = r[core]["odT"].T
        bgc = np.ascontiguousarray(np.asarray(b_gate[l], f32).reshape(4, 8, 128).transpose(2, 0, 1).reshape(128, 32))
        maps = []
        for core in range(8):
            b, j = core // 4, core % 4
            ts = slice(j * TPC, (j + 1) * TPC)
            maps.append(dict(xT=xc[b, ts, :].T, x=xc[b, ts, :], brT=BR[b][ts, :].T, mod=mod16[b], g1=_rep(gate1[b]), wg=gl, bg=bgc,
                             wbr=np.asarray(w_branch[l], f32).reshape(2048, 1024), wo=np.asarray(w_out[l], f32),
                             lng=_rep(np.asarray(ln_g[l], f32)[0]), lnb=_rep(np.asarray(ln_b[l], f32)[0])))
        r = _run("m", build_m, maps)
        x1 = np.stack([np.concatenate([r[b * 4 + j]["x1"] for j in range(4)], axis=0) for b in range(2)])
        if _DBG is not None:
            _DBG["x1_%d" % l] = x1
            if _DBG.get("stop_after_merge") == l:
                return x1
        skT = np.ascontiguousarray(np.asarray(p_subkeys[l], f32).reshape(16, 128, 128).transpose(2, 0, 1))
        maps = []
        PH = SEQ // 2
        for pc in range(4):
            b, hf = pc // 2, pc % 2
            xs_ = x1[b][hf * PH:(hf + 1) * PH]
            maps.append(dict(x1=xs_, x1T=xs_.T, mod=np.concatenate([_chunk128(scale2[b]), _chunk128(shift2[b])], axis=1),
                             modb=np.stack([_rep(scale2[b]), _rep(shift2[b]), _rep(gate2[b])], 1), wq=np.asarray(p_wq[l], f32), skT=skT,
                             u=np.asarray(p_u[l], f32), v=np.asarray(p_v[l], f32),
                             lng=_rep(np.asarray(ln_g[l], f32)[1]), lnb=_rep(np.asarray(ln_b[l], f32)[1]),
                             iot=_rep(np.arange(256, dtype=f32))))
        r = _run("p", lambda: build_p(PH // 128, PH), maps)
        xc = np.stack([np.concatenate([r[b * 2]["x2"], r[b * 2 + 1]["x2"]], axis=0) for b in range(2)])
        if _DBG is not None:
            _DBG["x2_%d" % l] = xc
            if _DBG.get("stop_after_layer") == l:
                return xc
    return xc.astype(np.float32)
```

```python
import math
import numpy as np
import concourse.bass as bass
import concourse.mybir as mybir
from concourse.bass_utils import run_bass_kernel_spmd

F32 = mybir.dt.float32
BF16 = mybir.dt.bfloat16
I32 = mybir.dt.int32
U32 = mybir.dt.uint32
AF = mybir.ActivationFunctionType
ALU = mybir.AluOpType
AX = mybir.AxisListType

D_MODEL = 1024
BATCH = 2
SEQ = 8192
DEPTH = 4
NCORES = 8
TPC = BATCH * SEQ // NCORES
ALPHA = (2.0 * DEPTH) ** 0.25
LN_EPS = 1e-5
NEG = -30000.0


class Buf:
    __slots__ = ("w", "r", "name")

    def __init__(self, name=""):
        self.w = None
        self.r = {}
        self.name = name


class Prog:
    ENGS = ("pe", "dve", "act", "pool", "sp")

    def __init__(self, nc):
        self.nc = nc
        self.ops = {e: [] for e in self.ENGS}
        self.cnt = {}
        self.waited = {e: {} for e in self.ENGS}
        self.dma_rr = {"sp": 0, "pool": 0, "act": 0}
        self.ndsem = 12

    def _waits(self, eng, deps):
        for (key, val) in deps:
            if key == "c_pe" and eng == "pe":
                continue
            if self.waited[eng].get(key, 0) >= val:
                continue
            self.waited[eng][key] = val
            self.ops[eng].append(("wait", key, val))

    def _deps(self, reads, writes):
        deps = {}
        def add(m):
            if m is None:
                return
            k, v = m
            if deps.get(k, 0) < v:
                deps[k] = v
        for b in reads:
            add(b.w)
        for b in writes:
            add(b.w)
            for k, v in b.r.items():
                add((k, v))
        return list(deps.items())

    def _mark(self, marker, reads, writes):
        k, v = marker
        for b in reads:
            if b.r.get(k, 0) < v:
                b.r[k] = v
        for b in writes:
            b.w = marker
            b.r = {}

    def op(self, eng, fn, reads=(), writes=()):
        self._waits(eng, self._deps(reads, writes))
        key = "c_" + eng
        self.cnt[key] = self.cnt.get(key, 0) + 1
        self.ops[eng].append(("op", fn, key, 1))
        self._mark((key, self.cnt[key]), reads, writes)

    def dma(self, q, out, in_, reads=(), writes=(), **kw):
        self._waits(q, self._deps(reads, writes))
        i = self.dma_rr[q]
        self.dma_rr[q] = (i + 1) % self.ndsem
        key = "d_%s%d" % (q, i)
        self.cnt[key] = self.cnt.get(key, 0) + 16
        self.ops[q].append(("op", lambda e: e.dma_start(out=out, in_=in_, **kw), key, 16))
        self._mark((key, self.cnt[key]), reads, writes)

    def finish(self, bufs):
        self._waits("sp", self._deps(bufs, ()))

    def emit(self, stack):
        nc = self.nc
        sems = {}
        for key in sorted(self.cnt):
            sems[key] = stack.enter_context(nc.semaphore(key))
        block = stack.enter_context(nc.Block())
        def run(eng):
            def body(e):
                for it in self.ops[eng]:
                    if it[0] == "wait":
                        e.wait_ge(sems[it[1]], it[2])
                    else:
                        it[1](e).then_inc(sems[it[2]], it[3])
            return body
        block.tensor(run("pe"))
        block.vector(run("dve"))
        block.scalar(run("act"))
        block.gpsimd(run("pool"))
        block.sync(run("sp"))


class Ring:
    def __init__(self, items):
        self.items = items
        self.i = 0

    def next(self):
        it = self.items[self.i]
        self.i = (self.i + 1) % len(self.items)
        return it


def _sb(stack, nc, name, shape, dt):
    return stack.enter_context(nc.sbuf_tensor(name, shape, dt))


def _psum_banks(stack, nc, n=8):
    return [(stack.enter_context(nc.psum_tensor("psb%d" % i, [128, 512], F32)), Buf("ps%d" % i)) for i in range(n)]


L1_NPAIR = 13
L1_NPLAIN = 16
L1_NF = L1_NPAIR + L1_NPLAIN
L1_WF_CHUNKS = 2 * L1_NPAIR + L1_NPLAIN
L1_TM = 1152


def build_l1():
    from contextlib import ExitStack
    nc = bass.Bass("TRN2", target_bir_lowering=False)
    T = TPC
    xT = nc.dram_tensor("xT", [1024, T], F32, kind="ExternalInput").ap()
    mod = nc.dram_tensor("mod", [128, 16], F32, kind="ExternalInput").ap()
    wf = nc.dram_tensor("wf", [1024, L1_WF_CHUNKS * 128], F32, kind="ExternalInput").ap()
    wt = nc.dram_tensor("wt", [1024, L1_TM], F32, kind="ExternalInput").ap()
    cs = nc.dram_tensor("cs", [128, 2, T], F32, kind="ExternalInput").ap()
    oF = nc.dram_tensor("oF", [L1_NF, 128, T], F32, kind="ExternalOutput").ap()
    oT = nc.dram_tensor("oT", [T, L1_TM], F32, kind="ExternalOutput").ap()
    with ExitStack() as st:
        P = Prog(nc)
        hT = _sb(st, nc, "hT", [128, 8, T], BF16); hT_b = [Buf() for _ in range(8)]
        xs = Ring([(_sb(st, nc, "xs%d" % i, [128, T], F32), Buf()) for i in range(2)])
        modt = _sb(st, nc, "modt", [128, 16], F32); mod_b = Buf()
        sc1p = _sb(st, nc, "sc1p", [128, 8], F32); sc1p_b = Buf()
        cst = _sb(st, nc, "cst", [128, 2, T], F32); cs_b = Buf()
        wtb = _sb(st, nc, "wtb", [128, 8, L1_TM], BF16); wt_b = Buf()
        wr = Ring([(_sb(st, nc, "wb%d" % i, [128, 8, 128], BF16), Buf()) for i in range(6)])
        stF = Ring([(_sb(st, nc, "stF%d" % i, [128, T], F32), Buf()) for i in range(3)])
        stT = Ring([(_sb(st, nc, "stT%d" % i, [128, L1_TM], F32), Buf()) for i in range(2)])
        t1r = Ring([(_sb(st, nc, "t1_%d" % i, [128, 512], F32), Buf()) for i in range(2)])
        t2r = Ring([(_sb(st, nc, "t2_%d" % i, [128, 512], F32), Buf()) for i in range(2)])
        banks = Ring(_psum_banks(st, nc))
        oF_b = Buf(); oT_b = Buf()

        P.dma("sp", modt[:], mod[:, :], writes=[mod_b])
        P.dma("sp", cst[:], cs[:, :, :], writes=[cs_b])
        P.op("dve", lambda e: e.tensor_scalar(out=sc1p[:], in0=modt[:, 0:8], scalar1=1.0, scalar2=None, op0=ALU.add),
             reads=[mod_b], writes=[sc1p_b])
        P.dma("pool", wtb[:], wt.rearrange("(k p) c -> p k c", p=128), writes=[wt_b])
        for k in range(8):
            xa, xb_ = xs.next()
            P.dma("sp", xa[:], xT[k * 128:(k + 1) * 128, :], writes=[xb_])
            P.op("act", lambda e, xa=xa, k=k: e.activation(out=hT[:, k, :], in_=xa[:], func=AF.Identity,
                                                          scale=sc1p[:, k:k + 1], bias=modt[:, 8 + k:9 + k]),
                 reads=[xb_, sc1p_b, mod_b], writes=[hT_b[k]])

        def load_w(c):
            wa, wb_ = wr.next()
            P.dma("pool", wa[:], wf[:, c * 128:(c + 1) * 128].rearrange("(k p) c -> p k c", p=128), writes=[wb_])
            return wa, wb_

        def mm_feat(wa, wb_, tb):
            ps, pb = banks.next()
            for k in range(8):
                P.op("pe", lambda e, ps=ps, wa=wa, k=k, tb=tb: e.matmul(
                    ps[:, :], wa[:, k, :], hT[:, k, tb * 512:(tb + 1) * 512], start=(k == 0), stop=(k == 7)),
                    reads=[wb_, hT_b[k]], writes=[pb])
            return ps, pb

        nevac = 0
        for u in range(L1_NF):
            sa, sb_ = stF.next()
            if u < L1_NPAIR:
                wA = load_w(2 * u); wB = load_w(2 * u + 1)
                for tb in range(4):
                    pA, pAb = mm_feat(wA[0], wA[1], tb)
                    pB, pBb = mm_feat(wB[0], wB[1], tb)
                    t1, t1b = t1r.next(); t2, t2b = t2r.next()
                    sl = slice(tb * 512, (tb + 1) * 512)
                    P.op("dve", lambda e, t1=t1, pA=pA, sl=sl: e.tensor_tensor(out=t1[:], in0=pA[:, :], in1=cst[:, 0, sl], op=ALU.mult),
                         reads=[pAb, cs_b], writes=[t1b])
                    P.op("dve", lambda e, t2=t2, pB=pB, sl=sl: e.tensor_tensor(out=t2[:], in0=pB[:, :], in1=cst[:, 1, sl], op=ALU.mult),
                         reads=[pBb, cs_b], writes=[t2b])
                    P.op("pool", lambda e, sa=sa, t1=t1, t2=t2, sl=sl: e.tensor_tensor(out=sa[:, sl], in0=t1[:], in1=t2[:], op=ALU.add),
                         reads=[t1b, t2b], writes=[sb_])
            else:
                wA = load_w(2 * L1_NPAIR + (u - L1_NPAIR))
                for tb in range(4):
                    pA, pAb = mm_feat(wA[0], wA[1], tb)
                    sl = slice(tb * 512, (tb + 1) * 512)
                    if nevac % 2 == 0:
                        P.op("act", lambda e, sa=sa, pA=pA, sl=sl: e.copy(out=sa[:, sl], in_=pA[:, :]), reads=[pAb], writes=[sb_])
                    else:
                        P.op("dve", lambda e, sa=sa, pA=pA, sl=sl: e.tensor_copy(out=sa[:, sl], in_=pA[:, :]), reads=[pAb], writes=[sb_])
                    nevac += 1
            P.dma("sp", oF[u, :, :], sa[:], reads=[sb_], writes=[oF_b])

        groups = [(0, 128), (128, 512), (640, 512)]
        for i in range(T // 128):
            sa, sb_ = stT.next()
            for (c0, n) in groups:
                ps, pb = banks.next()
                for k in range(8):
                    P.op("pe", lambda e, ps=ps, k=k, i=i, c0=c0, n=n: e.matmul(
                        ps[:, 0:n], hT[:, k, i * 128:(i + 1) * 128], wtb[:, k, c0:c0 + n], start=(k == 0), stop=(k == 7)),
                        reads=[wt_b, hT_b[k]], writes=[pb])
                if nevac % 2 == 0:
                    P.op("act", lambda e, sa=sa, ps=ps, c0=c0, n=n: e.copy(out=sa[:, c0:c0 + n], in_=ps[:, 0:n]), reads=[pb], writes=[sb_])
                else:
                    P.op("dve", lambda e, sa=sa, ps=ps, c0=c0, n=n: e.tensor_copy(out=sa[:, c0:c0 + n], in_=ps[:, 0:n]), reads=[pb], writes=[sb_])
                nevac += 1
            P.dma("sp", oT[i * 128:(i + 1) * 128, :], sa[:], reads=[sb_], writes=[oT_b])
        P.finish([oF_b, oT_b])
        P.emit(st)
    return nc


def build_c():
    from contextlib import ExitStack
    nc = bass.Bass("TRN2", target_bir_lowering=False)
    S = SEQ
    qT = nc.dram_tensor("qT", [128, S], F32, kind="ExternalInput").ap()
    kT = nc.dram_tensor("kT", [128, S], F32, kind="ExternalInput").ap()
    v = nc.dram_tensor("v", [S, 128], F32, kind="ExternalInput").ap()
    lamb = nc.dram_tensor("lamb", [128, 4, 64], F32, kind="ExternalInput").ap()
    cst = nc.dram_tensor("cst", [128, 2], F32, kind="ExternalInput").ap()
    ng = nc.dram_tensor("ng", [128, 128], F32, kind="ExternalInput").ap()
    oc = nc.dram_tensor("oc", [S, 128], F32, kind="ExternalOutput").ap()
    with ExitStack() as st:
        P = Prog(nc)
        qTb = _sb(st, nc, "qTb", [128, S], BF16); q_b = Buf()
        kTb = _sb(st, nc, "kTb", [128, S], BF16); k_b = Buf()
        va = _sb(st, nc, "va", [128, S // 128, 130], BF16); v_b = Buf()
        lt = _sb(st, nc, "lt", [128, 4, 64], F32); l_b = Buf()
        ct = _sb(st, nc, "ct", [128, 2], F32); c_b = Buf()
        ngt = _sb(st, nc, "ngt", [128, 128], F32); ng_b = Buf()
        gl = _sb(st, nc, "gl", [128, 128], F32); gl_b = Buf()
        junk = _sb(st, nc, "junk", [128, 128], F32); junk_b = Buf()
        sm = _sb(st, nc, "sm", [128, 8], F32); sm_b = Buf()
        nlam = _sb(st, nc, "nlam", [128, 1], F32); nlam_b = Buf()
        Er = Ring([(_sb(st, nc, "E%d" % i, [128, 512], BF16), Buf()) for i in range(4)])
        stg = Ring([(_sb(st, nc, "stg%d" % i, [128, 4, 128], F32), Buf()) for i in range(2)])
        tr = Ring([(_sb(st, nc, "tt%d" % i, [128, 128], F32), Buf()) for i in range(2)])
        orr = Ring([(_sb(st, nc, "oo%d" % i, [128, 128], F32), Buf()) for i in range(2)])
        sr = Ring([(_sb(st, nc, "ss%d" % i, [128, 8], F32), Buf()) for i in range(2)])
        banks = _psum_banks(st, nc)
        accb = banks[:3]
        stb = Ring(banks[3:])
        oc_b = Buf()

        P.dma("pool", qTb[:], qT[:, :], writes=[q_b])
        P.dma("pool", kTb[:], kT[:, :], writes=[k_b])
        P.dma("pool", va[:, :, 0:128], v.rearrange("(n p) d -> p n d", p=128), writes=[v_b])
        P.op("dve", lambda e: e.memset(va[:, :, 128:130], 1.0), writes=[v_b])
        P.dma("sp", lt[:], lamb[:, :, :], writes=[l_b])
        P.dma("sp", ct[:], cst[:, :], writes=[c_b])
        P.dma("sp", ngt[:], ng[:, :], writes=[ng_b])
        P.op("dve", lambda e: e.tensor_tensor(out=junk[:, 0:64], in0=lt[:, 0, :], in1=lt[:, 1, :], op=ALU.mult), reads=[l_b], writes=[junk_b])
        P.op("dve", lambda e: e.reduce_sum(out=sm[:, 0:1], in_=junk[:, 0:64], axis=AX.X), reads=[junk_b], writes=[sm_b])
        P.op("dve", lambda e: e.tensor_tensor(out=junk[:, 64:128], in0=lt[:, 2, :], in1=lt[:, 3, :], op=ALU.mult), reads=[l_b], writes=[junk_b])
        P.op("dve", lambda e: e.reduce_sum(out=sm[:, 1:2], in_=junk[:, 64:128], axis=AX.X), reads=[junk_b], writes=[sm_b])
        P.op("act", lambda e: e.activation(out=sm[:, 2:4], in_=sm[:, 0:2], func=AF.Exp), reads=[sm_b], writes=[sm_b])
        P.op("dve", lambda e: e.tensor_tensor(out=sm[:, 4:5], in0=sm[:, 3:4], in1=sm[:, 2:3], op=ALU.subtract), reads=[sm_b], writes=[sm_b])
        P.op("dve", lambda e: e.tensor_tensor(out=nlam[:], in0=sm[:, 4:5], in1=ct[:, 0:1], op=ALU.subtract), reads=[sm_b, c_b], writes=[nlam_b])
        P.op("dve", lambda e: e.tensor_scalar(out=gl[:], in0=ngt[:], scalar1=ct[:, 1:2], scalar2=None, op0=ALU.mult), reads=[ng_b, c_b], writes=[gl_b])

        def acc_ap(m, qb):
            a = m * 4 + qb
            return accb[a // 3][0], accb[a // 3][1], (a % 3) * 130

        NG = S // 512
        NK = S // 128
        for g in range(NG):
            started = set()
            its = [(kc, m) for kc in range(NK) for m in range(2)]
            pend = []

            def issue_st(kc, m, g=g):
                ps, pb = stb.next()
                P.op("pe", lambda e, ps=ps, m=m, kc=kc, g=g: e.matmul(
                    ps[:, :], kTb[m * 64:(m + 1) * 64, kc * 128:(kc + 1) * 128], qTb[m * 64:(m + 1) * 64, g * 512:(g + 1) * 512],
                    start=True, stop=True), reads=[k_b, q_b], writes=[pb])
                E, Eb = Er.next()
                P.op("act", lambda e, E=E, ps=ps: e.activation(out=E[:], in_=ps[:, :], func=AF.Exp, scale=0.125), reads=[pb], writes=[Eb])
                pend.append((kc, m, E, Eb))

            def issue_pv():
                kc, m, E, Eb = pend.pop(0)
                for qb in range(4):
                    acc, ab, off = acc_ap(m, qb)
                    bi = (m * 4 + qb) // 3
                    first = bi not in started
                    started.add(bi)
                    P.op("pe", lambda e, acc=acc, off=off, E=E, qb=qb, kc=kc, first=first: e.matmul(
                        acc[:, off:off + 129], E[:, qb * 128:(qb + 1) * 128], va[:, kc, 0:129],
                        start=first, stop=(kc == NK - 1), skip_group_check=True), reads=[Eb, v_b], writes=[ab])

            DEPTH_PF = 3
            for n_, (kc, m) in enumerate(its):
                issue_st(kc, m)
                if len(pend) > DEPTH_PF - 1:
                    issue_pv()
            while pend:
                issue_pv()
            sa, sb_ = stg.next()
            for qb in range(4):
                a0, a0b, o0 = acc_ap(0, qb)
                a1, a1b, o1 = acc_ap(1, qb)
                ss, ssb = sr.next(); tt, ttb = tr.next(); oo, oob = orr.next()
                P.op("dve", lambda e, ss=ss, a0=a0, o0=o0: e.reciprocal(out=ss[:, 0:1], in_=a0[:, o0 + 128:o0 + 129]), reads=[a0b], writes=[ssb])
                P.op("dve", lambda e, ss=ss, a1=a1, o1=o1: e.reciprocal(out=ss[:, 1:2], in_=a1[:, o1 + 128:o1 + 129]), reads=[a1b], writes=[ssb])
                P.op("dve", lambda e, ss=ss: e.tensor_tensor(out=ss[:, 2:3], in0=ss[:, 1:2], in1=nlam[:], op=ALU.mult), reads=[ssb, nlam_b], writes=[ssb])
                P.op("dve", lambda e, tt=tt, a1=a1, o1=o1, ss=ss: e.tensor_scalar(out=tt[:], in0=a1[:, o1:o1 + 128], scalar1=ss[:, 2:3], scalar2=None, op0=ALU.mult),
                     reads=[a1b, ssb], writes=[ttb])
                P.op("dve", lambda e, oo=oo, a0=a0, o0=o0, ss=ss, tt=tt: e.scalar_tensor_tensor(
                    out=oo[:], in0=a0[:, o0:o0 + 128], scalar=ss[:, 0:1], in1=tt[:], op0=ALU.mult, op1=ALU.add), reads=[a0b, ssb, ttb], writes=[oob])
                P.op("act", lambda e, oo=oo, ss=ss: e.activation(out=junk[:], in_=oo[:], func=AF.Square, accum_out=ss[:, 3:4]),
                     reads=[oob], writes=[junk_b, ssb])
                P.op("act", lambda e, ss=ss: e.activation(out=ss[:, 4:5], in_=ss[:, 3:4], func=AF.Sqrt, scale=1.0 / 128.0, bias=LN_EPS), reads=[ssb], writes=[ssb])
                P.op("dve", lambda e, ss=ss: e.reciprocal(out=ss[:, 5:6], in_=ss[:, 4:5]), reads=[ssb], writes=[ssb])
                P.op("dve", lambda e, sa=sa, qb=qb, oo=oo, ss=ss: e.scalar_tensor_tensor(
                    out=sa[:, qb, :], in0=oo[:], scalar=ss[:, 5:6], in1=gl[:], op0=ALU.mult, op1=ALU.mult), reads=[oob, ssb, gl_b], writes=[sb_])
            P.dma("sp", oc.rearrange("(n p) d -> p n d", p=128)[:, g * 4:(g + 1) * 4, :], sa[:], reads=[sb_], writes=[oc_b])
        P.finish([oc_b])
        P.emit(st)
    return nc


def build_a():
    from contextlib import ExitStack
    nc = bass.Bass("TRN2", target_bir_lowering=False)
    S = SEQ
    NB = S // 128
    qT = nc.dram_tensor("qT", [128, S], F32, kind="ExternalInput").ap()
    kT = nc.dram_tensor("kT", [128, S], F32, kind="ExternalInput").ap()
    v = nc.dram_tensor("v", [S, 64], F32, kind="ExternalInput").ap()
    msk = nc.dram_tensor("msk", [128, 384], F32, kind="ExternalInput").ap()
    idn = nc.dram_tensor("idn", [128, 128], F32, kind="ExternalInput").ap()
    snk = nc.dram_tensor("snk", [128, 2], F32, kind="ExternalInput").ap()
    oa = nc.dram_tensor("oa", [S, 128], F32, kind="ExternalOutput").ap()
    with ExitStack() as st:
        P = Prog(nc)
        qTb = _sb(st, nc, "qTb", [128, S], BF16); q_b = Buf()
        kTb = _sb(st, nc, "kTb", [128, S], BF16); k_b = Buf()
        va = _sb(st, nc, "va", [128, NB, 66], BF16); v_b = Buf()
        mb = _sb(st, nc, "mb", [128, 384], BF16); m_b = Buf()
        ib = _sb(st, nc, "ib", [128, 128], BF16); i_b = Buf()
        sk = _sb(st, nc, "sk", [128, 2], F32); sk_b = Buf()
        esk = _sb(st, nc, "esk", [128, 2], F32); esk_b = Buf()
        Er = Ring([(_sb(st, nc, "E%d" % i, [128, 384], BF16), Buf()) for i in range(4)])
        sr = Ring([(_sb(st, nc, "ss%d" % i, [128, 2], F32), Buf()) for i in range(4)])
        stg = _sb(st, nc, "stg", [128, NB, 128], F32); stg_b = Buf()
        banks = _psum_banks(st, nc)
        accb = banks[:4]
        stb = Ring(banks[4:])
        oa_b = Buf()
        P.dma("pool", qTb[:], qT[:, :], writes=[q_b])
        P.dma("pool", kTb[:], kT[:, :], writes=[k_b])
        P.dma("pool", va[:, :, 0:64], v.rearrange("(n p) d -> p n d", p=128), writes=[v_b])
        P.op("dve", lambda e: e.memset(va[:, :, 64:66], 1.0), writes=[v_b])
        P.dma("pool", mb[:], msk[:, :], writes=[m_b])
        P.dma("pool", ib[:], idn[:, :], writes=[i_b])
        P.dma("sp", sk[:], snk[:, :], writes=[sk_b])
        P.op("act", lambda e: e.activation(out=esk[:], in_=sk[:], func=AF.Exp), reads=[sk_b], writes=[esk_b])
        for hh in range(2):
            pr = slice(hh * 64, (hh + 1) * 64)
            for m in range(NB):
                lo = max(m - 1, 0); hi = min(m + 1, NB - 1)
                ncol = (hi - lo + 1) * 128
                off = (lo - (m - 1)) * 128
                ps, pb = stb.next()
                P.op("pe", lambda e, ps=ps, pr=pr, m=m, lo=lo, hi=hi, ncol=ncol: e.matmul(
                    ps[:, 0:ncol], kTb[pr, m * 128:(m + 1) * 128], qTb[pr, lo * 128:(hi + 1) * 128], start=True, stop=False),
                    reads=[k_b, q_b], writes=[pb])
                P.op("pe", lambda e, ps=ps, off=off, ncol=ncol: e.matmul(
                    ps[:, 0:ncol], ib[:, :], mb[:, off:off + ncol], start=False, stop=True), reads=[i_b, m_b], writes=[pb])
                E, Eb = Er.next()
                P.op("act", lambda e, E=E, ps=ps, ncol=ncol: e.activation(out=E[:, 0:ncol], in_=ps[:, 0:ncol], func=AF.Exp, scale=0.125),
                     reads=[pb], writes=[Eb])
                for qb in range(lo, hi + 1):
                    acc, ab = accb[qb % 4]
                    P.op("pe", lambda e, acc=acc, E=E, qb=qb, lo=lo, m=m: e.matmul(
                        acc[:, 0:65], E[:, (qb - lo) * 128:(qb - lo + 1) * 128], va[:, m, 0:65],
                        start=(m == max(qb - 1, 0)), stop=(m == min(qb + 1, NB - 1))), reads=[Eb, v_b], writes=[ab])
                done = [qb for qb in range(lo, hi + 1) if min(qb + 1, NB - 1) == m]
                for qb in done:
                    acc, ab = accb[qb % 4]
                    ss, ssb = sr.next()
                    P.op("dve", lambda e, ss=ss, acc=acc, hh=hh: e.tensor_tensor(out=ss[:, 0:1], in0=acc[:, 64:65], in1=esk[:, hh:hh + 1], op=ALU.add),
                         reads=[ab, esk_b], writes=[ssb])
                    P.op("dve", lambda e, ss=ss: e.reciprocal(out=ss[:, 1:2], in_=ss[:, 0:1]), reads=[ssb], writes=[ssb])
                    P.op("dve", lambda e, ss=ss, acc=acc, qb=qb, hh=hh: e.tensor_scalar(
                        out=stg[:, qb, hh * 64:(hh + 1) * 64], in0=acc[:, 0:64], scalar1=ss[:, 1:2], scalar2=None, op0=ALU.mult),
                        reads=[ab, ssb], writes=[stg_b])
        P.dma("sp", oa.rearrange("(n p) d -> p n d", p=128), stg[:], reads=[stg_b], writes=[oa_b])
        P.finish([oa_b])
        P.emit(st)
    return nc


def a_mask_np():
    kj = np.arange(128)[:, None]; qi = np.arange(128)[None, :]
    m = np.zeros((128, 384), np.float32)
    m[:, 0:128] = np.where(kj <= qi, 0.0, NEG)
    m[:, 256:384] = np.where(kj >= qi, 0.0, NEG)
    return m


def build_b():
    from contextlib import ExitStack
    nc = bass.Bass("TRN2", target_bir_lowering=False)
    S = SEQ
    NB = S // 128
    ROWS = S // 64
    qT = nc.dram_tensor("qT", [128, S], F32, kind="ExternalInput").ap()
    kT = nc.dram_tensor("kT", [128, S], F32, kind="ExternalInput").ap()
    v = nc.dram_tensor("v", [S, 128], F32, kind="ExternalInput").ap()
    vsh = nc.dram_tensor("vsh", [S, 128], F32, kind="ExternalInput").ap()
    bias = nc.dram_tensor("bias", [128, 2 * 8 * 256], F32, kind="ExternalInput").ap()
    idn = nc.dram_tensor("idn", [128, 128], F32, kind="ExternalInput").ap()
    ob = nc.dram_tensor("ob", [S, 128], F32, kind="ExternalOutput").ap()
    with ExitStack() as st:
        P = Prog(nc)
        qTb = _sb(st, nc, "qTb", [128, S], BF16); q_b = Buf()
        kTb = _sb(st, nc, "kTb", [128, S], BF16); k_b = Buf()
        va = [_sb(st, nc, "va%d" % i, [128, NB, 2, 66], BF16) for i in range(2)]; v_b = Buf()
        bb = _sb(st, nc, "bb", [128, 2, 8, 256], BF16); b_b = Buf()
        ib = _sb(st, nc, "ib", [128, 128], BF16); i_b = Buf()
        Er = Ring([(_sb(st, nc, "E%d" % i, [128, 256], BF16), Buf()) for i in range(4)])
        sr = Ring([(_sb(st, nc, "ss%d" % i, [64, 2], F32), Buf()) for i in range(4)])
        stg = _sb(st, nc, "stg", [64, ROWS, 128], F32); stg_b = Buf()
        banks = _psum_banks(st, nc)
        accb = Ring(banks[:4])
        stb = Ring(banks[4:])
        ob_b = Buf()
        P.dma("pool", qTb[:], qT[:, :], writes=[q_b])
        P.dma("pool", kTb[:], kT[:, :], writes=[k_b])
        for i, src in enumerate((v, vsh)):
            for h2 in range(2):
                P.dma("pool", va[i][:, :, h2, 0:64], src[:, h2 * 64:(h2 + 1) * 64].rearrange("(n p) d -> p n d", p=128), writes=[v_b])
            P.op("dve", lambda e, i=i: e.memset(va[i][:, :, :, 64:66], 1.0), writes=[v_b])
        P.dma("pool", bb[:], bias.rearrange("p (h c f) -> p h c f", h=2, c=8), writes=[b_b])
        P.dma("pool", ib[:], idn[:, :], writes=[i_b])
        for hh in range(2):
            pr = slice(hh * 64, (hh + 1) * 64)
            for r in range(ROWS):
                r0 = min(max(r - 4, 0), ROWS - 8)
                cls = r if r < 4 else (4 if r <= ROWS - 4 else r - (ROWS - 8))
                ps, pb = stb.next()
                for c4 in range(4):
                    kt = 64 * r0 + 128 * c4
                    P.op("pe", lambda e, ps=ps, pr=pr, kt=kt, r=r, c4=c4: e.matmul(
                        ps[:, c4 * 64:(c4 + 1) * 64], kTb[pr, kt:kt + 128], qTb[pr, r * 64:(r + 1) * 64],
                        start=(c4 == 0), stop=False, skip_group_check=True), reads=[k_b, q_b], writes=[pb])
                P.op("pe", lambda e, ps=ps, hh=hh, cls=cls: e.matmul(
                    ps[:, 0:256], ib[:, :], bb[:, hh, cls, :], start=False, stop=True, skip_group_check=True), reads=[i_b, b_b], writes=[pb])
                E, Eb = Er.next()
                P.op("act", lambda e, E=E, ps=ps: e.activation(out=E[:, :], in_=ps[:, 0:256], func=AF.Exp, scale=0.125), reads=[pb], writes=[Eb])
                acc, ab = accb.next()
                vsel = va[r0 % 2]
                for c4 in range(4):
                    n = r0 // 2 + c4
                    P.op("pe", lambda e, acc=acc, E=E, c4=c4, vsel=vsel, n=n, hh=hh: e.matmul(
                        acc[0:64, 0:65], E[:, c4 * 64:(c4 + 1) * 64], vsel[:, n, hh, 0:65], start=(c4 == 0), stop=(c4 == 3)),
                        reads=[Eb, v_b], writes=[ab])
                ss, ssb = sr.next()
                P.op("dve", lambda e, ss=ss, acc=acc: e.reciprocal(out=ss[:, 0:1], in_=acc[0:64, 64:65]), reads=[ab], writes=[ssb])
                P.op("dve", lambda e, ss=ss, acc=acc, r=r, hh=hh: e.tensor_scalar(
                    out=stg[:, r, hh * 64:(hh + 1) * 64], in0=acc[0:64, 0:64], scalar1=ss[:, 0:1], scalar2=None, op0=ALU.mult),
                    reads=[ab, ssb], writes=[stg_b])
        P.dma("sp", ob.rearrange("(r p) d -> p r d", p=64), stg[:], reads=[stg_b], writes=[ob_b])
        P.finish([ob_b])
        P.emit(st)
    return nc


def b_bias_np(rpb2):
    ROWS = SEQ // 64
    out = np.full((2, 8, 4, 128, 64), NEG, np.float32)
    reps = [0, 1, 2, 3, 4, ROWS - 3, ROWS - 2, ROWS - 1]
    qc = np.arange(64)
    cstart = np.clip(qc - 8, 0, 64 - 16)
    for ci, r in enumerate(reps):
        r0 = min(max(r - 4, 0), ROWS - 8)
        for c4 in range(4):
            for p in range(128):
                krow = r0 + 2 * c4 + p // 64
                kcol = p % 64
                drow = krow - r + 7
                valid = (kcol >= cstart) & (kcol < cstart + 16)
                dcol = np.clip(kcol - qc, -15, 15) + 15
                for h in range(2):
                    out[h, ci, c4, p, :] = np.where(valid, rpb2[h, drow, dcol], NEG)
    return np.ascontiguousarray(out.transpose(3, 0, 1, 2, 4)).reshape(128, 2 * 8 * 256)


def build_d(dbg=9):
    from contextlib import ExitStack
    nc = bass.Bass("TRN2", target_bir_lowering=False)
    S = SEQ
    CH = 2048
    NCH = S // CH
    dxT = nc.dram_tensor("dxT", [128, S], F32, kind="ExternalInput").ap()
    dgT = nc.dram_tensor("dgT", [128, S], F32, kind="ExternalInput").ap()
    wbd = nc.dram_tensor("wbd", [128, 4, 128], F32, kind="ExternalInput").ap()
    par = nc.dram_tensor("par", [128, 12], F32, kind="ExternalInput").ap()
    odT = nc.dram_tensor("odT", [128, S], F32, kind="ExternalOutput").ap()
    with ExitStack() as st:
        P = Prog(nc)
        X = _sb(st, nc, "X", [128, S], F32); X_b = Buf()
        XC = _sb(st, nc, "XC", [128, S], F32); XC_b = Buf()
        XCb = _sb(st, nc, "XCb", [128, S], BF16); XCb_b = Buf()
        wb = _sb(st, nc, "wb", [128, 4, 128], BF16); w_b = Buf()
        pt = _sb(st, nc, "pt", [128, 12], F32); p_b = Buf()
        sm = _sb(st, nc, "sm", [128, 8], F32); sm_b = Buf()
        HF_b = [Buf() for _ in range(NCH)]
        Rr = Ring([(_sb(st, nc, "R%d" % i, [128, CH], F32), Buf()) for i in range(2)])
        Ar = Ring([(_sb(st, nc, "A%d" % i, [128, CH], F32), Buf()) for i in range(2)])
        Ir = Ring([(_sb(st, nc, "I%d" % i, [128, CH], F32), Buf()) for i in range(2)])
        Sr = Ring([(_sb(st, nc, "S%d" % i, [128, CH], F32), Buf()) for i in range(2)])
        Hr = Ring([(_sb(st, nc, "H%d" % i, [128, CH], F32), Buf()) for i in range(2)])
        Gr = Ring([(_sb(st, nc, "G%d" % i, [128, CH], F32), Buf()) for i in range(2)])
        carry = _sb(st, nc, "carry", [128, 2], F32); carry_b = Buf()
        banks = Ring(_psum_banks(st, nc))
        od_b = Buf()
        P.dma("sp", X[:], dxT[:, :], writes=[X_b])
        P.dma("sp", pt[:], par[:, :], writes=[p_b])
        P.dma("pool", wb[:], wbd[:, :, :], writes=[w_b])
        P.op("act", lambda e: e.activation(out=sm[:, 0:2], in_=pt[:, 9:11], func=AF.Exp, scale=-1.0), reads=[p_b], writes=[sm_b])
        P.op("dve", lambda e: e.tensor_scalar(out=sm[:, 2:4], in0=sm[:, 0:2], scalar1=1.0, scalar2=None, op0=ALU.add), reads=[sm_b], writes=[sm_b])
        P.op("act", lambda e: e.activation(out=sm[:, 4:6], in_=sm[:, 2:4], func=AF.Ln), reads=[sm_b], writes=[sm_b])
        P.op("dve", lambda e: e.tensor_scalar(out=sm[:, 6:8], in0=sm[:, 4:6], scalar1=-8.0, scalar2=None, op0=ALU.mult), reads=[sm_b], writes=[sm_b])
        P.op("dve", lambda e: e.tensor_scalar(out=XC[:], in0=X[:], scalar1=pt[:, 2:3], scalar2=pt[:, 4:5], op0=ALU.mult, op1=ALU.add),
             reads=[X_b, p_b], writes=[XC_b])
        P.op("dve", lambda e: e.scalar_tensor_tensor(out=XC[:, 2:S], in0=X[:, 0:S - 2], scalar=pt[:, 0:1], in1=XC[:, 2:S], op0=ALU.mult, op1=ALU.add),
             reads=[X_b, p_b], writes=[XC_b])
        P.op("dve", lambda e: e.scalar_tensor_tensor(out=XC[:, 1:S], in0=X[:, 0:S - 1], scalar=pt[:, 1:2], in1=XC[:, 1:S], op0=ALU.mult, op1=ALU.add),
             reads=[X_b, p_b], writes=[XC_b])
        P.op("dve", lambda e: e.scalar_tensor_tensor(out=XC[:, 0:S - 1], in0=X[:, 1:S], scalar=pt[:, 3:4], in1=XC[:, 0:S - 1], op0=ALU.mult, op1=ALU.add),
             reads=[X_b, p_b], writes=[XC_b])
        P.op("act", lambda e: e.copy(out=XCb[:], in_=XC[:]), reads=[XC_b], writes=[XCb_b])

        if dbg == 0:
            P.dma("sp", odT[:, :], XC[:], reads=[XC_b, XCb_b, sm_b], writes=[od_b])
            P.finish([od_b]); P.emit(st)
            return nc

        def gate(dst, dstb, gi, bcol, c):
            for j in range(CH // 512):
                t0 = c * CH + j * 512
                ps, pb = banks.next()
                P.op("pe", lambda e, ps=ps, gi=gi, t0=t0: e.matmul(ps[:, :], wb[:, gi, :], XCb[:, t0:t0 + 512], start=True, stop=True),
                     reads=[w_b, XCb_b], writes=[pb])
                P.op("act", lambda e, ps=ps, dst=dst, j=j, bcol=bcol: e.activation(
                    out=dst[:, j * 512:(j + 1) * 512], in_=ps[:, :], func=AF.Sigmoid, bias=pt[:, bcol:bcol + 1]),
                    reads=[pb, p_b], writes=[dstb])

        def prep(d, c):
            R, Rb = Rr.next(); A, Ab = Ar.next(); I, Ib = Ir.next(); S2, S2b = Sr.next()
            gate(R, Rb, 2 * d, 5 + 2 * d, c)
            P.op("act", lambda e, A=A, R=R, d=d: e.activation(out=A[:], in_=R[:], func=AF.Exp, scale=sm[:, 6 + d:7 + d]), reads=[Rb, sm_b], writes=[Ab])
            gate(I, Ib, 2 * d + 1, 6 + 2 * d, c)
            P.op("dve", lambda e, I=I, c=c: e.tensor_tensor(out=I[:], in0=I[:], in1=XC[:, c * CH:(c + 1) * CH], op=ALU.mult), reads=[Ib, XC_b], writes=[Ib])
            P.op("pool", lambda e, S2=S2, A=A: e.tensor_tensor(out=S2[:], in0=A[:], in1=A[:], op=ALU.mult), reads=[Ab], writes=[S2b])
            P.op("dve", lambda e, S2=S2: e.tensor_scalar(out=S2[:], in0=S2[:], scalar1=-1.0, scalar2=1.0, op0=ALU.mult, op1=ALU.add), reads=[S2b], writes=[S2b])
            P.op("dve", lambda e, S2=S2: e.tensor_scalar(out=S2[:], in0=S2[:], scalar1=0.0, scalar2=None, op0=ALU.max), reads=[S2b], writes=[S2b])
            P.op("act", lambda e, S2=S2: e.activation(out=S2[:], in_=S2[:], func=AF.Sqrt), reads=[S2b], writes=[S2b])
            P.op("pool", lambda e, S2=S2, I=I: e.tensor_tensor(out=S2[:], in0=S2[:], in1=I[:], op=ALU.mult), reads=[S2b, Ib], writes=[S2b])
            return A, Ab, S2, S2b

        for c in range(NCH):
            A, Ab, Bv, Bvb = prep(0, c)
            sl = slice(c * CH, (c + 1) * CH)
            if c == 0:
                P.op("dve", lambda e, A=A, Bv=Bv, sl=sl: e.tensor_tensor_scan(out=X[:, sl], data0=A[:], data1=Bv[:], initial=0.0, op0=ALU.mult, op1=ALU.add),
                     reads=[Ab, Bvb, XC_b], writes=[X_b, HF_b[c]])
            else:
                P.op("dve", lambda e, A=A, Bv=Bv, sl=sl, c=c: e.tensor_tensor_scan(
                    out=X[:, sl], data0=A[:], data1=Bv[:], initial=X[:, c * CH - 1:c * CH], op0=ALU.mult, op1=ALU.add),
                    reads=[Ab, Bvb, HF_b[c - 1]], writes=[HF_b[c]])
        if dbg == 1:
            P.dma("sp", odT[:, :], X[:], reads=HF_b, writes=[od_b])
            P.finish([od_b]); P.emit(st)
            return nc
        for c in range(NCH - 1, -1, -1):
            A, Ab, Bv, Bvb = prep(1, c)
            H, Hb = Hr.next(); G, Gb = Gr.next()
            sl = slice(c * CH, (c + 1) * CH)
            P.dma("sp", G[:], dgT[:, sl], writes=[Gb])
            if c == NCH - 1:
                P.op("dve", lambda e, A=A, Bv=Bv, H=H: e.tensor_tensor_scan(
                    out=H[:, ::-1], data0=A[:, ::-1], data1=Bv[:, ::-1], initial=0.0, op0=ALU.mult, op1=ALU.add), reads=[Ab, Bvb], writes=[Hb])
            else:
                P.op("dve", lambda e, A=A, Bv=Bv, H=H: e.tensor_tensor_scan(
                    out=H[:, ::-1], data0=A[:, ::-1], data1=Bv[:, ::-1], initial=carry[:, 0:1], op0=ALU.mult, op1=ALU.add),
                    reads=[Ab, Bvb, carry_b], writes=[Hb])
            P.op("dve", lambda e, H=H: e.tensor_copy(out=carry[:, 0:1], in_=H[:, 0:1]), reads=[Hb], writes=[carry_b])
            P.op("pool", lambda e, H=H, sl=sl: e.tensor_tensor(out=H[:], in0=H[:], in1=X[:, sl], op=ALU.add), reads=[Hb, HF_b[c]], writes=[Hb])
            if dbg != 2:
                T1, T1b = Rr.next()
                P.op("pool", lambda e, T1=T1, G=G: e.tensor_tensor(out=T1[:], in0=G[:], in1=G[:], op=ALU.mult), reads=[Gb], writes=[T1b])
                P.op("dve", lambda e, T1=T1: e.tensor_scalar(out=T1[:], in0=T1[:], scalar1=0.044715, scalar2=1.0, op0=ALU.mult, op1=ALU.add), reads=[T1b], writes=[T1b])
                P.op("pool", lambda e, T1=T1, G=G: e.tensor_tensor(out=T1[:], in0=T1[:], in1=G[:], op=ALU.mult), reads=[Gb, T1b], writes=[T1b])
                P.op("act", lambda e, T1=T1: e.activation(out=T1[:], in_=T1[:], func=AF.Sigmoid, scale=2.0 * math.sqrt(2.0 / math.pi)), reads=[T1b], writes=[T1b])
                P.op("dve", lambda e, H=H, G=G: e.tensor_tensor(out=G[:], in0=H[:], in1=G[:], op=ALU.mult), reads=[Hb, Gb], writes=[Gb])
                P.op("pool", lambda e, T1=T1, G=G: e.tensor_tensor(out=G[:], in0=T1[:], in1=G[:], op=ALU.mult), reads=[Gb, T1b], writes=[Gb])
            else:
                P.op("dve", lambda e, H=H, G=G: e.tensor_copy(out=G[:], in_=H[:]), reads=[Hb, Gb], writes=[Gb])
            P.dma("sp", odT[:, sl], G[:], reads=[Gb], writes=[od_b])
        P.finish([od_b])
        P.emit(st)
    return nc


def _layernorm_tile(P, y, yb, out, outb, lngt, lnbt, ln_b, stt, mv, smb, tmp, tmpb):
    P.op("dve", lambda e: e.bn_stats(out=stt[:, 0, :], in_=y[:, 0:512]), reads=[yb], writes=[smb])
    P.op("dve", lambda e: e.bn_stats(out=stt[:, 1, :], in_=y[:, 512:1024]), reads=[yb], writes=[smb])
    P.op("dve", lambda e: e.bn_aggr(out=mv[:, 0:2], in_=stt[:, :, :]), reads=[smb], writes=[smb])
    P.op("act", lambda e: e.activation(out=mv[:, 2:3], in_=mv[:, 1:2], func=AF.Sqrt, bias=LN_EPS), reads=[smb], writes=[smb])
    P.op("dve", lambda e: e.reciprocal(out=mv[:, 3:4], in_=mv[:, 2:3]), reads=[smb], writes=[smb])
    P.op("dve", lambda e: e.scalar_tensor_tensor(out=mv[:, 4:5], in0=mv[:, 0:1], scalar=-1.0, in1=mv[:, 3:4], op0=ALU.mult, op1=ALU.mult),
         reads=[smb], writes=[smb])
    P.op("act", lambda e: e.activation(out=tmp[:], in_=y[:], func=AF.Identity, scale=mv[:, 3:4], bias=mv[:, 4:5]), reads=[yb, smb], writes=[tmpb])
    P.op("dve", lambda e: e.tensor_tensor(out=tmp[:], in0=tmp[:], in1=lngt[:], op=ALU.mult), reads=[tmpb, ln_b], writes=[tmpb])
    P.op("pool", lambda e: e.tensor_tensor(out=out[:], in0=tmp[:], in1=lnbt[:], op=ALU.add), reads=[tmpb, ln_b], writes=[outb])


def build_m():
    from contextlib import ExitStack
    nc = bass.Bass("TRN2", target_bir_lowering=False)
    T = TPC
    TH = T // 2
    xT = nc.dram_tensor("xT", [1024, T], F32, kind="ExternalInput").ap()
    x = nc.dram_tensor("x", [T, 1024], F32, kind="ExternalInput").ap()
    brT = nc.dram_tensor("brT", [2048, T], F32, kind="ExternalInput").ap()
    mod = nc.dram_tensor("mod", [128, 16], F32, kind="ExternalInput").ap()
    g1 = nc.dram_tensor("g1", [128, 1024], F32, kind="ExternalInput").ap()
    wg = nc.dram_tensor("wg", [1024, 4096], F32, kind="ExternalInput").ap()
    bg = nc.dram_tensor("bg", [128, 32], F32, kind="ExternalInput").ap()
    wbr = nc.dram_tensor("wbr", [2048, 1024], F32, kind="ExternalInput").ap()
    wo = nc.dram_tensor("wo", [1024, 1024], F32, kind="ExternalInput").ap()
    lng = nc.dram_tensor("lng", [128, 1024], F32, kind="ExternalInput").ap()
    lnb = nc.dram_tensor("lnb", [128, 1024], F32, kind="ExternalInput").ap()
    x1 = nc.dram_tensor("x1", [T, 1024], F32, kind="ExternalOutput").ap()
    with ExitStack() as st:
        P = Prog(nc)
        hT = _sb(st, nc, "hT", [128, 8, TH], BF16); hT_b = [Buf() for _ in range(8)]
        brb = _sb(st, nc, "brb", [128, 16, TH], BF16); br_b = Buf()
        mp = _sb(st, nc, "mp", [128, 8, TH], BF16); mp_b = [Buf() for _ in range(8)]
        wbrb = _sb(st, nc, "wbrb", [128, 16, 1024], BF16); wbr_b = Buf()
        wob = _sb(st, nc, "wob", [128, 8, 1024], BF16); wo_b = Buf()
        modt = _sb(st, nc, "modt", [128, 16], F32); mod_b = Buf()
        sc1p = _sb(st, nc, "sc1p", [128, 8], F32); sc1p_b = Buf()
        bgt = _sb(st, nc, "bgt", [128, 32], F32); bg_b = Buf()
        g1t = _sb(st, nc, "g1t", [128, 1024], F32); g1_b = Buf()
        lngt = _sb(st, nc, "lngt", [128, 1024], F32); lnbt = _sb(st, nc, "lnbt", [128, 1024], F32); ln_b = Buf()
        xs = Ring([(_sb(st, nc, "xs%d" % i, [128, TH], F32), Buf()) for i in range(2)])
        wr = Ring([(_sb(st, nc, "wb%d" % i, [128, 8, 128], BF16), Buf()) for i in range(8)])
        gtr = Ring([(_sb(st, nc, "gt%d" % i, [128, 512], F32), Buf()) for i in range(3)])
        tmr = Ring([(_sb(st, nc, "tm%d" % i, [128, 512], F32), Buf()) for i in range(2)])
        acr = Ring([(_sb(st, nc, "ac%d" % i, [128, 512], F32), Buf()) for i in range(2)])
        xtr = Ring([(_sb(st, nc, "xt%d" % i, [128, 1024], F32), Buf()) for i in range(2)])
        yr = Ring([(_sb(st, nc, "y%d" % i, [128, 1024], F32), Buf()) for i in range(2)])
        tr = Ring([(_sb(st, nc, "t%d" % i, [128, 1024], F32), Buf()) for i in range(2)])
        outr = Ring([(_sb(st, nc, "o%d" % i, [128, 1024], F32), Buf()) for i in range(2)])
        smr = Ring([((_sb(st, nc, "stt%d" % i, [128, 2, 6], F32), _sb(st, nc, "mv%d" % i, [128, 8], F32)), Buf()) for i in range(2)])
        banks = Ring(_psum_banks(st, nc))
        x1_b = Buf()
        P.dma("sp", modt[:], mod[:, :], writes=[mod_b])
        P.dma("sp", bgt[:], bg[:, :], writes=[bg_b])
        P.dma("sp", g1t[:], g1[:, :], writes=[g1_b])
        P.dma("sp", lngt[:], lng[:, :], writes=[ln_b])
        P.dma("sp", lnbt[:], lnb[:, :], writes=[ln_b])
        P.op("dve", lambda e: e.tensor_scalar(out=sc1p[:], in0=modt[:, 0:8], scalar1=1.0, scalar2=None, op0=ALU.add), reads=[mod_b], writes=[sc1p_b])
        P.dma("pool", wbrb[:], wbr.rearrange("(c p) d -> p c d", p=128), writes=[wbr_b])
        P.dma("pool", wob[:], wo.rearrange("(c p) d -> p c d", p=128), writes=[wo_b])
        for half in range(2):
            tsl = slice(half * TH, (half + 1) * TH)
            for k in range(8):
                xa, xb_ = xs.next()
                P.dma("sp", xa[:], xT[k * 128:(k + 1) * 128, tsl], writes=[xb_])
                P.op("act", lambda e, xa=xa, k=k: e.activation(out=hT[:, k, :], in_=xa[:], func=AF.Identity,
                                                              scale=sc1p[:, k:k + 1], bias=modt[:, 8 + k:9 + k]),
                     reads=[xb_, sc1p_b, mod_b], writes=[hT_b[k]])
            P.dma("pool", brb[:], brT[:, tsl].rearrange("(c p) t -> p c t", p=128), writes=[br_b])
            for dc in range(8):
                ws = []
                for n in range(4):
                    wa, wb_ = wr.next()
                    c0 = n * 1024 + dc * 128
                    P.dma("pool", wa[:], wg[:, c0:c0 + 128].rearrange("(k p) c -> p k c", p=128), writes=[wb_])
                    ws.append((wa, wb_))
                for tb in range(TH // 512):
                    bsl = slice(tb * 512, (tb + 1) * 512)
                    ac, acb = acr.next()
                    for n in range(4):
                        wa, wb_ = ws[n]
                        ps, pb = banks.next()
                        for k in range(8):
                            P.op("pe", lambda e, ps=ps, wa=wa, k=k, bsl=bsl: e.matmul(ps[:, :], wa[:, k, :], hT[:, k, bsl], start=(k == 0), stop=(k == 7)),
                                 reads=[wb_, hT_b[k]], writes=[pb])
                        gt, gtb = gtr.next()
                        P.op("act", lambda e, gt=gt, ps=ps, n=n, dc=dc: e.activation(out=gt[:], in_=ps[:, :], func=AF.Sigmoid, bias=bgt[:, n * 8 + dc:n * 8 + dc + 1]),
                             reads=[pb, bg_b], writes=[gtb])
                        ps2, pb2 = banks.next()
                        for ec in range(4):
                            P.op("pe", lambda e, ps2=ps2, n=n, ec=ec, dc=dc, bsl=bsl: e.matmul(
                                ps2[:, :], wbrb[:, n * 4 + ec, dc * 128:(dc + 1) * 128], brb[:, n * 4 + ec, bsl], start=(ec == 0), stop=(ec == 3)),
                                reads=[wbr_b, br_b], writes=[pb2])
                        if n == 0:
                            P.op("dve", lambda e, ac=ac, ps2=ps2, gt=gt: e.tensor_tensor(out=ac[:], in0=ps2[:, :], in1=gt[:], op=ALU.mult),
                                 reads=[pb2, gtb], writes=[acb])
                        else:
                            tm, tmb = tmr.next()
                            P.op("dve", lambda e, tm=tm, ps2=ps2, gt=gt: e.tensor_tensor(out=tm[:], in0=ps2[:, :], in1=gt[:], op=ALU.mult),
                                 reads=[pb2, gtb], writes=[tmb])
                            P.op("pool", lambda e, ac=ac, tm=tm: e.tensor_tensor(out=ac[:], in0=ac[:], in1=tm[:], op=ALU.add), reads=[tmb, acb], writes=[acb])
                    P.op("act", lambda e, ac=ac, dc=dc, bsl=bsl: e.copy(out=mp[:, dc, bsl], in_=ac[:]), reads=[acb], writes=[mp_b[dc]])
            for i in range(TH // 128):
                isl = slice(i * 128, (i + 1) * 128)
                row0 = half * TH + i * 128
                xt, xtb = xtr.next(); y, yb = yr.next(); tmp, tmpb = tr.next(); o, ob_ = outr.next(); (stt, mv), smb = smr.next()
                P.dma("sp", xt[:], x[row0:row0 + 128, :], writes=[xtb])
                for hf in range(2):
                    csl = slice(hf * 512, (hf + 1) * 512)
                    ps, pb = banks.next()
                    for k in range(8):
                        P.op("pe", lambda e, ps=ps, k=k, isl=isl, csl=csl: e.matmul(ps[:, :], mp[:, k, isl], wob[:, k, csl], start=(k == 0), stop=(k == 7)),
                             reads=[mp_b[k], wo_b], writes=[pb])
                    P.op("dve", lambda e, tmp=tmp, ps=ps, csl=csl: e.tensor_tensor(out=tmp[:, csl], in0=ps[:, :], in1=g1t[:, csl], op=ALU.mult),
                         reads=[pb, g1_b], writes=[tmpb])
                P.op("dve", lambda e, y=y, xt=xt, tmp=tmp: e.scalar_tensor_tensor(out=y[:], in0=xt[:], scalar=ALPHA, in1=tmp[:], op0=ALU.mult, op1=ALU.add),
                     reads=[xtb, tmpb], writes=[yb])
                _layernorm_tile(P, y, yb, o, ob_, lngt, lnbt, ln_b, stt, mv, smb, tmp, tmpb)
                P.dma("sp", x1[row0:row0 + 128, :], o[:], reads=[ob_], writes=[x1_b])
        P.finish([x1_b])
        P.emit(st)
    return nc


GELU_C = 2.0 * math.sqrt(2.0 / math.pi)


def build_p(ntiles=TPC // 128, T=TPC):
    from contextlib import ExitStack
    nc = bass.Bass("TRN2", target_bir_lowering=False)
    x1 = nc.dram_tensor("x1", [T, 1024], F32, kind="ExternalInput").ap()
    x1T = nc.dram_tensor("x1T", [1024, T], F32, kind="ExternalInput").ap()
    mod = nc.dram_tensor("mod", [128, 16], F32, kind="ExternalInput").ap()
    modb = nc.dram_tensor("modb", [128, 3, 1024], F32, kind="ExternalInput").ap()
    wq = nc.dram_tensor("wq", [1024, 2048], F32, kind="ExternalInput").ap()
    skT = nc.dram_tensor("skT", [128, 16, 128], F32, kind="ExternalInput").ap()
    u = nc.dram_tensor("u", [16384, 1024], F32, kind="ExternalInput").ap()
    v = nc.dram_tensor("v", [16384, 1024], F32, kind="ExternalInput").ap()
    lng = nc.dram_tensor("lng", [128, 1024], F32, kind="ExternalInput").ap()
    lnb = nc.dram_tensor("lnb", [128, 1024], F32, kind="ExternalInput").ap()
    iot = nc.dram_tensor("iot", [128, 256], F32, kind="ExternalInput").ap()
    x2 = nc.dram_tensor("x2", [T, 1024], F32, kind="ExternalOutput").ap()
    with ExitStack() as st:
        P = Prog(nc)
        wqb = _sb(st, nc, "wqb", [128, 8, 2048], BF16); wq_b = Buf()
        skb = _sb(st, nc, "skb", [128, 16, 128], BF16); sk_b = Buf()
        modt = _sb(st, nc, "modt", [128, 16], F32); mod_b = Buf()
        sc2p = _sb(st, nc, "sc2p", [128, 8], F32); sc2p_b = Buf()
        mbt = _sb(st, nc, "mbt", [128, 3, 1024], F32); mb_b = Buf()
        lngt = _sb(st, nc, "lngt", [128, 1024], F32); lnbt = _sb(st, nc, "lnbt", [128, 1024], F32); ln_b = Buf()
        xTt = _sb(st, nc, "xTt", [128, 8, 128], F32); xTt_b = Buf()
        h2T = _sb(st, nc, "h2T", [128, 8, 128], BF16); h2T_b = Buf()
        xt = _sb(st, nc, "xt", [128, 1024], F32); xt_b = Buf()
        h2 = _sb(st, nc, "h2", [128, 1024], F32); h2_b = Buf()
        qTb = _sb(st, nc, "qTb", [128, 16, 128], BF16); qT_b = Buf()
        sc = _sb(st, nc, "sc", [128, 16, 128], F32); sc_b = Buf()
        sc2 = _sb(st, nc, "sc2", [128, 16, 128], F32); sc2_b = Buf()
        sv = _sb(st, nc, "sv", [128, 16, 16], F32); sv_b = Buf()
        si = _sb(st, nc, "si", [128, 16, 16], U32); si_b = Buf()
        sif = _sb(st, nc, "sif", [128, 16, 16], F32); sif_b = Buf()
        si1x = _sb(st, nc, "si1x", [128, 8, 16], F32); si1x_b = Buf()
        cand = _sb(st, nc, "cand", [128, 8, 256], F32); cand_b = Buf()
        cand2 = _sb(st, nc, "cand2", [128, 8, 256], F32); cand2_b = Buf()
        candi = _sb(st, nc, "candi", [128, 8, 256], F32); candi_b = Buf()
        tv = _sb(st, nc, "tv", [128, 8, 16], F32); tv_b = Buf()
        tj = _sb(st, nc, "tj", [128, 8, 16], U32); tj_b = Buf()
        tjf = _sb(st, nc, "tjf", [128, 8, 16], F32); tjf_b = Buf()
        iott = _sb(st, nc, "iott", [128, 256], F32); iot_b = Buf()
        ev = _sb(st, nc, "ev", [128, 8, 16], F32); ev_b = Buf()
        gg = _sb(st, nc, "gg", [128, 128], F32); gg_b = Buf()
        sm = _sb(st, nc, "sm", [128, 32], F32); sm_b = Buf()
        idxf = _sb(st, nc, "idxf", [128, 128], F32); idxf_b = Buf()
        idxu = _sb(st, nc, "idxu", [128, 128], U32); idxu_b = Buf()
        hu = _sb(st, nc, "hu", [128, 128], F32); hu_b = Buf()
        tg = _sb(st, nc, "tg", [128, 128], F32); tg_b = Buf()
        ww = _sb(st, nc, "ww", [128, 128], F32); ww_b = Buf()
        junk = _sb(st, nc, "junk", [128, 1024], F32); junk_b = Buf()
        ubr = Ring([(_sb(st, nc, "ub%d" % i, [128, 1024], F32), Buf()) for i in range(6)])
        acc = _sb(st, nc, "acc", [128, 1024], F32); acc_b = Buf()
        y = _sb(st, nc, "y", [128, 1024], F32); y_b = Buf()
        tmp = _sb(st, nc, "tmp", [128, 1024], F32); tmp_b = Buf()
        outr = Ring([(_sb(st, nc, "o%d" % i, [128, 1024], F32), Buf()) for i in range(2)])
        stt = _sb(st, nc, "stt", [128, 2, 6], F32); mv = _sb(st, nc, "mv", [128, 8], F32); smb2 = Buf()
        banks = Ring(_psum_banks(st, nc))
        x2_b = Buf()
        P.dma("sp", modt[:], mod[:, :], writes=[mod_b])
        P.dma("sp", mbt[:], modb[:, :, :], writes=[mb_b])
        P.dma("sp", lngt[:], lng[:, :], writes=[ln_b])
        P.dma("sp", lnbt[:], lnb[:, :], writes=[ln_b])
        P.dma("sp", iott[:], iot[:, :], writes=[iot_b])
        P.dma("pool", wqb[:], wq.rearrange("(k p) c -> p k c", p=128), writes=[wq_b])
        P.dma("pool", skb[:], skT[:, :, :], writes=[sk_b])
        P.op("dve", lambda e: e.tensor_scalar(out=sc2p[:], in0=modt[:, 0:8], scalar1=1.0, scalar2=None, op0=ALU.add), reads=[mod_b], writes=[sc2p_b])
        P.op("dve", lambda e: e.tensor_scalar(out=mbt[:, 0, :], in0=mbt[:, 0, :], scalar1=1.0, scalar2=None, op0=ALU.add), reads=[mb_b], writes=[mb_b])

        def gather(table, slot):
            ub, ubb = ubr.next()
            P._waits("pool", P._deps([idxu_b], [ubb]))
            i = P.dma_rr["pool"]; P.dma_rr["pool"] = (i + 1) % P.ndsem
            key = "d_pool%d" % i
            P.cnt[key] = P.cnt.get(key, 0) + 16
            P.ops["pool"].append(("op", lambda e, ub=ub, slot=slot: e.indirect_dma_start(
                out=ub[:, :], out_offset=None, in_=table[:, :],
                in_offset=bass.IndirectOffsetOnAxis(ap=idxu[:, slot:slot + 1], axis=0)), key, 16))
            P._mark((key, P.cnt[key]), [idxu_b], [ubb])
            return ub, ubb

        for i in range(ntiles):
            tsl = slice(i * 128, (i + 1) * 128)
            P.dma("sp", xTt[:], x1T.rearrange("(k p) t -> p k t", p=128)[:, :, tsl], writes=[xTt_b])
            P.dma("sp", xt[:], x1[tsl, :], writes=[xt_b])
            for k in range(8):
                P.op("act", lambda e, k=k: e.activation(out=h2T[:, k, :], in_=xTt[:, k, :], func=AF.Identity, scale=sc2p[:, k:k + 1], bias=modt[:, 8 + k:9 + k]),
                     reads=[xTt_b, sc2p_b, mod_b], writes=[h2T_b])
            P.op("dve", lambda e: e.tensor_tensor(out=h2[:], in0=xt[:], in1=mbt[:, 0, :], op=ALU.mult), reads=[xt_b, mb_b], writes=[h2_b])
            P.op("pool", lambda e: e.tensor_tensor(out=h2[:], in0=h2[:], in1=mbt[:, 1, :], op=ALU.add), reads=[h2_b, mb_b], writes=[h2_b])
            for g4 in range(4):
                ps, pb = banks.next()
                for j in range(4):
                    hp = g4 * 4 + j
                    for k in range(8):
                        P.op("pe", lambda e, ps=ps, j=j, hp=hp, k=k: e.matmul(ps[:, j * 128:(j + 1) * 128], wqb[:, k, hp * 128:(hp + 1) * 128], h2T[:, k, :],
                                                                     start=(k == 0), stop=(k == 7), skip_group_check=True), reads=[wq_b, h2T_b], writes=[pb])
                P.op("act", lambda e, ps=ps, g4=g4: e.copy(out=qTb[:, g4 * 4:(g4 + 1) * 4, :], in_=ps[:, :].rearrange("p (a b) -> p a b", a=4)), reads=[pb], writes=[qT_b])
            for g4 in range(4):
                ps, pb = banks.next()
                for j in range(4):
                    hp = g4 * 4 + j
                    P.op("pe", lambda e, ps=ps, j=j, hp=hp: e.matmul(ps[:, j * 128:(j + 1) * 128], qTb[:, hp, :], skb[:, hp, :], start=True, stop=True, skip_group_check=True),
                         reads=[qT_b, sk_b], writes=[pb])
                P.op("act", lambda e, ps=ps, g4=g4: e.copy(out=sc[:, g4 * 4:(g4 + 1) * 4, :], in_=ps[:, :].rearrange("p (a b) -> p a b", a=4)), reads=[pb], writes=[sc_b])
            for hp in range(16):
                P.op("dve", lambda e, hp=hp: e.max(out=sv[:, hp, 0:8], in_=sc[:, hp, :]), reads=[sc_b], writes=[sv_b])
                P.op("dve", lambda e, hp=hp: e.match_replace(out=sc2[:, hp, :], in_to_replace=sv[:, hp, 0:8], in_values=sc[:, hp, :], imm_value=-1e30),
                     reads=[sc_b, sv_b], writes=[sc2_b])
                P.op("dve", lambda e, hp=hp: e.max(out=sv[:, hp, 8:16], in_=sc2[:, hp, :]), reads=[sc2_b], writes=[sv_b])
                P.op("dve", lambda e, hp=hp: e.max_index(out=si[:, hp, 0:8], in_max=sv[:, hp, 0:8], in_values=sc[:, hp, :]), reads=[sc_b, sv_b], writes=[si_b])
                P.op("dve", lambda e, hp=hp: e.max_index(out=si[:, hp, 8:16], in_max=sv[:, hp, 8:16], in_values=sc2[:, hp, :]), reads=[sc2_b, sv_b], writes=[si_b])
            P.op("dve", lambda e: e.tensor_copy(out=sif[:], in_=si[:]), reads=[si_b], writes=[sif_b])
            P.op("dve", lambda e: e.tensor_scalar(out=si1x[:], in0=sif[:].rearrange("p (h t) k -> p h t k", t=2)[:, :, 0, :], scalar1=128.0, scalar2=None, op0=ALU.mult),
                 reads=[sif_b], writes=[si1x_b])
            for h in range(8):
                for a in range(16):
                    P.op("dve", lambda e, h=h, a=a: e.tensor_scalar(out=cand[:, h, a * 16:(a + 1) * 16], in0=sv[:, 2 * h + 1, :], scalar1=sv[:, 2 * h, a:a + 1], scalar2=None, op0=ALU.add),
                         reads=[sv_b], writes=[cand_b])
                    P.op("pool", lambda e, h=h, a=a: e.tensor_scalar(out=candi[:, h, a * 16:(a + 1) * 16], in0=sif[:, 2 * h + 1, :], scalar1=si1x[:, h, a:a + 1], scalar2=None, op0=ALU.add),
                         reads=[sif_b, si1x_b], writes=[candi_b])
            for h in range(8):
                P.op("dve", lambda e, h=h: e.max(out=tv[:, h, 0:8], in_=cand[:, h, :]), reads=[cand_b], writes=[tv_b])
                P.op("dve", lambda e, h=h: e.match_replace(out=cand2[:, h, :], in_to_replace=tv[:, h, 0:8], in_values=cand[:, h, :], imm_value=-1e30),
                     reads=[cand_b, tv_b], writes=[cand2_b])
                P.op("dve", lambda e, h=h: e.max(out=tv[:, h, 8:16], in_=cand2[:, h, :]), reads=[cand2_b], writes=[tv_b])
                P.op("dve", lambda e, h=h: e.max_index(out=tj[:, h, 0:8], in_max=tv[:, h, 0:8], in_values=cand[:, h, :]), reads=[cand_b, tv_b], writes=[tj_b])
                P.op("dve", lambda e, h=h: e.max_index(out=tj[:, h, 8:16], in_max=tv[:, h, 8:16], in_values=cand2[:, h, :]), reads=[cand2_b, tv_b], writes=[tj_b])
            P.op("dve", lambda e: e.tensor_copy(out=tjf[:], in_=tj[:]), reads=[tj_b], writes=[tjf_b])
            for h in range(8):
                for k in range(16):
                    P.op("dve", lambda e, h=h, k=k: e.scalar_tensor_tensor(out=junk[:, 0:256], in0=iott[:, :], scalar=tjf[:, h, k:k + 1], in1=candi[:, h, :],
                                                                       op0=ALU.is_equal, op1=ALU.mult, accum_out=idxf[:, h * 16 + k:h * 16 + k + 1]),
                         reads=[iot_b, tjf_b, candi_b], writes=[junk_b, idxf_b])
            P.op("dve", lambda e: e.tensor_copy(out=idxu[:], in_=idxf[:]), reads=[idxf_b], writes=[idxu_b])
            P.op("dve", lambda e: e.tensor_scalar(out=sm[:, 0:8], in0=tv[:, :, 0], scalar1=-1.0, scalar2=None, op0=ALU.mult), reads=[tv_b], writes=[sm_b])
            for h in range(8):
                P.op("act", lambda e, h=h: e.activation(out=ev[:, h, :], in_=tv[:, h, :], func=AF.Exp, bias=sm[:, h:h + 1]), reads=[tv_b, sm_b], writes=[ev_b])
            P.op("dve", lambda e: e.tensor_reduce(out=sm[:, 8:16], in_=ev[:], axis=AX.X, op=ALU.add), reads=[ev_b], writes=[sm_b])
            P.op("dve", lambda e: e.reciprocal(out=sm[:, 16:24], in_=sm[:, 8:16]), reads=[sm_b], writes=[sm_b])
            for h in range(8):
                P.op("dve", lambda e, h=h: e.tensor_scalar(out=gg[:, h * 16:(h + 1) * 16], in0=ev[:, h, :], scalar1=sm[:, 16 + h:17 + h], scalar2=None, op0=ALU.mult),
                     reads=[ev_b, sm_b], writes=[gg_b])
            for slot in range(128):
                ub, ubb = gather(u, slot)
                P.op("dve", lambda e, ub=ub, slot=slot: e.scalar_tensor_tensor(out=junk[:], in0=ub[:], scalar=1.0, in1=h2[:], op0=ALU.mult, op1=ALU.mult,
                                                                            accum_out=hu[:, slot:slot + 1]), reads=[ubb, h2_b], writes=[junk_b, hu_b])
            P.op("dve", lambda e: e.tensor_tensor(out=tg[:], in0=hu[:], in1=hu[:], op=ALU.mult), reads=[hu_b], writes=[tg_b])
            P.op("dve", lambda e: e.tensor_scalar(out=tg[:], in0=tg[:], scalar1=0.044715, scalar2=1.0, op0=ALU.mult, op1=ALU.add), reads=[tg_b], writes=[tg_b])
            P.op("dve", lambda e: e.tensor_tensor(out=tg[:], in0=tg[:], in1=hu[:], op=ALU.mult), reads=[tg_b, hu_b], writes=[tg_b])
            P.op("act", lambda e: e.activation(out=tg[:], in_=tg[:], func=AF.Sigmoid, scale=GELU_C), reads=[tg_b], writes=[tg_b])
            P.op("dve", lambda e: e.tensor_tensor(out=ww[:], in0=hu[:], in1=gg[:], op=ALU.mult), reads=[hu_b, gg_b], writes=[ww_b])
            P.op("dve", lambda e: e.tensor_tensor(out=ww[:], in0=ww[:], in1=tg[:], op=ALU.mult), reads=[ww_b, tg_b], writes=[ww_b])
            for slot in range(128):
                vb, vbb = gather(v, slot)
                if slot == 0:
                    P.op("dve", lambda e, vb=vb: e.tensor_scalar(out=acc[:], in0=vb[:], scalar1=ww[:, 0:1], scalar2=None, op0=ALU.mult), reads=[vbb, ww_b], writes=[acc_b])
                else:
                    P.op("dve", lambda e, vb=vb, slot=slot: e.scalar_tensor_tensor(out=acc[:], in0=vb[:], scalar=ww[:, slot:slot + 1], in1=acc[:], op0=ALU.mult, op1=ALU.add),
                         reads=[vbb, ww_b, acc_b], writes=[acc_b])
            P.op("dve", lambda e: e.tensor_tensor(out=tmp[:], in0=acc[:], in1=mbt[:, 2, :], op=ALU.mult), reads=[acc_b, mb_b], writes=[tmp_b])
            P.op("dve", lambda e: e.scalar_tensor_tensor(out=y[:], in0=xt[:], scalar=ALPHA, in1=tmp[:], op0=ALU.mult, op1=ALU.add), reads=[xt_b, tmp_b], writes=[y_b])
            o, ob_ = outr.next()
            _layernorm_tile(P, y, y_b, o, ob_, lngt, lnbt, ln_b, stt, mv, smb2, tmp, tmp_b)
            P.dma("sp", x2[tsl, :], o[:], reads=[ob_], writes=[x2_b])
        P.finish([x2_b])
        P.emit(st)
    return nc


def build_ada():
    from contextlib import ExitStack
    nc = bass.Bass("TRN2", target_bir_lowering=False)
    wA = nc.dram_tensor("wA", [1024, 3072], F32, kind="ExternalInput").ap()
    cT = nc.dram_tensor("cT", [128, 8, 2], F32, kind="ExternalInput").ap()
    bA = nc.dram_tensor("bA", [128, 24], F32, kind="ExternalInput").ap()
    mo = nc.dram_tensor("mo", [128, 24, 2], F32, kind="ExternalOutput").ap()
    with ExitStack() as st:
        P = Prog(nc)
        w = _sb(st, nc, "w", [128, 8, 3072], F32); w_b = Buf()
        ct = _sb(st, nc, "ct", [128, 8, 2], F32); c_b = Buf()
        ca = _sb(st, nc, "ca", [128, 8, 2], F32); ca_b = Buf()
        bt = _sb(st, nc, "bt", [128, 24], F32); b_b = Buf()
        res = _sb(st, nc, "res", [128, 24, 2], F32); r_b = Buf()
        banks = _psum_banks(st, nc, 1)
        ps, pb = banks[0]
        mo_b = Buf()
        for k in range(8):
            P.dma("sp", w[:, k, :], wA[k * 128:(k + 1) * 128, :], writes=[w_b])
        P.dma("sp", ct[:], cT[:, :, :], writes=[c_b])
        P.dma("sp", bt[:], bA[:, :], writes=[b_b])
        P.op("act", lambda e: e.activation(out=ca[:], in_=ct[:], func=AF.Silu), reads=[c_b], writes=[ca_b])
        for j in range(24):
            for k in range(8):
                P.op("pe", lambda e, j=j, k=k: e.matmul(ps[:, 2 * j:2 * j + 2], w[:, k, j * 128:(j + 1) * 128], ca[:, k, :],
                                                       start=(k == 0), stop=(k == 7), skip_group_check=True), reads=[w_b, ca_b], writes=[pb])
        for b in range(2):
            P.op("dve", lambda e, b=b: e.tensor_tensor(out=res[:, :, b], in0=ps[:, 0:48].rearrange("p (j b) -> p j b", b=2)[:, :, b], in1=bt[:, :], op=ALU.add),
                 reads=[pb, b_b], writes=[r_b])
        P.dma("sp", mo[:, :, :], res[:], reads=[r_b], writes=[mo_b])
        P.finish([mo_b])
        P.emit(st)
    return nc


_PROGS = {}
_DBG = None


def _prog(name, fn):
    if name not in _PROGS:
        _PROGS[name] = fn()
    return _PROGS[name]


def _run(name, fn, in_maps):
    nc = _prog(name, fn)
    n = len(in_maps)
    in_maps = [{k: np.ascontiguousarray(v, dtype=np.float32) for k, v in m.items()} for m in in_maps]
    res = run_bass_kernel_spmd(nc, in_maps, core_ids=list(range(n)))
    return res.results


def _rep(vec):
    return np.ascontiguousarray(np.tile(np.asarray(vec, np.float32)[None], (128, 1)))


def _chunk128(vec):
    return np.ascontiguousarray(np.asarray(vec, np.float32).reshape(-1, 128).T)


def _swap_heads(w):
    k, n = w.shape
    w4 = w.reshape(k, n // 64, 2, 32)
    return np.ascontiguousarray(w4[:, :, ::-1, :]).reshape(k, n)


def _rope_tables(s0, n):
    inv = (1.0 / (10000.0 ** (np.arange(0, 64, 2, dtype=np.float32) / 64.0))).astype(np.float32)
    ang = (np.arange(s0, s0 + n, dtype=np.float32)[:, None] * inv[None, :]).astype(np.float32)
    cos = np.cos(ang).astype(np.float32).T
    sin = np.sin(ang).astype(np.float32).T
    out = np.zeros((128, 2, n), np.float32)
    for p in range(128):
        f = p % 32
        out[p, 0] = cos[f]
        out[p, 1] = -sin[f] if (p % 64) < 32 else sin[f]
    return out


def kernel(x, c, w_ada, b_ada, w_in, b_gate, a_sink, b_rpb, c_lambda, c_norm_g,
           d_conv_w, d_conv_b, d_wa, d_ba, d_wx, d_bx, d_lam, w_branch, w_out,
           ln_g, ln_b, p_wq, p_subkeys, p_u, p_v):
    f32 = np.float32
    x = np.asarray(x, f32); c = np.asarray(c, f32)
    B, S, D = BATCH, SEQ, D_MODEL
    cT = np.ascontiguousarray(c.reshape(2, 8, 128).transpose(2, 1, 0))
    maps = []
    for core in range(8):
        l, half = core // 2, core % 2
        maps.append(dict(wA=np.asarray(w_ada[l])[:, half * 3072:(half + 1) * 3072], cT=cT,
                         bA=_chunk128(np.asarray(b_ada[l])[half * 3072:(half + 1) * 3072])))
    r = _run("ada", build_ada, maps)
    mod = np.zeros((DEPTH, 2, 6144), f32)
    for core in range(8):
        l, half = core // 2, core % 2
        mo = r[core]["mo"]
        mod[l, :, half * 3072:(half + 1) * 3072] = mo.transpose(2, 1, 0).reshape(2, 3072)
    if _DBG is not None:
        _DBG["mod"] = mod
    eye = np.eye(128, dtype=f32)
    amask = a_mask_np()
    cs_tabs = [_rope_tables(j * TPC, TPC) for j in range(4)]
    xc = x.copy()
    l0 = 0
    if _DBG is not None and "start" in _DBG:
        l0, xc = _DBG["start"]
    for l in range(l0, DEPTH):
        wl = np.asarray(w_in[l], f32)
        shift1, scale1, gate1, shift2, scale2, gate2 = [mod[l][:, i * 1024:(i + 1) * 1024] for i in range(6)]
        aq, ak, av = wl[:, 0:512], wl[:, 512:640], wl[:, 640:768]
        bq, bk, bv = wl[:, 768:1280], wl[:, 1280:1792], wl[:, 1792:2304]
        cq, ck, cv = wl[:, 2304:2816], wl[:, 2816:3328], wl[:, 3328:3840]
        dx, dg, gl = wl[:, 3840:4352], wl[:, 4352:4864], wl[:, 4864:8960]
        pairs = []
        for (wm, n) in ((aq, 4), (ak, 1), (cq, 4), (ck, 4)):
            ws = _swap_heads(wm)
            for i in range(n):
                pairs.append(wm[:, i * 128:(i + 1) * 128]); pairs.append(ws[:, i * 128:(i + 1) * 128])
        plains = []
        for wm in (bq, bk, dx, dg):
            for i in range(4):
                plains.append(wm[:, i * 128:(i + 1) * 128])
        wf = np.ascontiguousarray(np.concatenate(pairs + plains, axis=1))
        wt = np.ascontiguousarray(np.concatenate([av, bv, cv], axis=1))
        mod16 = [np.concatenate([_chunk128(scale1[b]), _chunk128(shift1[b])], axis=1) for b in range(2)]
        maps = []
        for core in range(8):
            b, j = core // 4, core % 4
            maps.append(dict(xT=xc[b, j * TPC:(j + 1) * TPC, :].T, mod=mod16[b], wf=wf, wt=wt, cs=cs_tabs[j]))
        r = _run("l1", build_l1, maps)
        F = [np.concatenate([r[b * 4 + j]["oF"] for j in range(4)], axis=2) for b in range(2)]
        TM = [np.concatenate([r[b * 4 + j]["oT"] for j in range(4)], axis=0) for b in range(2)]
        BR = [np.zeros((S, 2048), f32) for _ in range(2)]
        if _DBG is not None:
            _DBG["F%d" % l] = F; _DBG["TM%d" % l] = TM; _DBG["BR%d" % l] = BR
        maps = []
        for core in range(8):
            b, j = core // 4, core % 4
            kv = j // 2
            k1 = F[b][4][kv * 64:(kv + 1) * 64]
            maps.append(dict(qT=F[b][j], kT=np.concatenate([k1, k1], 0), v=TM[b][:, kv * 64:(kv + 1) * 64], msk=amask, idn=eye,
                             snk=_rep(np.asarray(a_sink[l], f32)[2 * j:2 * j + 2])))
        r = _run("a", build_a, maps)
        for core in range(8):
            b, j = core // 4, core % 4
            BR[b][:, j * 128:(j + 1) * 128] = r[core]["oa"]
        maps = []
        for core in range(8):
            b, j = core // 4, core % 4
            vv = TM[b][:, 128 + j * 128:128 + (j + 1) * 128]
            vsh = np.zeros_like(vv); vsh[:-64] = vv[64:]
            maps.append(dict(qT=F[b][13 + j], kT=F[b][17 + j], v=vv, vsh=vsh, bias=b_bias_np(np.asarray(b_rpb[l], f32)[2 * j:2 * j + 2]), idn=8.0 * eye))
        r = _run("b", build_b, maps)
        for core in range(8):
            b, j = core // 4, core % 4
            BR[b][:, 512 + j * 128:512 + (j + 1) * 128] = r[core]["ob"]
        lam_init = 0.8 - 0.6 * math.exp(-0.3 * l)
        maps = []
        for core in range(8):
            b, j = core // 4, core % 4
            maps.append(dict(qT=F[b][5 + j], kT=F[b][9 + j], v=TM[b][:, 640 + j * 128:640 + (j + 1) * 128],
                             lamb=np.tile(np.asarray(c_lambda[l], f32)[None], (128, 1, 1)),
                             cst=_rep(np.array([lam_init, 1.0 - lam_init], f32)), ng=_rep(np.asarray(c_norm_g[l], f32)[j * 128:(j + 1) * 128])))
        r = _run("c", build_c, maps)
        for core in range(8):
            b, j = core // 4, core % 4
            BR[b][:, 1024 + j * 128:1024 + (j + 1) * 128] = r[core]["oc"]
        maps = []
        for core in range(8):
            b, j = core // 4, core % 4
            wbd = np.zeros((128, 4, 128), f32)
            for d in range(2):
                for g in range(2):
                    wbd[g * 64:(g + 1) * 64, 2 * d, g * 64:(g + 1) * 64] = np.asarray(d_wa[l], f32)[d, 2 * j + g]
                    wbd[g * 64:(g + 1) * 64, 2 * d + 1, g * 64:(g + 1) * 64] = np.asarray(d_wx[l], f32)[d, 2 * j + g]
            ch = slice(j * 128, (j + 1) * 128)
            par = np.zeros((128, 12), f32)
            par[:, 0:4] = np.asarray(d_conv_w[l], f32)[:, ch].T
            par[:, 4] = np.asarray(d_conv_b[l], f32)[ch]
            par[:, 5] = np.asarray(d_ba[l], f32)[0, ch]; par[:, 6] = np.asarray(d_bx[l], f32)[0, ch]
            par[:, 7] = np.asarray(d_ba[l], f32)[1, ch]; par[:, 8] = np.asarray(d_bx[l], f32)[1, ch]
            par[:, 9] = np.asarray(d_lam[l], f32)[0, ch]; par[:, 10] = np.asarray(d_lam[l], f32)[1, ch]
            maps.append(dict(dxT=F[b][21 + j], dgT=F[b][25 + j], wbd=wbd, par=par))
        r = _run("d", build_d, maps)
        for core in range(8):
            b, j = core // 4, core % 4
            BR[b][:, 1536 + j * 128:1536 + (j + 1) * 128] = r[core]["odT"].T
        bgc = np.ascontiguousarray(np.asarray(b_gate[l], f32).reshape(4, 8, 128).transpose(2, 0, 1).reshape(128, 32))
        maps = []
        for core in range(8):
            b, j = core // 4, core % 4
            ts = slice(j * TPC, (j + 1) * TPC)
            maps.append(dict(xT=xc[b, ts, :].T, x=xc[b, ts, :], brT=BR[b][ts, :].T, mod=mod16[b], g1=_rep(gate1[b]), wg=gl, bg=bgc,
                             wbr=np.asarray(w_branch[l], f32).reshape(2048, 1024), wo=np.asarray(w_out[l], f32),
                             lng=_rep(np.asarray(ln_g[l], f32)[0]), lnb=_rep(np.asarray(ln_b[l], f32)[0])))
        r = _run("m", build_m, maps)
        x1 = np.stack([np.concatenate([r[b * 4 + j]["x1"] for j in range(4)], axis=0) for b in range(2)])
        if _DBG is not None:
            _DBG["x1_%d" % l] = x1
            if _DBG.get("stop_after_merge") == l:
                return x1
        skT = np.ascontiguousarray(np.asarray(p_subkeys[l], f32).reshape(16, 128, 128).transpose(2, 0, 1))
        maps = []
        PH = TPC
        for pc in range(8):
            b, hf = pc // 4, pc % 4
            xs_ = x1[b][hf * PH:(hf + 1) * PH]
            maps.append(dict(x1=xs_, x1T=xs_.T, mod=np.concatenate([_chunk128(scale2[b]), _chunk128(shift2[b])], axis=1),
                             modb=np.stack([_rep(scale2[b]), _rep(shift2[b]), _rep(gate2[b])], 1), wq=np.asarray(p_wq[l], f32), skT=skT,
                             u=np.asarray(p_u[l], f32), v=np.asarray(p_v[l], f32),
                             lng=_rep(np.asarray(ln_g[l], f32)[1]), lnb=_rep(np.asarray(ln_b[l], f32)[1]),
                             iot=_rep(np.arange(256, dtype=f32))))
        r = _run("p", lambda: build_p(PH // 128, PH), maps)
        xc = np.stack([np.concatenate([r[b * 4 + j]["x2"] for j in range(4)], axis=0) for b in range(2)])
        if _DBG is not None:
            _DBG["x2_%d" % l] = xc
            if _DBG.get("stop_after_layer") == l:
                return xc
    return xc.astype(np.float32)
```
